# Optimizing a Trainium2 kernel written in Bass

```python
import jax, jax.numpy as jnp
from jax import lax
import numpy as np

D_MODEL = 2048
BATCH = 4
SEQ = 8192
DEPTH = 1
DEC_BATCH = 8
DEC_SEQ = 32
PAST_LEN = 4096

CHUNK = 64
D_CONV = D_MODEL
CONV_W = 3
GLA_HEADS = 4
GLA_DK = D_MODEL // 2
GLA_DV = D_MODEL
GLA_HK = GLA_DK // GLA_HEADS
GLA_HV = GLA_DV // GLA_HEADS
GATE_RANK = 16
GATE_NORMALIZER = 16.0
D_FF = 5632
D_PLE = 256
EPS = 1e-6
IN_SPLITS = (D_CONV, D_CONV, D_CONV, GLA_DK, GLA_DK, GLA_DV, GLA_DV, GATE_RANK, D_MODEL, D_MODEL)
D_IN = sum(IN_SPLITS)

kernel_name = "hybrid_shortconv_gla_convffn_stream_step"


def _rmsnorm(x, g):
    xf = x.astype(jnp.float32)
    r = lax.rsqrt(jnp.mean(xf * xf, axis=-1, keepdims=True) + EPS)
    return (xf * r * g.astype(jnp.float32)).astype(x.dtype)


def _causal_dwconv(u, buf, w, bias=None):
    T = u.shape[1]
    full = jnp.concatenate([buf.astype(u.dtype), u], axis=1)
    out = w[0] * full[:, 0:T]
    for j in range(1, CONV_W):
        out = out + w[j] * full[:, j:j + T]
    if bias is not None:
        out = out + bias
    return out, full[:, -(CONV_W - 1):]


def _gla_chunked(q, k, v, log_a, s0, chunk):
    out_dtype = v.dtype
    bsz, T = q.shape[0], q.shape[1]
    n = T // chunk
    f32 = jnp.float32

    def to_chunks(t):
        return t.astype(f32).reshape(bsz, n, chunk, *t.shape[2:]).swapaxes(0, 1)

    causal = jnp.tril(jnp.ones((chunk, chunk), dtype=bool))[None, :, :, None, None]

    def step(S, inp):
        qc, kc, vc, ac = inp
        b = jnp.cumsum(ac, axis=1)
        o_inter = jnp.einsum('blhk,bhkv->blhv', qc * jnp.exp(b), S)
        diff = b[:, :, None] - b[:, None, :]
        decay = jnp.exp(jnp.where(causal, diff, -jnp.inf))
        scores = jnp.einsum('bthk,btshk,bshk->bhts', qc, decay, kc)
        o_intra = jnp.einsum('bhts,bshv->bthv', scores, vc)
        b_last = b[:, -1]
        S_new = S * jnp.exp(b_last)[..., None] + jnp.einsum(
            'bshk,bshv->bhkv', kc * jnp.exp(b_last[:, None] - b), vc)
        return S_new, o_inter + o_intra

    S_fin, o = lax.scan(step, s0.astype(f32),
                        (to_chunks(q), to_chunks(k), to_chunks(v), to_chunks(log_a)))
    o = o.swapaxes(0, 1).reshape(bsz, T, q.shape[2], v.shape[-1])
    return o, S_fin.astype(out_dtype)


def _layer(x, p_i, conv_buf, gla_S, ffn_buf, norm_mix, w_in, conv_a_w, w_a_out,
           w_gate2, b_gate, gla_norm, w_b_out, w_o, norm_ffn, w_up, ffn_conv_w,
           ffn_conv_b, w_down, norm_ple, w_ple_gate, w_ple):
    bsz, T, _ = x.shape
    xn = _rmsnorm(x, norm_mix)
    proj = xn @ w_in
    offs = [0]
    for s in IN_SPLITS:
        offs.append(offs[-1] + s)
    b_a, c_a, v_a, q, k, v, g_o, a_lr, m_a, m_b = [
        proj[..., offs[j]:offs[j + 1]] for j in range(len(IN_SPLITS))]
    conv_out, new_conv_buf = _causal_dwconv(c_a * v_a, conv_buf, conv_a_w)
    y_a = (b_a * conv_out) @ w_a_out
    log_a = jax.nn.log_sigmoid((a_lr @ w_gate2 + b_gate).astype(jnp.float32)) / GATE_NORMALIZER
    qh = (q * (GLA_HK ** -0.5)).reshape(bsz, T, GLA_HEADS, GLA_HK)
    kh = k.reshape(bsz, T, GLA_HEADS, GLA_HK)
    vh = v.reshape(bsz, T, GLA_HEADS, GLA_HV)
    ah = log_a.reshape(bsz, T, GLA_HEADS, GLA_HK)
    o, new_S = _gla_chunked(qh, kh, vh, ah, gla_S, min(CHUNK, T))
    o = _rmsnorm(o, gla_norm.reshape(GLA_HEADS, GLA_HV))
    o = o * jax.nn.silu(g_o.reshape(bsz, T, GLA_HEADS, GLA_HV))
    y_b = o.reshape(bsz, T, GLA_DV) @ w_b_out
    x = x + (jax.nn.sigmoid(m_a) * y_a + jax.nn.sigmoid(m_b) * y_b) @ w_o
    hn = _rmsnorm(x, norm_ffn)
    u = hn @ w_up
    uc, new_ffn_buf = _causal_dwconv(u, ffn_buf, ffn_conv_w, ffn_conv_b)
    val, gate = uc[..., :D_FF], uc[..., D_FF:]
    x = x + (jax.nn.silu(gate) * val) @ w_down
    pn = _rmsnorm(x, norm_ple)
    x = x + jax.nn.sigmoid(pn @ w_ple_gate) * (p_i @ w_ple)
    return x, new_conv_buf, new_S, new_ffn_buf


def setup_inputs(seed: int = 0) -> dict:
    key = jax.random.key(seed)
    ks = jax.random.split(key, 32)
    f32 = jnp.float32

    def nrm(k, shape, scale=1.0):
        return jax.random.normal(k, shape, f32) * scale

    return {
        "x_prompt": nrm(ks[0], (BATCH, SEQ, D_MODEL)),
        "x_sample": nrm(ks[1], (DEC_BATCH, DEC_SEQ, D_MODEL)),
        "p_prompt": nrm(ks[2], (DEPTH, BATCH, SEQ, D_PLE)),
        "p_sample": nrm(ks[3], (DEPTH, DEC_BATCH, DEC_SEQ, D_PLE)),
        "state_conv_a": nrm(ks[4], (DEPTH, DEC_BATCH, CONV_W - 1, D_CONV)),
        "state_gla": nrm(ks[5], (DEPTH, DEC_BATCH, GLA_HEADS, GLA_HK, GLA_HV)),
        "state_ffn_conv": nrm(ks[6], (DEPTH, DEC_BATCH, CONV_W - 1, 2 * D_FF)),
        "norm_mix": 1.0 + nrm(ks[7], (DEPTH, D_MODEL), 0.02),
        "w_in": nrm(ks[8], (DEPTH, D_MODEL, D_IN), D_MODEL ** -0.5),
        "conv_a_w": nrm(ks[9], (DEPTH, CONV_W, D_CONV), CONV_W ** -0.5),
        "w_a_out": nrm(ks[10], (DEPTH, D_CONV, D_MODEL), D_CONV ** -0.5),
        "w_gate2": nrm(ks[11], (DEPTH, GATE_RANK, GLA_DK), GATE_RANK ** -0.5),
        "b_gate": nrm(ks[12], (DEPTH, GLA_DK), 0.02),
        "gla_norm": 1.0 + nrm(ks[13], (DEPTH, GLA_DV), 0.02),
        "w_b_out": nrm(ks[14], (DEPTH, GLA_DV, D_MODEL), GLA_DV ** -0.5),
        "w_o": nrm(ks[15], (DEPTH, D_MODEL, D_MODEL), D_MODEL ** -0.5),
        "norm_ffn": 1.0 + nrm(ks[16], (DEPTH, D_MODEL), 0.02),
        "w_up": nrm(ks[17], (DEPTH, D_MODEL, 2 * D_FF), D_MODEL ** -0.5),
        "ffn_conv_w": nrm(ks[18], (DEPTH, CONV_W, 2 * D_FF), CONV_W ** -0.5),
        "ffn_conv_b": nrm(ks[19], (DEPTH, 2 * D_FF), 0.02),
        "w_down": nrm(ks[20], (DEPTH, D_FF, D_MODEL), D_FF ** -0.5),
        "norm_ple": 1.0 + nrm(ks[21], (DEPTH, D_MODEL), 0.02),
        "w_ple_gate": nrm(ks[22], (DEPTH, D_MODEL, D_MODEL), D_MODEL ** -0.5),
        "w_ple": nrm(ks[23], (DEPTH, D_PLE, D_MODEL), D_PLE ** -0.5),
        "norm_final": 1.0 + nrm(ks[24], (D_MODEL,), 0.02),
    }


def reference(x_prompt, x_sample, p_prompt, p_sample, state_conv_a, state_gla,
              state_ffn_conv, norm_mix, w_in, conv_a_w, w_a_out, w_gate2, b_gate,
              gla_norm, w_b_out, w_o, norm_ffn, w_up, ffn_conv_w, ffn_conv_b, w_down,
              norm_ple, w_ple_gate, w_ple, norm_final):
    bp = x_prompt.shape[0]
    dt = x_prompt.dtype
    hp, hs = x_prompt, x_sample
    cap, gp, fp, cas, gs, fs = [], [], [], [], [], []
    for i in range(DEPTH):
        params = (norm_mix[i], w_in[i], conv_a_w[i], w_a_out[i], w_gate2[i], b_gate[i],
                  gla_norm[i], w_b_out[i], w_o[i], norm_ffn[i], w_up[i], ffn_conv_w[i],
                  ffn_conv_b[i], w_down[i], norm_ple[i], w_ple_gate[i], w_ple[i])
        hp, c1, s1, f1 = _layer(
            hp, p_prompt[i],
            jnp.zeros((bp, CONV_W - 1, D_CONV), dt),
            jnp.zeros((bp, GLA_HEADS, GLA_HK, GLA_HV), dt),
            jnp.zeros((bp, CONV_W - 1, 2 * D_FF), dt),
            *params)
        hs, c2, s2, f2 = _layer(hs, p_sample[i], state_conv_a[i], state_gla[i],
                                state_ffn_conv[i], *params)
        cap.append(c1); gp.append(s1); fp.append(f1)
        cas.append(c2); gs.append(s2); fs.append(f2)
    y_prompt = _rmsnorm(hp, norm_final)
    y_sample = _rmsnorm(hs, norm_final)
    return (y_prompt, y_sample, jnp.stack(cap), jnp.stack(gp), jnp.stack(fp),
            jnp.stack(cas), jnp.stack(gs), jnp.stack(fs))
```

```python
import numpy as np
import ml_dtypes
from collections import deque
from contextlib import ExitStack
import concourse.bass as bass
import concourse.mybir as mybir
from concourse.bass_utils import run_bass_kernel_spmd

F32 = mybir.dt.float32
BF16 = mybir.dt.bfloat16
ALU = mybir.AluOpType
AF = mybir.ActivationFunctionType

D = 2048
KC = 16
DIN = 16400
DFF = 5632
NFC = 88
DPLE = 256
EPS = 1e-6
OFF = dict(b_a=0, c_a=2048, v_a=4096, q=6144, k=7168, v=8192, g=10240, alr=12288, m_a=12304, m_b=14352)
ENGS = ["tensor", "vector", "scalar", "gpsimd", "sync"]
NSLOT = 4
WB = 256
NSAMP = 32


class FW:
    def __init__(self, nc):
        self.nc = nc
        self.prog = {e: [] for e in ENGS}
        self.cnt = {e: 0 for e in ENGS}
        self.waited = {e: {} for e in ENGS}
        self.sems = {}
        self.dma_cnt = {}
        self.free = {"f": deque(), "t": deque()}
        self.bank_rel = {}

    def _emit_waits(self, eng, deps):
        w = self.waited[eng]
        need = {}
        for d in deps:
            if d is None:
                continue
            k, v = d
            if w.get(k, 0) >= v:
                continue
            if need.get(k, 0) < v:
                need[k] = v
        for k, v in need.items():
            w[k] = v
            sem = self.sems[k]
            self.prog[eng].append(lambda e, sem=sem, v=v: e.wait_ge(sem, v))

    def op(self, eng, fn, deps=(), signal=True):
        self._emit_waits(eng, deps)
        if signal:
            self.cnt[eng] += 1
            tok = ("p_" + eng, self.cnt[eng])
            sem = self.sems["p_" + eng]
            self.prog[eng].append(lambda e, fn=fn, sem=sem: fn(e).then_inc(sem, 1))
            return tok
        self.prog[eng].append(lambda e, fn=fn: fn(e))
        return None

    def dma(self, eng, semname, fn, deps=()):
        self._emit_waits(eng, deps)
        self.dma_cnt[semname] = self.dma_cnt.get(semname, 0) + 16
        sem = self.sems[semname]
        self.prog[eng].append(lambda e, fn=fn, sem=sem: fn(e).then_inc(sem, 16))
        return (semname, self.dma_cnt[semname])

    def now(self, engs=ENGS):
        return [("p_" + e, self.cnt[e]) for e in engs if self.cnt[e] > 0]

    def alloc(self, pool):
        b = self.free[pool].popleft()
        return b, self.bank_rel.get((pool, b), [])

    def release(self, pool, b, toks):
        self.bank_rel[(pool, b)] = [t for t in toks if t is not None]
        self.free[pool].append(b)

    def replay(self, block):
        for en in ENGS:
            lst = self.prog[en]

            def body(e, lst=lst):
                for f in lst:
                    f(e)
            getattr(block, en)(body)


def build_program(HALF):
    assert HALF % 512 == 0
    WARM = 128
    pre_tiles = [512] * (HALF // 512 - 1) + [512 - WARM]
    main_tiles = [WARM] + [512] * (HALF // 512)
    NTOK = 2 * HALF + NSAMP
    NP = HALF + WARM + NSAMP
    nc = bass.Bass("TRN2", target_bir_lowering=False)

    def din(name, shape, dt=F32):
        return nc.dram_tensor(name, shape, dt, kind="ExternalInput").ap()

    def dout(name, shape, dt=F32):
        return nc.dram_tensor(name, shape, dt, kind="ExternalOutput").ap()

    xcat = din("xcat", [NTOK, D])
    pcat = din("pcat", [NP, DPLE])
    st_ca = din("st_ca", [128, KC, 2])
    st_gla = din("st_gla", [128, 8, 512])
    st_ffn = din("st_ffn", [128, NFC, 2])
    w_in = din("w_in", [D, DIN])
    w_a_out = din("w_a_out", [D, D])
    w_b_out = din("w_b_out", [D, D])
    w_o = din("w_o", [D, D])
    w_up = din("w_up", [D, 2 * DFF])
    w_down = din("w_down", [DFF, D])
    w_pg = din("w_pg", [D, D])
    w_ple = din("w_ple", [DPLE, D])
    gam = din("gam", [128, 4, KC])
    gfin = din("gfin", [128, D])
    caw = din("caw", [128, KC, 3])
    fcw = din("fcw", [128, NFC, 3])
    fcb = din("fcb", [128, NFC])
    wg2 = din("wg2", [33, 1024])
    identd = din("identd", [128, 128], BF16)
    trid = din("trid", [128, 128])

    y_main = dout("y_main", [HALF, D])
    y_samp = dout("y_samp", [NSAMP, D])
    o_ca = [dout("o_ca_p", [128, KC, 2]), dout("o_ca_s", [128, KC, 2])]
    o_gla = [dout("o_gla_p", [128, 8, 512]), dout("o_gla_s", [128, 8, 512])]
    o_ffn = [dout("o_ffn_p", [128, NFC, 2]), dout("o_ffn_s", [128, NFC, 2])]

    es = ExitStack()
    with es:
        def sb(name, shape, dt):
            return es.enter_context(nc.sbuf_tensor(name, shape, dt))

        xres = sb("xres", [128, 4, D], F32)
        actT = sb("actT", [128, KC, 512], BF16)
        RA = sb("RA", [128, 24 * 1024], BF16)
        R1 = RA[:, 0:8192].rearrange("p (c t) -> p c t", t=512)
        R2 = RA[:, 8192:16384].rearrange("p (c t) -> p c t", t=512)
        qe = RA[:, 8192:12288].rearrange("p (c t) -> p c t", t=512)
        ke = RA[:, 12288:16384].rearrange("p (c t) -> p c t", t=512)
        vtm = RA[:, 16384:24576].rearrange("p (s d) -> p s d", d=D)
        hbuf = RA[:, 0:44 * 512].rearrange("p (c t) -> p c t", t=512)
        ytile = [RA[:, 16384:20480].bitcast(F32), RA[:, 0:4096].bitcast(F32), RA[:, 4096:8192].bitcast(F32)]
        pin = RA[:, 20480:22528].bitcast(F32).rearrange("p (s d) -> p s d", d=DPLE)
        pbf = RA[:, 22528:23552].rearrange("p (s d) -> p s d", d=DPLE)
        pT = RA[:, 23552:24576].rearrange("p (c t) -> p c t", t=512)
        S = sb("S", [128, 8, 512], F32)
        Sbf = sb("Sbf", [128, 8, 512], BF16)
        wring = [sb("wr%d" % i, [128, KC, WB], BF16) for i in range(NSLOT)]
        walr = sb("walr", [128, KC, 16], BF16)
        gam_t = sb("gam_t", [128, 4, KC], F32)
        gfin_t = sb("gfin_t", [128, D], F32)
        caw_t = sb("caw_t", [128, KC, 3], F32)
        fcw_t = sb("fcw_t", [128, NFC, 3], F32)
        fcb_t = sb("fcb_t", [128, NFC], F32)
        wg2_t = sb("wg2_t", [33, 1024], F32)
        ident = sb("ident", [128, 128], BF16)
        tri = sb("tri", [128, 128], F32)
        tri4 = sb("tri4", [128, 512], F32)
        ones = sb("ones", [128, 128], F32)
        mhalf = sb("mhalf", [128, 4], F32)
        cvhalo = sb("cvhalo", [128, KC, 2], BF16)
        uhalo = sb("uhalo", [128, NFC, 2], BF16)
        cvst = sb("cvst", [128, KC, 2], F32)
        ust = sb("ust", [128, NFC, 2], F32)
        alr = sb("alr", [33, 512], F32)
        eblast = sb("eblast", [128, 8, 4], F32)
        stat = sb("stat", [128, 32], F32)
        xs = [sb("xs%d" % i, [128, D], BF16) for i in range(2)]
        w1 = [sb("w1_%d" % i, [128, 516], F32) for i in range(2)]
        w2 = [sb("w2_%d" % i, [128, 516], F32) for i in range(2)]
        w3 = [sb("w3_%d" % i, [128, 516], F32) for i in range(2)]
        ub = [sb("ub%d" % i, [128, 516], BF16) for i in range(4)]
        scm = [sb("scm%d" % i, [128, 512], BF16) for i in range(2)]
        ketm = [sb("ketm%d" % i, [128, 1024], BF16) for i in range(2)]
        PSF = es.enter_context(nc.psum_tensor("PSF", [128, 6, 512], F32))
        PST = es.enter_context(nc.psum_tensor("PST", [128, 2, 1024], BF16))

        fw = FW(nc)
        semnames = ["p_" + e for e in ENGS] + ["d_c", "d_x0", "d_x1", "d_x2", "d_x3", "d_p", "d_y", "d_st", "d_so"] + ["d_w%d" % i for i in range(NSLOT)] + ["d_wb%d" % i for i in range(NSLOT)] + ["d_wc%d" % i for i in range(NSLOT)] + ["d_wa"]
        for n in semnames:
            fw.sems[n] = es.enter_context(nc.semaphore(n))
        for b in range(6):
            fw.free["f"].append(b)
        for b in range(2):
            fw.free["t"].append(b)

        tc = []
        for dst, src in [(gam_t, gam), (gfin_t, gfin), (caw_t, caw), (fcw_t, fcw), (fcb_t, fcb), (wg2_t, wg2), (ident, identd), (tri, trid)]:
            tc.append(fw.dma("sync", "d_c", lambda e, dst=dst, src=src: e.dma_start(out=dst[:], in_=src)))
        t_c = tc[-1]
        t_walr = fw.dma("gpsimd", "d_wa", lambda e: e.dma_start(out=walr[:], in_=w_in[:, OFF["alr"]:OFF["alr"] + 16].rearrange("(k p) n -> p k n", p=128)))
        t_m0 = fw.op("vector", lambda e: e.memset(ones[:], 1.0))
        t_m1 = fw.op("vector", lambda e: e.memset(mhalf[:], -0.5))
        t_m2 = fw.op("vector", lambda e: e.memset(alr[:], 0.0))
        t_m3 = fw.op("vector", lambda e: e.memset(alr[32:33, :], 1.0), deps=[t_m2])
        t_m4 = fw.op("vector", lambda e: e.memset(S[:], 0.0))
        t_m5 = fw.op("vector", lambda e: e.memset(Sbf[:], 0.0))
        t_m6 = fw.op("vector", lambda e: e.memset(cvhalo[:], 0.0))
        t_m7 = fw.op("vector", lambda e: e.memset(uhalo[:], 0.0))
        t_m8 = fw.op("vector", lambda e: e.memset(stat[:], 1.0))
        t_t4 = None
        for h_ in range(4):
            t_t4 = fw.op("vector", lambda e, h_=h_: e.tensor_copy(out=tri4[:, h_ * 128:(h_ + 1) * 128], in_=tri[:, :]), deps=[t_c])
        INIT = [t_c, t_walr, t_m8, t_t4]
        for _e in ENGS:
            fw._emit_waits(_e, [t_c, t_walr, t_t4, t_m0, t_m1, t_m3, t_m4, t_m5, t_m6, t_m7, t_m8])

        blocks = []

        wkeys = {}

        def wsrc(w, r0, nk, c0, ncols):
            key = (w.tensor.name, r0, c0)
            if key not in wkeys:
                wkeys[key] = len(wkeys)
            return (w[r0:r0 + nk * 128, c0:c0 + ncols].rearrange("(k p) n -> p k n", p=128), nk, ncols, wkeys[key])

        def tile_blocks(kind, nhalf=2):
            bl = []
            if kind == "pre":
                for h in range(4):
                    bl.append(wsrc(w_in, 0, KC, OFF["k"] + WB * h, WB))
                for jj in range(8):
                    bl.append(wsrc(w_in, 0, KC, OFF["v"] + WB * jj, WB))
                return bl
            if nhalf == 2:
                for jj in range(8):
                    bl.append(wsrc(w_in, 0, KC, OFF["v"] + WB * jj, WB))
            for h in range(4):
                bl.append(wsrc(w_in, 0, KC, OFF["q"] + WB * h, WB))
                bl.append(wsrc(w_in, 0, KC, OFF["k"] + WB * h, WB))
            for jj in range(8):
                bl.append(wsrc(w_in, 0, KC, OFF["v"] + WB * jj, WB))
            for jj in range(8):
                bl.append(wsrc(w_in, 0, KC, OFF["g"] + WB * jj, WB))
            for jj in range(8):
                bl.append(wsrc(w_b_out, 0, KC, WB * jj, WB))
                bl.append(wsrc(w_in, 0, KC, OFF["m_b"] + WB * jj, WB))
            for jj in range(8):
                for nm in ("c_a", "v_a", "b_a"):
                    bl.append(wsrc(w_in, 0, KC, OFF[nm] + WB * jj, WB))
            for jj in range(8):
                bl.append(wsrc(w_a_out, 0, KC, WB * jj, WB))
                bl.append(wsrc(w_in, 0, KC, OFF["m_a"] + WB * jj, WB))
            for _h in range(nhalf):
                for jj in range(8):
                    bl.append(wsrc(w_o, 0, KC, WB * jj, WB))
            for jj in range(22):
                bl.append(wsrc(w_up, 0, KC, WB * jj, WB))
                bl.append(wsrc(w_up, 0, KC, DFF + WB * jj, WB))
            for _h in range(nhalf):
                for cb0 in range(0, 8, 2):
                    for cb, rb in ((cb0, 0), (cb0, 1), (cb0 + 1, 0), (cb0 + 1, 1), (cb0, 2), (cb0 + 1, 2)):
                        bl.append(wsrc(w_down, 2048 * rb, 12 if rb == 2 else 16, WB * cb, WB))
            for _h in range(nhalf):
                for cb in range(8):
                    bl.append(wsrc(w_pg, 0, KC, WB * cb, WB))
                    bl.append(wsrc(w_ple, 0, 2, WB * cb, WB))
            return bl

        full_bl = tile_blocks("full")
        pre_own = tile_blocks("pre")
        own_idx = set(b_[3] for b_ in pre_own)
        extras_all = []
        seen_ = set()
        for b_ in full_bl:
            if b_[3] not in own_idx and b_[3] not in seen_:
                extras_all.append(b_)
                seen_.add(b_[3])
        NLEAVE = 80
        lazy_idx = set(b_[3] for b_ in extras_all[len(extras_all) - NLEAVE:])
        extras_all = extras_all[:len(extras_all) - NLEAVE]
        npre_ = len(pre_tiles)
        per_ = (len(extras_all) + npre_ - 1) // npre_
        extras = [extras_all[i_ * per_:(i_ + 1) * per_] for i_ in range(npre_)]
        pre_counts = []
        for t_ in range(npre_):
            n_t = len(extras[t_])
            no_ = len(pre_own)
            cnts = []
            for i_ in range(no_):
                lo_, hi_ = (i_ * n_t) // no_, ((i_ + 1) * n_t) // no_
                blocks.append(pre_own[i_])
                blocks.extend(extras[t_][lo_:hi_])
                cnts.append(hi_ - lo_)
            pre_counts.append(cnts)
        warm_lo = len(blocks)
        warm_hi = warm_lo
        for it__, tt__ in enumerate(main_tiles + [NSAMP]):
            blocks.extend(tile_blocks("full", 2 if tt__ >= 512 else 1))
            if it__ == 0:
                warm_hi = len(blocks)
        wq = nc.dram_tensor("wq", [len(wkeys), 128, KC * WB], BF16, kind="Internal").ap()
        wst = {"issued": 0, "cur": -1, "load_tok": {}, "done_tok": {}, "wb_tok": {}, "conv": set()}

        def wget():
            i = wst["cur"] + 1
            wst["cur"] = i
            while wst["issued"] < min(len(blocks), i + NSLOT) and (wst["issued"] < NSLOT or (wst["issued"] - NSLOT) in wst["done_tok"]):
                m = wst["issued"]
                src, nk, ncols, widx = blocks[m]
                slot = m % NSLOT
                deps = wst["done_tok"].get(m - NSLOT, []) + wst["wb_tok"].get(m - NSLOT, [])
                qv = wq[widx, :, 0:nk * ncols].rearrange("p (k n) -> p k n", n=ncols)
                if widx in wst["conv"]:
                    wball = [("d_wb%d" % i_, fw.dma_cnt.get("d_wb%d" % i_, 0)) for i_ in range(NSLOT) if fw.dma_cnt.get("d_wb%d" % i_, 0) > 0]
                    wst["load_tok"][m] = fw.dma("sync", "d_w%d" % slot,
                                                lambda e, qv=qv, nk=nk, ncols=ncols, slot=slot: e.dma_start(out=wring[slot][:, 0:nk, 0:ncols], in_=qv),
                                                deps=deps + wball)
                elif warm_lo <= m < warm_hi and widx in lazy_idx:
                    wst["load_tok"][m] = fw.dma("gpsimd", "d_wc%d" % slot,
                                                lambda e, src=src, nk=nk, ncols=ncols, slot=slot: e.dma_start(out=wring[slot][:, 0:nk, 0:ncols], in_=src),
                                                deps=deps)
                else:
                    wst["load_tok"][m] = fw.dma("gpsimd", "d_wc%d" % slot,
                                                lambda e, src=src, nk=nk, ncols=ncols, slot=slot: e.dma_start(out=wring[slot][:, 0:nk, 0:ncols], in_=src),
                                                deps=deps)
                    wst["wb_tok"][m] = [fw.dma("sync", "d_wb%d" % slot,
                                               lambda e, qv=qv, nk=nk, ncols=ncols, slot=slot: e.dma_start(out=qv, in_=wring[slot][:, 0:nk, 0:ncols]),
                                               deps=[wst["load_tok"][m]])]
                    wst["conv"].add(widx)
                wst["issued"] += 1
            assert wst["issued"] > i, (i, wst["issued"])
            return wring[i % NSLOT], wst["load_tok"][i], i

        def wdone(i, toks):
            wst["done_tok"][i] = [t for t in toks if t is not None]

        def mm_group(out_ap, pairs, deps):
            n = len(pairs)
            tok = None
            for i, (l, r) in enumerate(pairs):
                tok = fw.op("tensor", lambda e, l=l, r=r, i=i: e.matmul(out_ap, lhsT=l, rhs=r, start=(i == 0), stop=(i == n - 1)),
                            deps=deps if i == 0 else (), signal=(i == n - 1))
            return tok

        rr = {"xs": 0, "w1": 0, "w2": 0, "w3": 0, "ub": 0, "yt": 0, "scm": 0, "ketm": 0, "oraw": 0}
        last_use = {}

        def ring(name, lst):
            i = rr[name]
            rr[name] = (i + 1) % len(lst)
            return lst[i], last_use.get((name, i), []), (name, i)

        def used(key, toks):
            last_use[key] = [t for t in toks if t is not None]

        def norm_p1(src_fn, subs, nt, ready_fn):
            hs = {}
            for s in subs:
                src = src_fn(s)
                xb, xdeps, xkey = ring("xs", xs)
                q0 = 16 + 4 * (xkey[1] % 2)
                t_sq = fw.op("scalar", lambda e, src=src, xb=xb, q0=q0: e.activation(out=xb[0:nt, :], in_=src, func=AF.Square, accum_out=stat[0:nt, q0:q0 + 1]), deps=ready_fn(s) + xdeps)
                t_ms = fw.op("vector", lambda e, q0=q0: e.tensor_scalar(out=stat[0:nt, q0 + 1:q0 + 2], in0=stat[0:nt, q0:q0 + 1], scalar1=1.0 / D, scalar2=EPS, op0=ALU.mult, op1=ALU.add), deps=[t_sq])
                t_rs = fw.op("gpsimd", lambda e, q0=q0: e.tensor_tensor(out=stat[0:nt, q0 + 2:q0 + 3], in0=stat[0:nt, q0 + 1:q0 + 2], in1=mhalf[0:nt, 0:1], op=ALU.pow), deps=[t_ms, t_m1])
                t_xs = fw.op("vector", lambda e, src=src, xb=xb, q0=q0: e.tensor_scalar(out=xb[0:nt, :], in0=src, scalar1=stat[0:nt, q0 + 2:q0 + 3], scalar2=None, op0=ALU.mult), deps=[t_rs, t_sq])
                hs[s] = (xb, xkey, t_xs)
            return hs

        def norm_p2(hs, subs, nt, gidx, actT_free_fn):
            toks = []
            per_s = {}
            for s in subs:
                xb, xkey, t_xs = hs[s]
                tl = []
                mine = []
                af = actT_free_fn(s)
                for g4 in range(4):
                    b, bd = fw.alloc("t")
                    tp = None
                    for j in range(4):
                        c = g4 * 4 + j
                        tp = fw.op("tensor", lambda e, c=c, j=j, b=b, xb=xb: e.transpose(out=PST[:, b, j * 128:j * 128 + nt], in_=xb[0:nt, c * 128:(c + 1) * 128], identity=ident[0:nt, 0:nt]),
                                   deps=[t_xs] + bd + INIT if j == 0 else (), signal=(j == 3))
                    te = None
                    for j in range(4):
                        c = g4 * 4 + j
                        if g4 % 2 == 0:
                            te = fw.op("vector", lambda e, c=c, j=j, b=b, s=s: e.tensor_scalar(out=actT[:, c, s * 128:s * 128 + nt], in0=PST[:, b, j * 128:j * 128 + nt], scalar1=gam_t[:, gidx, c:c + 1], scalar2=None, op0=ALU.mult),
                                       deps=[tp] + af)
                        else:
                            te = fw.op("scalar", lambda e, c=c, j=j, b=b, s=s: e.activation(out=actT[:, c, s * 128:s * 128 + nt], in_=PST[:, b, j * 128:j * 128 + nt], func=AF.Copy, scale=gam_t[:, gidx, c:c + 1]),
                                       deps=[tp] + af)
                    fw.release("t", b, [te])
                    tl.append(tp)
                    toks.append(te)
                    mine.append(te)
                used(xkey, tl)
                per_s[s] = mine
            return toks, per_s

        def norm_to_actT(src_fn, nsub, nt, gidx, ready_fn, actT_free_fn):
            toks, per_s = [], {}
            for s0 in range(0, nsub, 2):
                subs = list(range(s0, min(nsub, s0 + 2)))
                hs = norm_p1(src_fn, subs, nt, ready_fn)
                t_, p_ = norm_p2(hs, subs, nt, gidx, actT_free_fn)
                toks += t_
                per_s.update(p_)
            return toks, [per_s[s] for s in range(nsub)]

        state = {"actT_readers": [], "x_tok": None}

        def emit_tile(kind, TT, xrow0, prow0, yout, first_of_ctx_load=None, emit_state_out=None, nextra=0, next_x=None, extra_counts=None):
            nsub = max(1, TT // 128)
            nt = min(TT, 128)
            full = kind == "full"
            ecnt = list(extra_counts) if extra_counts is not None else []

            def conv_some():
                if ecnt:
                    for _ in range(ecnt.pop(0)):
                        (_sl, _tk, _wi) = wget()
                        wdone(_wi, [])
            pro = state.pop("pro", None)
            xfree = state.get("xres_free", [])
            t_x = {}
            for s in range(nsub):
                if pro is not None and s in pro["t_x"]:
                    t_x[s] = pro["t_x"][s]
                    continue
                if len(xfree) == nsub:
                    xd = xfree[s]
                else:
                    xd = [t for l_ in xfree for t in l_]
                t_x[s] = fw.dma("gpsimd", "d_x%d" % s, lambda e, s=s: e.dma_start(out=xres[0:nt, s, :], in_=xcat[xrow0 + s * nt:xrow0 + (s + 1) * nt, :]), deps=xd)
            ext = []
            if first_of_ctx_load is not None:
                ext = first_of_ctx_load()
            RBF = state.get("RA_free", [])
            arf = state["actT_readers"]
            arf_fn = (lambda s: arf[s]) if (isinstance(arf, dict) and len(arf) == nsub) else (lambda s: ([t for l_ in arf.values() for t in l_] if isinstance(arf, dict) else arf))
            vsplit = full and nsub >= 4
            hvA = list(range(0, nsub // 2)) if vsplit else list(range(nsub))
            hvB = list(range(nsub // 2, nsub)) if vsplit else []
            act_readers = []
            v_w = []
            vfree = state.get("vtm_free", [])

            def vproj(subs, rdy):
                for jj in range(8):
                    (wv_s, wv_t, wv_i) = wget()
                    tg = None
                    for s in subs:
                        b, bd = fw.alloc("f")
                        tg = mm_group(PSF[0:nt, b, 0:WB], [(actT[:, c, s * 128:s * 128 + nt], wv_s[:, c, 0:WB]) for c in range(KC)], deps=rdy + [wv_t] + bd)
                        tv = fw.op("scalar", lambda e, b=b, s=s, jj=jj: e.activation(out=vtm[0:nt, s, jj * WB:(jj + 1) * WB], in_=PSF[0:nt, b, 0:WB], func=AF.Copy), deps=[tg] + vfree + RBF)
                        fw.release("f", b, [tv])
                        v_w.append(tv)
                        act_readers.append(tg)
                    wdone(wv_i, [tg])
                    conv_some()

            src_fn = lambda s: xres[0:nt, s, :]
            tn = []
            tn_map = {}
            def norm_chunk(subs, hs=None):
                if hs is None:
                    hs = norm_p1(src_fn, subs, nt, lambda s: [t_x[s]])
                t_, p_ = norm_p2(hs, subs, nt, 0, arf_fn)
                tn.extend(t_)
                tn_map.update(p_)
                return t_
            tnA = []
            first = True
            for i0 in range(0, len(hvA), 2):
                subs = hvA[i0:i0 + 2]
                tnA += norm_chunk(subs, pro["hs"] if (pro is not None and first) else None)
                first = False
            if vsplit:
                vproj(hvA, tnA)
            for i0 in range(0, len(hvB), 2):
                norm_chunk(hvB[i0:i0 + 2])
            tn_ps = [tn_map[s] for s in range(nsub)]
            A_RDY = tn

            b, bd = fw.alloc("f")
            tga = mm_group(PSF[0:16, b, 0:TT], [(walr[:, c, 0:16], actT[:, c, 0:TT]) for c in range(KC)], deps=A_RDY + bd + INIT)
            t_al = fw.op("vector", lambda e, b=b: e.tensor_copy(out=alr[0:16, 0:TT], in_=PSF[0:16, b, 0:TT]), deps=[tga, t_m3] + state.get("alr_free", []))
            fw.release("f", b, [t_al])
            act_readers.append(tga)
            z_readers = []
            qk_w = []
            for h in range(4):
                if full:
                    (wq_s, wq_t, wq_i) = wget()
                (wk_s, wk_t, wk_i) = wget()
                for jc in range(2):
                    j = h * 2 + jc
                    bz, bd = fw.alloc("f")
                    tgz = mm_group(PSF[:, bz, 0:TT], [(wg2_t[0:33, j * 128:(j + 1) * 128], alr[0:33, 0:TT])], deps=[t_al] + bd + INIT)
                    z_readers.append(tgz)
                    wa, wad, wak = ring("w1", w1)
                    t1 = fw.op("scalar", lambda e, bz=bz, wa=wa: e.activation(out=wa[:, 0:TT], in_=PSF[:, bz, 0:TT], func=AF.Exp, scale=-1.0), deps=[tgz] + wad)
                    fw.release("f", bz, [t1])
                    t2 = fw.op("scalar", lambda e, wa=wa: e.activation(out=wa[:, 0:TT], in_=wa[:, 0:TT], func=AF.Ln, bias=1.0), deps=[t1])
                    wb_, wbd, wbk = ring("w2", w2)
                    ts = None
                    for s in range(nsub):
                        ts = fw.op("vector", lambda e, s=s, wa=wa, wb_=wb_: e.tensor_tensor_scan(out=wb_[:, s * 128:s * 128 + nt], data0=ones[:, 0:nt], data1=wa[:, s * 128:s * 128 + nt], initial=0.0, op0=ALU.mult, op1=ALU.add),
                                   deps=[t2, t_m0] + wbd if s == 0 else ())
                    used(wak, [ts])
                    wc, wcd, wck = ring("w3", w3)
                    t3 = fw.op("scalar", lambda e, wb_=wb_, wc=wc: e.activation(out=wc[:, 0:TT], in_=wb_[:, 0:TT], func=AF.Exp, scale=-1.0 / 16), deps=[ts] + wcd)
                    t3b = fw.op("vector", lambda e, wc=wc, j=j: e.tensor_copy(out=eblast[:, j, 0:nsub], in_=wc[:, nt - 1:TT:128] if nsub > 1 else wc[:, nt - 1:nt]), deps=[t3] + state.get("eblast_free", []))
                    t4 = fw.op("scalar", lambda e, wb_=wb_: e.activation(out=wb_[:, 0:TT], in_=wb_[:, 0:TT], func=AF.Exp, scale=1.0 / 16), deps=[t3])
                    if full:
                        bq, bd = fw.alloc("f")
                        tgq = mm_group(PSF[:, bq, 0:TT], [(wq_s[:, c, jc * 128:(jc + 1) * 128], actT[:, c, 0:TT]) for c in range(KC)], deps=[wq_t] + bd)
                        t5 = fw.op("vector", lambda e, bq=bq, wc=wc, j=j: e.scalar_tensor_tensor(out=qe[:, j, 0:TT], in0=PSF[:, bq, 0:TT], scalar=float(256 ** -0.5), in1=wc[:, 0:TT], op0=ALU.mult, op1=ALU.mult), deps=[tgq, t3] + RBF)
                        fw.release("f", bq, [t5])
                        qk_w.append(t5)
                        act_readers.append(tgq)
                        used(wck, [t5, t3b])
                    else:
                        used(wck, [t3b])
                    bk, bd = fw.alloc("f")
                    tgk = mm_group(PSF[:, bk, 0:TT], [(wk_s[:, c, jc * 128:(jc + 1) * 128], actT[:, c, 0:TT]) for c in range(KC)], deps=[wk_t] + bd)
                    t6 = fw.op("vector", lambda e, bk=bk, wb_=wb_, j=j: e.tensor_tensor(out=ke[:, j, 0:TT], in0=PSF[:, bk, 0:TT], in1=wb_[:, 0:TT], op=ALU.mult), deps=[tgk, t4] + RBF)
                    fw.release("f", bk, [t6])
                    used(wbk, [t6])
                    qk_w.append(t6)
                    act_readers.append(tgk)
                if full:
                    wdone(wq_i, [tgq])
                wdone(wk_i, [tgk])
                conv_some()
            state["alr_free"] = z_readers
            vproj(hvB if vsplit else hvA, A_RDY)
            S_tok = state.get("S_tok", [t_m4, t_m5]) + ext
            o_w = []
            gla_readers = []
            G = {}

            def gla_T(s):
                c0 = s * 128
                kt, ktd, ktk = ring("ketm", ketm)
                tkt = []
                for g4 in range(2):
                    b, bd = fw.alloc("t")
                    tp = None
                    for j4 in range(4):
                        j = g4 * 4 + j4
                        tp = fw.op("tensor", lambda e, j=j, j4=j4, b=b, c0=c0: e.transpose(out=PST[0:nt, b, j4 * 128:(j4 + 1) * 128], in_=ke[:, j, c0:c0 + nt], identity=ident[:, :]),
                                   deps=qk_w + bd + INIT if j4 == 0 else (), signal=(j4 == 3))
                    te = fw.op("scalar", lambda e, b=b, g4=g4, kt=kt: e.activation(out=kt[0:nt, g4 * 512:(g4 + 1) * 512], in_=PST[0:nt, b, 0:512], func=AF.Copy), deps=[tp] + ktd)
                    fw.release("t", b, [te])
                    tkt.append(te)
                    gla_readers.append(tp)
                G[("kt", s)] = (kt, ktk, tkt)

            def gla_SC(s):
                c0 = s * 128
                b, bd = fw.alloc("f")
                tgs = None
                for h in range(4):
                    tgs = mm_group(PSF[0:nt, b, h * 128:h * 128 + nt], [(ke[:, h * 2 + kc, c0:c0 + nt], qe[:, h * 2 + kc, c0:c0 + nt]) for kc in range(2)], deps=qk_w + bd if h == 0 else [])
                sc, scd, sck = ring("scm", scm)
                tm_ = fw.op("vector", lambda e, b=b, sc=sc: e.tensor_tensor(out=sc[0:nt, :].rearrange("p (h t) -> p h t", t=128)[:, :, 0:nt], in0=PSF[0:nt, b, :].rearrange("p (h t) -> p h t", t=128)[:, :, 0:nt], in1=tri4[0:nt, :].rearrange("p (h t) -> p h t", t=128)[:, :, 0:nt], op=ALU.mult), deps=[tgs] + scd + INIT)
                fw.release("f", b, [tm_])
                gla_readers.append(tgs)
                G[("sc", s)] = (sc, sck, tm_)

            def gla_O(s):
                c0 = s * 128
                sc, sck, tm_ = G[("sc", s)]
                orw, ord_, ork = ring("xs", xs)
                t_or = []
                tgos = []
                t_sq = None
                for h in range(4):
                    bo, bd = fw.alloc("f")
                    pairs = [(qe[:, h * 2 + kc, c0:c0 + nt], Sbf[:, h * 2 + kc, :]) for kc in range(2)] + [(sc[0:nt, h * 128:h * 128 + nt], vtm[0:nt, s, h * 512:(h + 1) * 512])]
                    tgo = mm_group(PSF[0:nt, bo, :], pairs, deps=[tm_] + v_w + G["S_tok"] + bd)
                    t_sq = fw.op("scalar", lambda e, bo=bo, h=h, orw=orw: e.activation(out=orw[0:nt, h * 512:(h + 1) * 512], in_=PSF[0:nt, bo, :], func=AF.Square, accum_out=stat[0:nt, 4 + h:5 + h]), deps=[tgo] + state.get("stat_free", []) + ord_)
                    t_cp = fw.op("vector", lambda e, bo=bo, h=h, orw=orw: e.tensor_copy(out=orw[0:nt, h * 512:(h + 1) * 512], in_=PSF[0:nt, bo, :]), deps=[tgo, t_sq] + ord_)
                    fw.release("f", bo, [t_cp])
                    t_or.append(t_cp)
                    tgos.append(tgo)
                    gla_readers.append(tgo)
                used(sck, [tgos[-1]])
                t_ms = fw.op("vector", lambda e: e.tensor_scalar(out=stat[0:nt, 8:12], in0=stat[0:nt, 4:8], scalar1=1.0 / 512, scalar2=EPS, op0=ALU.mult, op1=ALU.add), deps=[t_sq])
                t_rs = fw.op("gpsimd", lambda e: e.tensor_tensor(out=stat[0:nt, 12:16], in0=stat[0:nt, 8:12], in1=mhalf[0:nt, 0:4], op=ALU.pow), deps=[t_ms, t_m1])
                tn_ = []
                for h in range(4):
                    tn_.append(fw.op("vector", lambda e, h=h, orw=orw: e.tensor_scalar(out=orw[0:nt, h * 512:(h + 1) * 512], in0=orw[0:nt, h * 512:(h + 1) * 512], scalar1=stat[0:nt, 12 + h:13 + h], scalar2=None, op0=ALU.mult), deps=[t_rs] + t_or))
                state["stat_free"] = [t_ms] + tn_
                G[("o", s)] = (orw, ork, tn_)
                G[("tgo", s)] = tgos

            def gla_P(s):
                kt, ktk, tkt = G[("kt", s)]
                tgos = G.get(("tgo", s), [None] * 4)
                kt_readers = []
                newS = []
                for h in range(4):
                    tgo = tgos[h]
                    for kc in range(2):
                        j = h * 2 + kc
                        bp, bd = fw.alloc("f")
                        tgp = mm_group(PSF[:, bp, :], [(kt[0:nt, j * 128:(j + 1) * 128], vtm[0:nt, s, h * 512:(h + 1) * 512])], deps=tkt + v_w + bd)
                        kt_readers.append(tgp)
                        if full:
                            tu1 = fw.op("gpsimd", lambda e, j=j, s=s: e.tensor_scalar(out=S[:, j, :], in0=S[:, j, :], scalar1=eblast[:, j, s:s + 1], scalar2=1.0, op0=ALU.mult, op1=ALU.mult), deps=G["S_tok"])
                        else:
                            tu1 = fw.op("vector", lambda e, j=j, s=s: e.tensor_scalar(out=S[:, j, :], in0=S[:, j, :], scalar1=eblast[:, j, s:s + 1], scalar2=None, op0=ALU.mult), deps=G["S_tok"])
                        tu2 = fw.op("vector", lambda e, j=j, bp=bp, s=s: e.scalar_tensor_tensor(out=S[:, j, :], in0=PSF[:, bp, :], scalar=eblast[:, j, s:s + 1], in1=S[:, j, :], op0=ALU.mult, op1=ALU.add), deps=[tgp, tu1])
                        fw.release("f", bp, [tu2])
                        tu3 = fw.op("scalar", lambda e, j=j: e.activation(out=Sbf[:, j, :], in_=S[:, j, :], func=AF.Copy), deps=[tu2, tgo])
                        newS += [tu2, tu3]
                        gla_readers.append(tgp)
                used(ktk, kt_readers)
                G["S_tok"] = newS

            def gla_N(s):
                c0 = s * 128
                orw, ork, tn_ = G[("o", s)]
                tls = []
                for g4 in range(4):
                    b, bd = fw.alloc("t")
                    tp = None
                    for j4 in range(4):
                        c = g4 * 4 + j4
                        tp = fw.op("tensor", lambda e, c=c, j4=j4, b=b, orw=orw: e.transpose(out=PST[:, b, j4 * 128:j4 * 128 + nt], in_=orw[0:nt, c * 128:(c + 1) * 128], identity=ident[0:nt, 0:nt]),
                                   deps=tn_ + bd if j4 == 0 else (), signal=(j4 == 3))
                    te = None
                    for j4 in range(4):
                        c = g4 * 4 + j4
                        if g4 % 2 == 0:
                            te = fw.op("vector", lambda e, c=c, j4=j4, b=b, c0=c0: e.tensor_scalar(out=R1[:, c, c0:c0 + nt], in0=PST[:, b, j4 * 128:j4 * 128 + nt], scalar1=gam_t[:, 3, c:c + 1], scalar2=None, op0=ALU.mult), deps=[tp] + RBF)
                        else:
                            te = fw.op("scalar", lambda e, c=c, j4=j4, b=b, c0=c0: e.activation(out=R1[:, c, c0:c0 + nt], in_=PST[:, b, j4 * 128:j4 * 128 + nt], func=AF.Copy, scale=gam_t[:, 3, c:c + 1]), deps=[tp] + RBF)
                        o_w.append(te)
                    fw.release("t", b, [te])
                    tls.append(tp)
                used(ork, tls)

            G["S_tok"] = S_tok
            for s in range(min(2, nsub)):
                gla_T(s)
                if full:
                    gla_SC(s)
            for s in range(nsub):
                if full:
                    gla_O(s)
                gla_P(s)
                if full and s >= 1:
                    gla_N(s - 1)
                if s + 2 < nsub:
                    gla_T(s + 2)
                    if full:
                        gla_SC(s + 2)
            if full:
                gla_N(nsub - 1)
            S_tok = G["S_tok"]
            state["S_tok"] = S_tok
            state["vtm_free"] = gla_readers
            state["eblast_free"] = S_tok
            if not full:
                state["actT_readers"] = act_readers
                state["xres_free"] = tn_ps
                while ecnt:
                    conv_some()
                return
            og_w = []
            for jj in range(8):
                (wg_s, wg_t, wg_i) = wget()
                for jc in range(2):
                    j = jj * 2 + jc
                    b, bd = fw.alloc("f")
                    tg = mm_group(PSF[:, b, 0:TT], [(wg_s[:, c, jc * 128:(jc + 1) * 128], actT[:, c, 0:TT]) for c in range(KC)], deps=[wg_t] + bd)
                    wa, wad, wak = ring("w1", w1)
                    t1 = fw.op("scalar", lambda e, b=b, wa=wa: e.activation(out=wa[:, 0:TT], in_=PSF[:, b, 0:TT], func=AF.Tanh, scale=0.5), deps=[tg] + wad)
                    wb_, wbd, wbk = ring("w2", w2)
                    t2 = fw.op("vector", lambda e, b=b, wa=wa, wb_=wb_: e.scalar_tensor_tensor(out=wb_[:, 0:TT], in0=wa[:, 0:TT], scalar=1.0, in1=PSF[:, b, 0:TT], op0=ALU.add, op1=ALU.mult), deps=[t1] + wbd)
                    fw.release("f", b, [t2])
                    t3 = fw.op("vector", lambda e, wb_=wb_, j=j: e.scalar_tensor_tensor(out=R1[:, j, 0:TT], in0=wb_[:, 0:TT], scalar=0.5, in1=R1[:, j, 0:TT], op0=ALU.mult, op1=ALU.mult), deps=[t2] + o_w)
                    used(wak, [t2]); used(wbk, [t3])
                    og_w.append(t3)
                    act_readers.append(tg)
                wdone(wg_i, [tg])
            mg2_w = []
            for jj in range(8):
                (wb_s, wb_t, wb_i), (wm_s, wm_t, wm_i) = wget(), wget()
                for jc in range(2):
                    j = jj * 2 + jc
                    by, bd = fw.alloc("f")
                    tgy = mm_group(PSF[:, by, 0:TT], [(wb_s[:, c, jc * 128:(jc + 1) * 128], R1[:, c, 0:TT]) for c in range(KC)], deps=og_w + [wb_t] + bd)
                    bm, bd = fw.alloc("f")
                    tgm = mm_group(PSF[:, bm, 0:TT], [(wm_s[:, c, jc * 128:(jc + 1) * 128], actT[:, c, 0:TT]) for c in range(KC)], deps=[wm_t] + bd)
                    wa, wad, wak = ring("w1", w1)
                    t1 = fw.op("scalar", lambda e, bm=bm, wa=wa: e.activation(out=wa[:, 0:TT], in_=PSF[:, bm, 0:TT], func=AF.Tanh, scale=0.5), deps=[tgm] + wad)
                    fw.release("f", bm, [t1])
                    wb_, wbd, wbk = ring("w2", w2)
                    t2 = fw.op("vector", lambda e, by=by, wa=wa, wb_=wb_: e.scalar_tensor_tensor(out=wb_[:, 0:TT], in0=wa[:, 0:TT], scalar=1.0, in1=PSF[:, by, 0:TT], op0=ALU.add, op1=ALU.mult), deps=[tgy, t1] + wbd)
                    fw.release("f", by, [t2])
                    t3 = fw.op("vector", lambda e, wb_=wb_, j=j: e.tensor_scalar(out=R2[:, j, 0:TT], in0=wb_[:, 0:TT], scalar1=0.5, scalar2=None, op0=ALU.mult), deps=[t2] + gla_readers)
                    used(wak, [t2]); used(wbk, [t3])
                    mg2_w.append(t3)
                    act_readers.append(tgm)
                wdone(wb_i, [tgy]); wdone(wm_i, [tgm])
            R1_rd7 = [tgy]
            ya_w = []
            for jj in range(8):
                (wsl, wtok, wi) = wget()
                ca = {}
                tg = None
                for jc in range(2):
                    b_, bd = fw.alloc("f")
                    tg = mm_group(PSF[:, b_, 0:TT], [(wsl[:, c, jc * 128:(jc + 1) * 128], actT[:, c, 0:TT]) for c in range(KC)], deps=A_RDY + [wtok] + bd)
                    wa, wad, wak = ring("w1", w1)
                    t1 = fw.op("scalar", lambda e, b_=b_, wa=wa: e.activation(out=wa[:, 0:TT], in_=PSF[:, b_, 0:TT], func=AF.Copy), deps=[tg] + wad)
                    fw.release("f", b_, [t1])
                    ca[jc] = (wa, wak, t1)
                    act_readers.append(tg)
                wdone(wi, [tg])
                (wsl, wtok, wi) = wget()
                cv = {}
                for jc in range(2):
                    j = jj * 2 + jc
                    wa, wak, t1 = ca[jc]
                    bv, bd = fw.alloc("f")
                    tg = mm_group(PSF[:, bv, 0:TT], [(wsl[:, c, jc * 128:(jc + 1) * 128], actT[:, c, 0:TT]) for c in range(KC)], deps=[wtok] + bd)
                    tgv = tg
                    ubf, ubd, ubk = ring("ub", ub)
                    t2 = fw.op("vector", lambda e, bv=bv, wa=wa, ubf=ubf: e.tensor_tensor(out=ubf[:, 2:2 + TT], in0=PSF[:, bv, 0:TT], in1=wa[:, 0:TT], op=ALU.mult), deps=[tgv, t1] + ubd)
                    t2h = fw.op("vector", lambda e, ubf=ubf, j=j: e.tensor_copy(out=ubf[:, 0:2], in_=cvhalo[:, j, :]), deps=ubd + [t_m6] + ext)
                    rel = [t2]
                    if emit_state_out is not None:
                        t2s = fw.op("vector", lambda e, bv=bv, wa=wa, j=j: e.tensor_tensor(out=cvst[:, j, :], in0=PSF[:, bv, TT - 2:TT], in1=wa[:, TT - 2:TT], op=ALU.mult), deps=[tgv, t1] + state.get("cvst_free", []))
                        rel.append(t2s)
                    fw.release("f", bv, rel)
                    used(wak, rel)
                    wb_, wbd, wbk = ring("w2", w2)
                    t3 = fw.op("vector", lambda e, ubf=ubf, wb_=wb_, j=j: e.tensor_scalar(out=wb_[:, 0:TT], in0=ubf[:, 2:2 + TT], scalar1=caw_t[:, j, 2:3], scalar2=None, op0=ALU.mult), deps=[t2, t2h] + wbd + INIT)
                    t4 = fw.op("vector", lambda e, ubf=ubf, wb_=wb_, j=j: e.scalar_tensor_tensor(out=wb_[:, 0:TT], in0=ubf[:, 1:1 + TT], scalar=caw_t[:, j, 1:2], in1=wb_[:, 0:TT], op0=ALU.mult, op1=ALU.add), deps=[t3])
                    t5 = fw.op("vector", lambda e, ubf=ubf, wb_=wb_, j=j: e.scalar_tensor_tensor(out=wb_[:, 0:TT], in0=ubf[:, 0:TT], scalar=caw_t[:, j, 0:1], in1=wb_[:, 0:TT], op0=ALU.mult, op1=ALU.add), deps=[t4])
                    t5h = fw.op("vector", lambda e, ubf=ubf, j=j: e.tensor_copy(out=cvhalo[:, j, :], in_=ubf[:, TT:TT + 2]), deps=[t5, t2h])
                    used(ubk, [t5h])
                    cv[jc] = (wb_, wbk, t5)
                    act_readers.append(tg)
                wdone(wi, [tg])
                (wsl, wtok, wi) = wget()
                for jc in range(2):
                    j = jj * 2 + jc
                    wb_, wbk, t5 = cv[jc]
                    bb, bd = fw.alloc("f")
                    tg = mm_group(PSF[:, bb, 0:TT], [(wsl[:, c, jc * 128:(jc + 1) * 128], actT[:, c, 0:TT]) for c in range(KC)], deps=[wtok] + bd)
                    t6 = fw.op("vector", lambda e, bb=bb, wb_=wb_, j=j: e.tensor_tensor(out=R1[:, j, 0:TT], in0=PSF[:, bb, 0:TT], in1=wb_[:, 0:TT], op=ALU.mult), deps=[tg, t5] + R1_rd7)
                    fw.release("f", bb, [t6])
                    used(wbk, [t6])
                    ya_w.append(t6)
                    act_readers.append(tg)
                wdone(wi, [tg])
            mg_w = []
            for jj in range(8):
                (wa_s, wa_t, wa_i), (wm_s, wm_t, wm_i) = wget(), wget()
                for jc in range(2):
                    j = jj * 2 + jc
                    by, bd = fw.alloc("f")
                    tgy = mm_group(PSF[:, by, 0:TT], [(wa_s[:, c, jc * 128:(jc + 1) * 128], R1[:, c, 0:TT]) for c in range(KC)], deps=ya_w + [wa_t] + bd)
                    bm, bd = fw.alloc("f")
                    tgm = mm_group(PSF[:, bm, 0:TT], [(wm_s[:, c, jc * 128:(jc + 1) * 128], actT[:, c, 0:TT]) for c in range(KC)], deps=[wm_t] + bd)
                    wa, wad, wak = ring("w1", w1)
                    t1 = fw.op("scalar", lambda e, bm=bm, wa=wa: e.activation(out=wa[:, 0:TT], in_=PSF[:, bm, 0:TT], func=AF.Tanh, scale=0.5), deps=[tgm] + wad)
                    fw.release("f", bm, [t1])
                    wb_, wbd, wbk = ring("w2", w2)
                    t2 = fw.op("vector", lambda e, by=by, wa=wa, wb_=wb_: e.scalar_tensor_tensor(out=wb_[:, 0:TT], in0=wa[:, 0:TT], scalar=1.0, in1=PSF[:, by, 0:TT], op0=ALU.add, op1=ALU.mult), deps=[tgy, t1] + wbd)
                    fw.release("f", by, [t2])
                    t3 = fw.op("vector", lambda e, wb_=wb_, j=j: e.scalar_tensor_tensor(out=R2[:, j, 0:TT], in0=wb_[:, 0:TT], scalar=0.5, in1=R2[:, j, 0:TT], op0=ALU.mult, op1=ALU.add), deps=[t2] + mg2_w)
                    used(wak, [t2]); used(wbk, [t3])
                    mg_w.append(t3)
                    act_readers.append(tgm)
                wdone(wa_i, [tgy]); wdone(wm_i, [tgm])
            halves = [list(range(0, nsub // 2)), list(range(nsub // 2, nsub))] if nsub >= 2 else [[0]]
            x1_ps = {}
            tn2 = []
            pend = None
            all_act_readers = act_readers
            for hv in halves:
                for jj in range(8):
                    (wo_s, wo_t, wo_i) = wget()
                    for s in hv:
                        b, bd = fw.alloc("f")
                        tg = mm_group(PSF[0:nt, b, 0:WB], [(R2[:, c, s * 128:s * 128 + nt], wo_s[:, c, 0:WB]) for c in range(KC)], deps=mg_w + [wo_t] + bd)
                        ta = fw.op("vector", lambda e, b=b, s=s, jj=jj: e.tensor_tensor(out=xres[0:nt, s, jj * WB:(jj + 1) * WB], in0=PSF[0:nt, b, 0:WB], in1=xres[0:nt, s, jj * WB:(jj + 1) * WB], op=ALU.add), deps=[tg] + tn)
                        fw.release("f", b, [ta])
                        x1_ps.setdefault(s, []).append(ta)
                    wdone(wo_i, [tg])
                if pend is not None:
                    t_, _p = norm_p2(pend[0], pend[1], nt, 1, lambda s: all_act_readers)
                    tn2 += t_
                hs_ = norm_p1(lambda s: xres[0:nt, s, :], hv, nt, lambda s: x1_ps[s])
                pend = (hs_, hv)
            t_, _p = norm_p2(pend[0], pend[1], nt, 1, lambda s: all_act_readers)
            tn2 += t_
            ra_readers = [tg]
            act_readers = []
            h_w = []
            for jj in range(22):
                (wv_s, wv_t, wv_i), (wg_s, wg_t, wg_i) = wget(), wget()
                for jc in range(2):
                    j = jj * 2 + jc
                    res = []
                    for (wsl, wtok, ch) in ((wv_s, wv_t, j), (wg_s, wg_t, 44 + j)):
                        b, bd = fw.alloc("f")
                        tg = mm_group(PSF[:, b, 0:TT], [(wsl[:, c, jc * 128:(jc + 1) * 128], actT[:, c, 0:TT]) for c in range(KC)], deps=tn2 + [wtok] + bd)
                        act_readers.append(tg)
                        ubf, ubd, ubk = ring("ub", ub)
                        t1 = fw.op("scalar", lambda e, b=b, ubf=ubf: e.activation(out=ubf[:, 2:2 + TT], in_=PSF[:, b, 0:TT], func=AF.Copy), deps=[tg] + ubd)
                        t1h = fw.op("gpsimd", lambda e, ubf=ubf, ch=ch: e.tensor_copy(out=ubf[:, 0:2], in_=uhalo[:, ch, :]), deps=ubd + [t_m7] + ext)
                        if emit_state_out is not None:
                            t1s = fw.op("scalar", lambda e, b=b, ch=ch: e.activation(out=ust[:, ch, :], in_=PSF[:, b, TT - 2:TT], func=AF.Copy), deps=[tg] + state.get("ust_free", []))
                            fw.release("f", b, [t1, t1s])
                        else:
                            fw.release("f", b, [t1])
                        wb_, wbd, wbk = ring("w2", w2) if ch < 44 else ring("w3", w3)
                        t3 = fw.op("vector", lambda e, ubf=ubf, wb_=wb_, ch=ch: e.tensor_scalar(out=wb_[:, 0:TT], in0=ubf[:, 2:2 + TT], scalar1=fcw_t[:, ch, 2:3], scalar2=fcb_t[:, ch:ch + 1], op0=ALU.mult, op1=ALU.add), deps=[t1, t1h] + wbd + INIT)
                        t4 = fw.op("vector", lambda e, ubf=ubf, wb_=wb_, ch=ch: e.scalar_tensor_tensor(out=wb_[:, 0:TT], in0=ubf[:, 1:1 + TT], scalar=fcw_t[:, ch, 1:2], in1=wb_[:, 0:TT], op0=ALU.mult, op1=ALU.add), deps=[t3])
                        t5 = fw.op("vector", lambda e, ubf=ubf, wb_=wb_, ch=ch: e.scalar_tensor_tensor(out=wb_[:, 0:TT], in0=ubf[:, 0:TT], scalar=fcw_t[:, ch, 0:1], in1=wb_[:, 0:TT], op0=ALU.mult, op1=ALU.add), deps=[t4])
                        t5h = fw.op("gpsimd", lambda e, ubf=ubf, ch=ch: e.tensor_copy(out=uhalo[:, ch, :], in_=ubf[:, TT:TT + 2]), deps=[t5, t1h])
                        used(ubk, [t5h, t5])
                        res.append((wb_, wbk, t5))
                    (uv, uvk, tv5), (ug, ugk, tg5) = res
                    wa, wad, wak = ring("w1", w1)
                    t6 = fw.op("scalar", lambda e, ug=ug, wa=wa: e.activation(out=wa[:, 0:TT], in_=ug[:, 0:TT], func=AF.Tanh, scale=0.5), deps=[tg5] + wad)
                    t7 = fw.op("vector", lambda e, ug=ug, wa=wa: e.scalar_tensor_tensor(out=wa[:, 0:TT], in0=wa[:, 0:TT], scalar=1.0, in1=ug[:, 0:TT], op0=ALU.add, op1=ALU.mult), deps=[t6])
                    t8 = fw.op("vector", lambda e, uv=uv, wa=wa, j=j: e.scalar_tensor_tensor(out=hbuf[:, j, 0:TT], in0=wa[:, 0:TT], scalar=0.5, in1=uv[:, 0:TT], op0=ALU.mult, op1=ALU.mult), deps=[t7, tv5] + ra_readers)
                    used(wak, [t8]); used(uvk, [t8]); used(ugk, [t7])
                    h_w.append(t8)
                wdone(wv_i, [tg]); wdone(wg_i, [tg])
            x2_ps = {}
            tn3 = []
            pend = None
            all_act_readers = act_readers
            tg = None
            for hv in halves:
                for cb0 in range(0, 8, 2):
                  banks = {}
                  for cb_ in (cb0, cb0 + 1):
                    for s in hv:
                        b, bd = fw.alloc("f")
                        banks[(cb_, s)] = (b, bd)
                  for cb, rb in ((cb0, 0), (cb0, 1), (cb0 + 1, 0), (cb0 + 1, 1), (cb0, 2), (cb0 + 1, 2)):
                        nk = 12 if rb == 2 else 16
                        (wd_s, wd_t, wd_i) = wget()
                        hdep = [h_w[rb * 16 + nk - 1]]
                        for s in hv:
                            b, bd = banks[(cb, s)]
                            for c in range(nk):
                                kc = rb * 16 + c
                                first = (kc == 0)
                                last = (kc == 43)
                                tg = fw.op("tensor", lambda e, b=b, s=s, kc=kc, c=c, wd_s=wd_s, first=first, last=last: e.matmul(PSF[0:nt, b, 0:WB], lhsT=hbuf[:, kc, s * 128:s * 128 + nt], rhs=wd_s[:, c, 0:WB], start=first, stop=last),
                                           deps=(hdep + [wd_t] + bd) if c == 0 else (), signal=(c == nk - 1))
                            if rb == 2:
                                ta = fw.op("vector", lambda e, b=b, s=s, cb=cb: e.tensor_tensor(out=xres[0:nt, s, cb * WB:(cb + 1) * WB], in0=PSF[0:nt, b, 0:WB], in1=xres[0:nt, s, cb * WB:(cb + 1) * WB], op=ALU.add), deps=[tg] + tn2)
                                fw.release("f", b, [ta])
                                x2_ps.setdefault(s, []).append(ta)
                        wdone(wd_i, [tg])
                if pend is not None:
                    t_, _p = norm_p2(pend[0], pend[1], nt, 2, lambda s: all_act_readers)
                    tn3 += t_
                hs_ = norm_p1(lambda s: xres[0:nt, s, :], hv, nt, lambda s: x2_ps[s])
                pend = (hs_, hv)
            tg11 = tg
            t_p = fw.dma("gpsimd", "d_p", lambda e: e.dma_start(out=pin[0:nt, 0:nsub, :], in_=pcat[prow0:prow0 + TT, :].rearrange("(s p) d -> p s d", p=nt)), deps=[tg11])
            t_, _p = norm_p2(pend[0], pend[1], nt, 2, lambda s: all_act_readers)
            tn3 += t_
            act_readers = []
            t_pb = fw.op("vector", lambda e: e.tensor_copy(out=pbf[0:nt, 0:nsub, :], in_=pin[0:nt, 0:nsub, :]), deps=[t_p, tg11] + state.get("pbf_free", []))
            state["pin_free"] = [t_pb]
            pT_w = []
            tps = []
            for s in range(nsub):
                b, bd = fw.alloc("t")
                tp = None
                for c in range(2):
                    tp = fw.op("tensor", lambda e, c=c, b=b, s=s: e.transpose(out=PST[:, b, c * 128:c * 128 + nt], in_=pbf[0:nt, s, c * 128:(c + 1) * 128], identity=ident[0:nt, 0:nt]),
                               deps=[t_pb] + bd + state.get("pT_free", []) if c == 0 else (), signal=(c == 1))
                te = None
                for c in range(2):
                    te = fw.op("scalar", lambda e, c=c, b=b, s=s: e.activation(out=pT[:, c, s * 128:s * 128 + nt], in_=PST[:, b, c * 128:c * 128 + nt], func=AF.Copy), deps=[tp] + state.get("pT_free", []))
                    pT_w.append(te)
                fw.release("t", b, [te])
                tps.append(tp)
            state["pbf_free"] = tps
            x3_ps = {}
            pT_r = []
            ar_ps = {}
            youts = {}
            ylast = []
            for hv in halves:
                for cb in range(8):
                    (wg_s, wg_t, wg_i), (wp_s, wp_t, wp_i) = wget(), wget()
                    for s in hv:
                        bg, bd = fw.alloc("f")
                        tgg = mm_group(PSF[0:nt, bg, 0:WB], [(actT[:, c, s * 128:s * 128 + nt], wg_s[:, c, 0:WB]) for c in range(KC)], deps=tn3 + [wg_t] + bd)
                        bp, bd = fw.alloc("f")
                        tgp = mm_group(PSF[0:nt, bp, 0:WB], [(pT[:, c, s * 128:s * 128 + nt], wp_s[:, c, 0:WB]) for c in range(2)], deps=pT_w + [wp_t] + bd)
                        ar_ps.setdefault(s, []).append(tgg)
                        pT_r.append(tgp)
                        wa, wad, wak = ring("w1", w1)
                        t1 = fw.op("scalar", lambda e, bg=bg, wa=wa: e.activation(out=wa[0:nt, 0:WB], in_=PSF[0:nt, bg, 0:WB], func=AF.Tanh, scale=0.5), deps=[tgg] + wad)
                        fw.release("f", bg, [t1])
                        t2 = fw.op("vector", lambda e, bp=bp, wa=wa: e.scalar_tensor_tensor(out=wa[0:nt, 0:WB], in0=wa[0:nt, 0:WB], scalar=1.0, in1=PSF[0:nt, bp, 0:WB], op0=ALU.add, op1=ALU.mult), deps=[tgp, t1])
                        fw.release("f", bp, [t2])
                        t3 = fw.op("vector", lambda e, wa=wa, s=s, cb=cb: e.scalar_tensor_tensor(out=xres[0:nt, s, cb * WB:(cb + 1) * WB], in0=wa[0:nt, 0:WB], scalar=0.5, in1=xres[0:nt, s, cb * WB:(cb + 1) * WB], op0=ALU.mult, op1=ALU.add), deps=[t2] + tn3)
                        used(wak, [t3])
                        x3_ps.setdefault(s, []).append(t3)
                    wdone(wg_i, [tgg]); wdone(wp_i, [tgp])
                if hv is not halves[0] and state.get("pro_x") is not None:
                    txn, hvn = state.pop("pro_x")
                    hsn = norm_p1(lambda s: xres[0:nt, s, :], hvn, nt, lambda s: [txn[s]])
                    state["pro"] = {"t_x": txn, "hs": hsn}
                for s in hv:
                    yt, ytd, ytk = ring("yt", ytile)
                    t_sq = fw.op("scalar", lambda e, s=s, yt=yt: e.activation(out=yt[0:nt, :], in_=xres[0:nt, s, :], func=AF.Square, accum_out=stat[0:nt, 0:1]), deps=x3_ps[s] + ylast + ytd + [tg11])
                    t_ms = fw.op("vector", lambda e: e.tensor_scalar(out=stat[0:nt, 1:2], in0=stat[0:nt, 0:1], scalar1=1.0 / D, scalar2=EPS, op0=ALU.mult, op1=ALU.add), deps=[t_sq])
                    t_rs = fw.op("gpsimd", lambda e: e.tensor_tensor(out=stat[0:nt, 2:3], in0=stat[0:nt, 1:2], in1=mhalf[0:nt, 0:1], op=ALU.pow), deps=[t_ms, t_m1])
                    t_y = fw.op("vector", lambda e, s=s, yt=yt: e.scalar_tensor_tensor(out=yt[0:nt, :], in0=xres[0:nt, s, :], scalar=stat[0:nt, 2:3], in1=gfin_t[0:nt, :], op0=ALU.mult, op1=ALU.mult), deps=[t_rs, t_sq] + INIT)
                    youts[s] = t_y
                    ylast = [t_y]
                    if yout is not None:
                        t_o = fw.dma("gpsimd", "d_y", lambda e, s=s, yt=yt: e.dma_start(out=yout[s * 128:s * 128 + nt, :], in_=yt[0:nt, :]), deps=[t_y])
                        used(ytk, [t_o])
                        state["final"].append(t_o)
                    else:
                        used(ytk, [t_y])
                if next_x is not None and hv is halves[0] and len(halves) == 2 and nsub == 4:
                    txn = {}
                    for s in hv:
                        txn[s] = fw.dma("gpsimd", "d_x%d" % s, lambda e, s=s: e.dma_start(out=xres[0:nt, s, :], in_=xcat[next_x + s * nt:next_x + (s + 1) * nt, :]), deps=[youts[s]])
                    state["pro_x"] = (txn, hv)
            state["pT_free"] = pT_r
            state["actT_readers"] = ar_ps
            youts = [youts[s] for s in range(nsub)]
            state["xres_free"] = [[t] for t in youts]
            state["RA_free"] = [tg11, t_pb] + pT_r[-1:] + tps[-1:] + last_use.get(("yt", 0), []) + last_use.get(("yt", 1), []) + last_use.get(("yt", 2), [])
            if emit_state_out is not None:
                emit_state_out()

        state["final"] = []
        row = 0
        for it_, tt_ in enumerate(pre_tiles):
            emit_tile("pre", tt_, row, 0, None, extra_counts=pre_counts[it_])
            row += tt_
        prow = 0
        n_main = len(main_tiles)
        for i in range(n_main):
            last = (i == n_main - 1)
            tt_ = main_tiles[i]

            def so_main():
                ws = fw.now()
                t1 = fw.dma("gpsimd", "d_so", lambda e: e.dma_start(out=o_ca[0], in_=cvst[:]), deps=ws)
                t2 = fw.dma("gpsimd", "d_so", lambda e: e.dma_start(out=o_gla[0], in_=S[:]), deps=ws)
                t3 = fw.dma("gpsimd", "d_so", lambda e: e.dma_start(out=o_ffn[0], in_=ust[:]), deps=ws)
                state["final"] += [t1, t2, t3]
                state["so_main"] = [t1, t2, t3]
            nx_ = (row + tt_) if (i >= 1 and not last and main_tiles[i + 1] == 512 and tt_ == 512) else None
            emit_tile("full", tt_, row, prow, None if i == 0 else y_main[(i - 1) * 512:i * 512, :], emit_state_out=so_main if last else None, next_x=nx_)
            row += tt_
            prow += tt_

        def load_sample_state():
            ws = fw.now() + state["so_main"]
            t1 = fw.dma("gpsimd", "d_st", lambda e: e.dma_start(out=S[:], in_=st_gla), deps=ws)
            t2 = fw.dma("gpsimd", "d_st", lambda e: e.dma_start(out=ust[:], in_=st_ffn), deps=ws)
            t3 = fw.dma("gpsimd", "d_st", lambda e: e.dma_start(out=cvst[:], in_=st_ca), deps=ws)
            t4 = fw.op("vector", lambda e: e.tensor_copy(out=Sbf[:], in_=S[:]), deps=[t3])
            t5 = fw.op("vector", lambda e: e.tensor_copy(out=uhalo[:], in_=ust[:]), deps=[t3])
            t6 = fw.op("vector", lambda e: e.tensor_copy(out=cvhalo[:], in_=cvst[:]), deps=[t3])
            state["S_tok"] = [t4]
            state["cvst_free"] = [t6]
            state["ust_free"] = [t5]
            return [t4, t5, t6]

        def so_samp():
            ws = fw.now()
            t1 = fw.dma("gpsimd", "d_so", lambda e: e.dma_start(out=o_ca[1], in_=cvst[:]), deps=ws)
            t2 = fw.dma("gpsimd", "d_so", lambda e: e.dma_start(out=o_gla[1], in_=S[:]), deps=ws)
            t3 = fw.dma("gpsimd", "d_so", lambda e: e.dma_start(out=o_ffn[1], in_=ust[:]), deps=ws)
            state["final"] += [t1, t2, t3]
        emit_tile("full", NSAMP, row, prow, y_samp, first_of_ctx_load=load_sample_state, emit_state_out=so_samp)
        fw._emit_waits("gpsimd", state["final"])
        with nc.Block() as block:
            fw.replay(block)
    return nc


_CACHE = {}


def _get_program(HALF):
    if HALF not in _CACHE:
        _CACHE[HALF] = build_program(HALF)
    return _CACHE[HALF]


def _fm(v, nchunk):
    return np.ascontiguousarray(np.asarray(v, np.float32).reshape(nchunk, 128).T)


def kernel(x_prompt, x_sample, p_prompt, p_sample, state_conv_a, state_gla, state_ffn_conv, norm_mix, w_in,
           conv_a_w, w_a_out, w_gate2, b_gate, gla_norm, w_b_out, w_o, norm_ffn, w_up, ffn_conv_w, ffn_conv_b,
           w_down, norm_ple, w_ple_gate, w_ple, norm_final):
    f = np.float32
    x_prompt = np.asarray(x_prompt, f); x_sample = np.asarray(x_sample, f)
    p_prompt = np.asarray(p_prompt, f); p_sample = np.asarray(p_sample, f)
    B, SEQ, _ = x_prompt.shape
    HALF = SEQ // 2
    ncores = 2 * B
    assert x_sample.shape[0] == ncores
    nc = _get_program(HALF)
    gam = np.stack([_fm(norm_mix[0], 16), _fm(norm_ffn[0], 16), _fm(norm_ple[0], 16), _fm(gla_norm[0], 16)], axis=1)
    gfin = np.ascontiguousarray(np.broadcast_to(np.asarray(norm_final, f)[None, :], (128, D)))
    caw = np.ascontiguousarray(np.asarray(conv_a_w[0], f).reshape(3, 16, 128).transpose(2, 1, 0))
    fcw = np.ascontiguousarray(np.asarray(ffn_conv_w[0], f).reshape(3, NFC, 128).transpose(2, 1, 0))
    fcb = _fm(ffn_conv_b[0], NFC)
    wg2 = np.zeros((33, 1024), f)
    wg2[0:16] = np.asarray(w_gate2[0], f)
    wg2[32] = np.asarray(b_gate[0], f)
    identd = np.eye(128, dtype=f).astype(ml_dtypes.bfloat16)
    trid = np.triu(np.ones((128, 128), f))
    shared = dict(w_in=np.ascontiguousarray(w_in[0], f), w_a_out=np.ascontiguousarray(w_a_out[0], f), w_b_out=np.ascontiguousarray(w_b_out[0], f),
                  w_o=np.ascontiguousarray(w_o[0], f), w_up=np.ascontiguousarray(w_up[0], f), w_down=np.ascontiguousarray(w_down[0], f),
                  w_pg=np.ascontiguousarray(w_ple_gate[0], f), w_ple=np.ascontiguousarray(w_ple[0], f), gam=np.ascontiguousarray(gam), gfin=gfin,
                  caw=caw, fcw=fcw, fcb=fcb, wg2=wg2, identd=identd, trid=trid)
    in_maps = []
    for c in range(ncores):
        b, h = c // 2, c % 2
        xc = np.zeros((2 * HALF + NSAMP, D), f)
        WARM = 128
        pc = np.zeros((HALF + WARM + NSAMP, DPLE), f)
        if h == 1:
            xc[0:2 * HALF] = x_prompt[b]
            pc[0:HALF + WARM] = p_prompt[0, b, HALF - WARM:]
        else:
            xc[HALF:2 * HALF] = x_prompt[b, 0:HALF]
            pc[WARM:HALF + WARM] = p_prompt[0, b, 0:HALF]
        xc[2 * HALF:] = x_sample[c]
        pc[HALF + WARM:] = p_sample[0, c]
        m = dict(shared)
        m["xcat"] = xc
        m["pcat"] = pc
        m["st_ca"] = np.ascontiguousarray(np.asarray(state_conv_a[0, c], f).reshape(2, 16, 128).transpose(2, 1, 0))
        m["st_gla"] = np.ascontiguousarray(np.asarray(state_gla[0, c], f).reshape(4, 2, 128, 512).transpose(2, 0, 1, 3).reshape(128, 8, 512))
        m["st_ffn"] = np.ascontiguousarray(np.asarray(state_ffn_conv[0, c], f).reshape(2, NFC, 128).transpose(2, 1, 0))
        in_maps.append(m)
    res = run_bass_kernel_spmd(nc, in_maps, core_ids=list(range(ncores)))
    R = res.results
    y_prompt = np.zeros((B, SEQ, D), f)
    for c in range(ncores):
        b, h = c // 2, c % 2
        y_prompt[b, h * HALF:(h + 1) * HALF] = R[c]["y_main"]
    y_sample = np.stack([R[c]["y_samp"] for c in range(ncores)], 0)

    def un_ca(a):
        return np.ascontiguousarray(a.transpose(2, 1, 0).reshape(2, D))

    def un_gla(a):
        return np.ascontiguousarray(a.reshape(128, 4, 2, 512).transpose(1, 2, 0, 3).reshape(4, 256, 512))

    def un_ffn(a):
        return np.ascontiguousarray(a.transpose(2, 1, 0).reshape(2, 2 * DFF))
    cap = np.stack([un_ca(R[2 * b + 1]["o_ca_p"]) for b in range(B)], 0)[None]
    gp = np.stack([un_gla(R[2 * b + 1]["o_gla_p"]) for b in range(B)], 0)[None]
    fp = np.stack([un_ffn(R[2 * b + 1]["o_ffn_p"]) for b in range(B)], 0)[None]
    cas = np.stack([un_ca(R[c]["o_ca_s"]) for c in range(ncores)], 0)[None]
    gs = np.stack([un_gla(R[c]["o_gla_s"]) for c in range(ncores)], 0)[None]
    fs = np.stack([un_ffn(R[c]["o_ffn_s"]) for c in range(ncores)], 0)[None]
    return (y_prompt, y_sample, cap.astype(f), gp.astype(f), fp.astype(f), cas.astype(f), gs.astype(f), fs.astype(f))
```

```python
import numpy as np
import ml_dtypes
from collections import deque
from contextlib import ExitStack
import concourse.bass as bass
import concourse.mybir as mybir
from concourse.bass_utils import run_bass_kernel_spmd

F32 = mybir.dt.float32
BF16 = mybir.dt.bfloat16
ALU = mybir.AluOpType
AF = mybir.ActivationFunctionType

D = 2048
KC = 16
DIN = 16400
DFF = 5632
NFC = 88
DPLE = 256
EPS = 1e-6
OFF = dict(b_a=0, c_a=2048, v_a=4096, q=6144, k=7168, v=8192, g=10240, alr=12288, m_a=12304, m_b=14352)
ENGS = ["tensor", "vector", "scalar", "gpsimd", "sync"]
NSLOT = 4
WB = 256
NSAMP = 32


class FW:
    def __init__(self, nc):
        self.nc = nc
        self.prog = {e: [] for e in ENGS}
        self.cnt = {e: 0 for e in ENGS}
        self.waited = {e: {} for e in ENGS}
        self.sems = {}
        self.dma_cnt = {}
        self.free = {"f": deque(), "t": deque()}
        self.bank_rel = {}

    def _emit_waits(self, eng, deps):
        w = self.waited[eng]
        need = {}
        for d in deps:
            if d is None:
                continue
            k, v = d
            if w.get(k, 0) >= v:
                continue
            if need.get(k, 0) < v:
                need[k] = v
        for k, v in need.items():
            w[k] = v
            sem = self.sems[k]
            self.prog[eng].append(lambda e, sem=sem, v=v: e.wait_ge(sem, v))

    def op(self, eng, fn, deps=(), signal=True):
        self._emit_waits(eng, deps)
        if signal:
            self.cnt[eng] += 1
            tok = ("p_" + eng, self.cnt[eng])
            sem = self.sems["p_" + eng]
            self.prog[eng].append(lambda e, fn=fn, sem=sem: fn(e).then_inc(sem, 1))
            return tok
        self.prog[eng].append(lambda e, fn=fn: fn(e))
        return None

    def dma(self, eng, semname, fn, deps=()):
        self._emit_waits(eng, deps)
        self.dma_cnt[semname] = self.dma_cnt.get(semname, 0) + 16
        sem = self.sems[semname]
        self.prog[eng].append(lambda e, fn=fn, sem=sem: fn(e).then_inc(sem, 16))
        return (semname, self.dma_cnt[semname])

    def now(self, engs=ENGS):
        return [("p_" + e, self.cnt[e]) for e in engs if self.cnt[e] > 0]

    def alloc(self, pool):
        b = self.free[pool].popleft()
        return b, self.bank_rel.get((pool, b), [])

    def release(self, pool, b, toks):
        self.bank_rel[(pool, b)] = [t for t in toks if t is not None]
        self.free[pool].append(b)

    def replay(self, block):
        for en in ENGS:
            lst = self.prog[en]

            def body(e, lst=lst):
                for f in lst:
                    f(e)
            getattr(block, en)(body)


def build_program(HALF):
    assert HALF % 512 == 0
    WARM = 128
    pre_tiles = [512] * (HALF // 512 - 1) + [512 - WARM]
    main_tiles = [WARM] + [512] * (HALF // 512)
    NTOK = 2 * HALF + NSAMP
    NP = HALF + WARM + NSAMP
    nc = bass.Bass("TRN2", target_bir_lowering=False)

    def din(name, shape, dt=F32):
        return nc.dram_tensor(name, shape, dt, kind="ExternalInput").ap()

    def dout(name, shape, dt=F32):
        return nc.dram_tensor(name, shape, dt, kind="ExternalOutput").ap()

    xcat = din("xcat", [NTOK, D])
    pcat = din("pcat", [NP, DPLE])
    st_ca = din("st_ca", [128, KC, 2])
    st_gla = din("st_gla", [128, 8, 512])
    st_ffn = din("st_ffn", [128, NFC, 2])
    w_in = din("w_in", [D, DIN])
    w_a_out = din("w_a_out", [D, D])
    w_b_out = din("w_b_out", [D, D])
    w_o = din("w_o", [D, D])
    w_up = din("w_up", [D, 2 * DFF])
    w_down = din("w_down", [DFF, D])
    w_pg = din("w_pg", [D, D])
    w_ple = din("w_ple", [DPLE, D])
    gam = din("gam", [128, 4, KC])
    gfin = din("gfin", [128, D])
    caw = din("caw", [128, KC, 3])
    fcw = din("fcw", [128, NFC, 3])
    fcb = din("fcb", [128, NFC])
    wg2 = din("wg2", [33, 1024])
    identd = din("identd", [128, 128], BF16)
    trid = din("trid", [128, 128])

    y_main = dout("y_main", [HALF, D])
    y_samp = dout("y_samp", [NSAMP, D])
    o_ca = [dout("o_ca_p", [128, KC, 2]), dout("o_ca_s", [128, KC, 2])]
    o_gla = [dout("o_gla_p", [128, 8, 512]), dout("o_gla_s", [128, 8, 512])]
    o_ffn = [dout("o_ffn_p", [128, NFC, 2]), dout("o_ffn_s", [128, NFC, 2])]

    es = ExitStack()
    with es:
        def sb(name, shape, dt):
            return es.enter_context(nc.sbuf_tensor(name, shape, dt))

        xres = sb("xres", [128, 4, D], F32)
        actT = sb("actT", [128, KC, 512], BF16)
        RA = sb("RA", [128, 24 * 1024], BF16)
        R1 = RA[:, 0:8192].rearrange("p (c t) -> p c t", t=512)
        R2 = RA[:, 8192:16384].rearrange("p (c t) -> p c t", t=512)
        qe = RA[:, 8192:12288].rearrange("p (c t) -> p c t", t=512)
        ke = RA[:, 12288:16384].rearrange("p (c t) -> p c t", t=512)
        vtm = RA[:, 16384:24576].rearrange("p (s d) -> p s d", d=D)
        ke_alt = RA[:, 0:4096].rearrange("p (c t) -> p c t", t=512)
        vtm_alt = RA[:, 4096:12288].rearrange("p (s d) -> p s d", d=D)
        hbuf = RA[:, 0:44 * 512].rearrange("p (c t) -> p c t", t=512)
        ytile = [RA[:, 16384:20480].bitcast(F32), RA[:, 0:4096].bitcast(F32), RA[:, 4096:8192].bitcast(F32)]
        pin = RA[:, 20480:22528].bitcast(F32).rearrange("p (s d) -> p s d", d=DPLE)
        pbf = RA[:, 22528:23552].rearrange("p (s d) -> p s d", d=DPLE)
        pT = RA[:, 23552:24576].rearrange("p (c t) -> p c t", t=512)
        S = sb("S", [128, 8, 512], F32)
        Sbf = sb("Sbf", [128, 8, 512], BF16)
        wring = [sb("wr%d" % i, [128, KC, WB], BF16) for i in range(NSLOT)]
        walr = sb("walr", [128, KC, 16], BF16)
        gam_t = sb("gam_t", [128, 4, KC], F32)
        gfin_t = sb("gfin_t", [128, D], F32)
        caw_t = sb("caw_t", [128, KC, 3], F32)
        fcw_t = sb("fcw_t", [128, NFC, 3], F32)
        fcb_t = sb("fcb_t", [128, NFC], F32)
        wg2_t = sb("wg2_t", [33, 1024], F32)
        ident = sb("ident", [128, 128], BF16)
        tri = sb("tri", [128, 128], F32)
        tri4 = sb("tri4", [128, 512], F32)
        ones = sb("ones", [128, 128], F32)
        mhalf = sb("mhalf", [128, 4], F32)
        cvhalo = sb("cvhalo", [128, KC, 2], BF16)
        uhalo = sb("uhalo", [128, NFC, 2], BF16)
        cvst = sb("cvst", [128, KC, 2], F32)
        ust = sb("ust", [128, NFC, 2], F32)
        alr = sb("alr", [33, 512], F32)
        eblast = sb("eblast", [128, 8, 4], F32)
        eblast_alt = sb("eblast_alt", [128, 8, 4], F32)
        KE_SETS = [(ke, vtm, eblast), (ke_alt, vtm_alt, eblast_alt)]
        stat = sb("stat", [128, 32], F32)
        xs = [sb("xs%d" % i, [128, D], BF16) for i in range(2)]
        w1 = [sb("w1_%d" % i, [128, 516], F32) for i in range(2)]
        w2 = [sb("w2_%d" % i, [128, 516], F32) for i in range(2)]
        w3 = [sb("w3_%d" % i, [128, 516], F32) for i in range(2)]
        ub = [sb("ub%d" % i, [128, 516], BF16) for i in range(4)]
        scm = [sb("scm%d" % i, [128, 512], BF16) for i in range(2)]
        ketm = [sb("ketm%d" % i, [128, 1024], BF16) for i in range(2)]
        PSF = es.enter_context(nc.psum_tensor("PSF", [128, 6, 512], F32))
        PST = es.enter_context(nc.psum_tensor("PST", [128, 2, 1024], BF16))

        fw = FW(nc)
        semnames = ["p_" + e for e in ENGS] + ["d_c", "d_x0", "d_x1", "d_x2", "d_x3", "d_p", "d_y", "d_st", "d_so"] + ["d_w%d" % i for i in range(NSLOT)] + ["d_wb%d" % i for i in range(NSLOT)] + ["d_wc%d" % i for i in range(NSLOT)] + ["d_wa"]
        for n in semnames:
            fw.sems[n] = es.enter_context(nc.semaphore(n))
        for b in range(6):
            fw.free["f"].append(b)
        for b in range(2):
            fw.free["t"].append(b)

        tc = []
        for dst, src in [(gam_t, gam), (gfin_t, gfin), (caw_t, caw), (fcw_t, fcw), (fcb_t, fcb), (wg2_t, wg2), (ident, identd), (tri, trid)]:
            tc.append(fw.dma("sync", "d_c", lambda e, dst=dst, src=src: e.dma_start(out=dst[:], in_=src)))
        t_c = tc[-1]
        t_walr = fw.dma("gpsimd", "d_wa", lambda e: e.dma_start(out=walr[:], in_=w_in[:, OFF["alr"]:OFF["alr"] + 16].rearrange("(k p) n -> p k n", p=128)))
        t_m0 = fw.op("vector", lambda e: e.memset(ones[:], 1.0))
        t_m1 = fw.op("vector", lambda e: e.memset(mhalf[:], -0.5))
        t_m2 = fw.op("vector", lambda e: e.memset(alr[:], 0.0))
        t_m3 = fw.op("vector", lambda e: e.memset(alr[32:33, :], 1.0), deps=[t_m2])
        t_m4 = fw.op("vector", lambda e: e.memset(S[:], 0.0))
        t_m5 = fw.op("vector", lambda e: e.memset(Sbf[:], 0.0))
        t_m6 = fw.op("vector", lambda e: e.memset(cvhalo[:], 0.0))
        t_m7 = fw.op("vector", lambda e: e.memset(uhalo[:], 0.0))
        t_m8 = fw.op("vector", lambda e: e.memset(stat[:], 1.0))
        t_t4 = None
        for h_ in range(4):
            t_t4 = fw.op("vector", lambda e, h_=h_: e.tensor_copy(out=tri4[:, h_ * 128:(h_ + 1) * 128], in_=tri[:, :]), deps=[t_c])
        INIT = [t_c, t_walr, t_m8, t_t4]
        for _e in ENGS:
            fw._emit_waits(_e, [t_c, t_walr, t_t4, t_m0, t_m1, t_m3, t_m4, t_m5, t_m6, t_m7, t_m8])

        blocks = []

        wkeys = {}

        def wsrc(w, r0, nk, c0, ncols):
            key = (w.tensor.name, r0, c0)
            if key not in wkeys:
                wkeys[key] = len(wkeys)
            return (w[r0:r0 + nk * 128, c0:c0 + ncols].rearrange("(k p) n -> p k n", p=128), nk, ncols, wkeys[key])

        def tile_blocks(kind, nhalf=2, tail=True):
            bl = []
            if kind == "pre":
                for h in range(4):
                    bl.append(wsrc(w_in, 0, KC, OFF["k"] + WB * h, WB))
                for jj in range(8):
                    bl.append(wsrc(w_in, 0, KC, OFF["v"] + WB * jj, WB))
                return bl
            if nhalf == 2:
                for jj in range(8):
                    bl.append(wsrc(w_in, 0, KC, OFF["v"] + WB * jj, WB))
            for h in range(4):
                bl.append(wsrc(w_in, 0, KC, OFF["q"] + WB * h, WB))
                bl.append(wsrc(w_in, 0, KC, OFF["k"] + WB * h, WB))
            for jj in range(8):
                bl.append(wsrc(w_in, 0, KC, OFF["v"] + WB * jj, WB))
            for jj in range(8):
                bl.append(wsrc(w_in, 0, KC, OFF["g"] + WB * jj, WB))
            for jj in range(8):
                bl.append(wsrc(w_b_out, 0, KC, WB * jj, WB))
                bl.append(wsrc(w_in, 0, KC, OFF["m_b"] + WB * jj, WB))
            for jj in range(8):
                for nm in ("c_a", "v_a", "b_a"):
                    bl.append(wsrc(w_in, 0, KC, OFF[nm] + WB * jj, WB))
            for jj in range(8):
                bl.append(wsrc(w_a_out, 0, KC, WB * jj, WB))
                bl.append(wsrc(w_in, 0, KC, OFF["m_a"] + WB * jj, WB))
            for _h in range(nhalf):
                for jj in range(8):
                    bl.append(wsrc(w_o, 0, KC, WB * jj, WB))
            for jj in range(22):
                bl.append(wsrc(w_up, 0, KC, WB * jj, WB))
                bl.append(wsrc(w_up, 0, KC, DFF + WB * jj, WB))
            if not tail:
                return bl
            for _h in range(nhalf):
                for cb0 in range(0, 8, 2):
                    for cb, rb in ((cb0, 0), (cb0, 1), (cb0 + 1, 0), (cb0 + 1, 1), (cb0, 2), (cb0 + 1, 2)):
                        bl.append(wsrc(w_down, 2048 * rb, 12 if rb == 2 else 16, WB * cb, WB))
            for _h in range(nhalf):
                for cb in range(8):
                    bl.append(wsrc(w_pg, 0, KC, WB * cb, WB))
                    bl.append(wsrc(w_ple, 0, 2, WB * cb, WB))
            return bl

        full_bl = tile_blocks("full")
        pre_own = tile_blocks("pre")
        own_idx = set(b_[3] for b_ in pre_own)
        extras_all = []
        seen_ = set()
        for b_ in full_bl:
            if b_[3] not in own_idx and b_[3] not in seen_:
                extras_all.append(b_)
                seen_.add(b_[3])
        NLEAVE = 40
        extras_all = extras_all[:len(extras_all) - NLEAVE]
        npre_ = len(pre_tiles)
        per_ = (len(extras_all) + npre_ - 1) // npre_
        extras = [extras_all[i_ * per_:(i_ + 1) * per_] for i_ in range(npre_)]
        pre_counts = []
        for t_ in range(npre_):
            n_t = len(extras[t_])
            no_ = len(pre_own)
            cnts = []
            for i_ in range(no_):
                lo_, hi_ = (i_ * n_t) // no_, ((i_ + 1) * n_t) // no_
                blocks.append(pre_own[i_])
                blocks.extend(extras[t_][lo_:hi_])
                cnts.append(hi_ - lo_)
            pre_counts.append(cnts)
        for it__, tt__ in enumerate(main_tiles + [NSAMP]):
            blocks.extend(tile_blocks("full", 2 if tt__ >= 512 else 1, tail=(it__ != 0)))
        wq = nc.dram_tensor("wq", [len(wkeys), 128, KC * WB], BF16, kind="Internal").ap()
        wst = {"issued": 0, "cur": -1, "load_tok": {}, "done_tok": {}, "wb_tok": {}, "conv": set()}

        def wget():
            i = wst["cur"] + 1
            wst["cur"] = i
            while wst["issued"] < min(len(blocks), i + NSLOT) and (wst["issued"] < NSLOT or (wst["issued"] - NSLOT) in wst["done_tok"]):
                m = wst["issued"]
                src, nk, ncols, widx = blocks[m]
                slot = m % NSLOT
                deps = wst["done_tok"].get(m - NSLOT, []) + wst["wb_tok"].get(m - NSLOT, [])
                qv = wq[widx, :, 0:nk * ncols].rearrange("p (k n) -> p k n", n=ncols)
                if widx in wst["conv"]:
                    wball = [("d_wb%d" % i_, fw.dma_cnt.get("d_wb%d" % i_, 0)) for i_ in range(NSLOT) if fw.dma_cnt.get("d_wb%d" % i_, 0) > 0]
                    wst["load_tok"][m] = fw.dma("sync", "d_w%d" % slot,
                                                lambda e, qv=qv, nk=nk, ncols=ncols, slot=slot: e.dma_start(out=wring[slot][:, 0:nk, 0:ncols], in_=qv),
                                                deps=deps + wball)
                else:
                    wst["load_tok"][m] = fw.dma("gpsimd", "d_wc%d" % slot,
                                                lambda e, src=src, nk=nk, ncols=ncols, slot=slot: e.dma_start(out=wring[slot][:, 0:nk, 0:ncols], in_=src),
                                                deps=deps)
                    wst["wb_tok"][m] = [fw.dma("sync", "d_wb%d" % slot,
                                               lambda e, qv=qv, nk=nk, ncols=ncols, slot=slot: e.dma_start(out=qv, in_=wring[slot][:, 0:nk, 0:ncols]),
                                               deps=[wst["load_tok"][m]])]
                    wst["conv"].add(widx)
                wst["issued"] += 1
            assert wst["issued"] > i, (i, wst["issued"])
            return wring[i % NSLOT], wst["load_tok"][i], i

        def wdone(i, toks):
            wst["done_tok"][i] = [t for t in toks if t is not None]

        def mm_group(out_ap, pairs, deps):
            n = len(pairs)
            tok = None
            for i, (l, r) in enumerate(pairs):
                tok = fw.op("tensor", lambda e, l=l, r=r, i=i: e.matmul(out_ap, lhsT=l, rhs=r, start=(i == 0), stop=(i == n - 1)),
                            deps=deps if i == 0 else (), signal=(i == n - 1))
            return tok

        rr = {"xs": 0, "w1": 0, "w2": 0, "w3": 0, "ub": 0, "yt": 0, "scm": 0, "ketm": 0, "oraw": 0}
        last_use = {}

        def ring(name, lst):
            i = rr[name]
            rr[name] = (i + 1) % len(lst)
            return lst[i], last_use.get((name, i), []), (name, i)

        def used(key, toks):
            last_use[key] = [t for t in toks if t is not None]

        def norm_p1(src_fn, subs, nt, ready_fn):
            hs = {}
            for s in subs:
                src = src_fn(s)
                xb, xdeps, xkey = ring("xs", xs)
                q0 = 16 + 4 * (xkey[1] % 2)
                t_sq = fw.op("scalar", lambda e, src=src, xb=xb, q0=q0: e.activation(out=xb[0:nt, :], in_=src, func=AF.Square, accum_out=stat[0:nt, q0:q0 + 1]), deps=ready_fn(s) + xdeps)
                t_ms = fw.op("vector", lambda e, q0=q0: e.tensor_scalar(out=stat[0:nt, q0 + 1:q0 + 2], in0=stat[0:nt, q0:q0 + 1], scalar1=1.0 / D, scalar2=EPS, op0=ALU.mult, op1=ALU.add), deps=[t_sq])
                t_rs = fw.op("gpsimd", lambda e, q0=q0: e.tensor_tensor(out=stat[0:nt, q0 + 2:q0 + 3], in0=stat[0:nt, q0 + 1:q0 + 2], in1=mhalf[0:nt, 0:1], op=ALU.pow), deps=[t_ms, t_m1])
                t_xs = fw.op("vector", lambda e, src=src, xb=xb, q0=q0: e.tensor_scalar(out=xb[0:nt, :], in0=src, scalar1=stat[0:nt, q0 + 2:q0 + 3], scalar2=None, op0=ALU.mult), deps=[t_rs, t_sq])
                hs[s] = (xb, xkey, t_xs)
            return hs

        def norm_p2(hs, subs, nt, gidx, actT_free_fn):
            toks = []
            per_s = {}
            for s in subs:
                xb, xkey, t_xs = hs[s]
                tl = []
                mine = []
                af = actT_free_fn(s)
                for g4 in range(4):
                    b, bd = fw.alloc("t")
                    tp = None
                    for j in range(4):
                        c = g4 * 4 + j
                        tp = fw.op("tensor", lambda e, c=c, j=j, b=b, xb=xb: e.transpose(out=PST[:, b, j * 128:j * 128 + nt], in_=xb[0:nt, c * 128:(c + 1) * 128], identity=ident[0:nt, 0:nt]),
                                   deps=[t_xs] + bd + INIT if j == 0 else (), signal=(j == 3))
                    te = None
                    for j in range(4):
                        c = g4 * 4 + j
                        if g4 % 2 == 0:
                            te = fw.op("vector", lambda e, c=c, j=j, b=b, s=s: e.tensor_scalar(out=actT[:, c, s * 128:s * 128 + nt], in0=PST[:, b, j * 128:j * 128 + nt], scalar1=gam_t[:, gidx, c:c + 1], scalar2=None, op0=ALU.mult),
                                       deps=[tp] + af)
                        else:
                            te = fw.op("scalar", lambda e, c=c, j=j, b=b, s=s: e.activation(out=actT[:, c, s * 128:s * 128 + nt], in_=PST[:, b, j * 128:j * 128 + nt], func=AF.Copy, scale=gam_t[:, gidx, c:c + 1]),
                                       deps=[tp] + af)
                    fw.release("t", b, [te])
                    tl.append(tp)
                    toks.append(te)
                    mine.append(te)
                used(xkey, tl)
                per_s[s] = mine
            return toks, per_s

        def norm_to_actT(src_fn, nsub, nt, gidx, ready_fn, actT_free_fn):
            toks, per_s = [], {}
            for s0 in range(0, nsub, 2):
                subs = list(range(s0, min(nsub, s0 + 2)))
                hs = norm_p1(src_fn, subs, nt, ready_fn)
                t_, p_ = norm_p2(hs, subs, nt, gidx, actT_free_fn)
                toks += t_
                per_s.update(p_)
            return toks, [per_s[s] for s in range(nsub)]

        state = {"actT_readers": [], "x_tok": None}

        def emit_tile(kind, TT, xrow0, prow0, yout, first_of_ctx_load=None, emit_state_out=None, nextra=0, next_x=None, extra_counts=None, parity=0, defer=False, skip_tail=False):
            nsub = max(1, TT // 128)
            nt = min(TT, 128)
            full = kind == "full"
            ke, vtm, eblast = KE_SETS[parity]
            PK = "_%d" % parity
            ecnt = list(extra_counts) if extra_counts is not None else []

            def conv_some():
                if ecnt:
                    for _ in range(ecnt.pop(0)):
                        (_sl, _tk, _wi) = wget()
                        wdone(_wi, [])
            pro = state.pop("pro", None)
            xfree = state.get("xres_free", [])
            t_x = {}
            for s in range(nsub):
                if pro is not None and s in pro["t_x"]:
                    t_x[s] = pro["t_x"][s]
                    continue
                if len(xfree) == nsub:
                    xd = xfree[s]
                else:
                    xd = [t for l_ in xfree for t in l_]
                t_x[s] = fw.dma("gpsimd", "d_x%d" % s, lambda e, s=s: e.dma_start(out=xres[0:nt, s, :], in_=xcat[xrow0 + s * nt:xrow0 + (s + 1) * nt, :]), deps=xd)
            ext = []
            if first_of_ctx_load is not None:
                ext = first_of_ctx_load()
            RBF = state.get("RA_free", [])
            arf = state["actT_readers"]
            arf_fn = (lambda s: arf[s]) if (isinstance(arf, dict) and len(arf) == nsub) else (lambda s: ([t for l_ in arf.values() for t in l_] if isinstance(arf, dict) else arf))
            vsplit = full and nsub >= 4
            hvA = list(range(0, nsub // 2)) if vsplit else list(range(nsub))
            hvB = list(range(nsub // 2, nsub)) if vsplit else []
            act_readers = []
            v_w = []
            vfree = state.get("vtm_free" + PK, [])

            def vproj(subs, rdy):
                for jj in range(8):
                    (wv_s, wv_t, wv_i) = wget()
                    tg = None
                    for s in subs:
                        b, bd = fw.alloc("f")
                        tg = mm_group(PSF[0:nt, b, 0:WB], [(actT[:, c, s * 128:s * 128 + nt], wv_s[:, c, 0:WB]) for c in range(KC)], deps=rdy + [wv_t] + bd)
                        tv = fw.op("scalar", lambda e, b=b, s=s, jj=jj: e.activation(out=vtm[0:nt, s, jj * WB:(jj + 1) * WB], in_=PSF[0:nt, b, 0:WB], func=AF.Copy), deps=[tg] + vfree + RBF)
                        fw.release("f", b, [tv])
                        v_w.append(tv)
                        act_readers.append(tg)
                    wdone(wv_i, [tg])
                    conv_some()

            src_fn = lambda s: xres[0:nt, s, :]
            tn = []
            tn_map = {}
            def norm_chunk(subs, hs=None):
                if hs is None:
                    hs = norm_p1(src_fn, subs, nt, lambda s: [t_x[s]])
                t_, p_ = norm_p2(hs, subs, nt, 0, arf_fn)
                tn.extend(t_)
                tn_map.update(p_)
                return t_
            tnA = []
            first = True
            for i0 in range(0, len(hvA), 2):
                subs = hvA[i0:i0 + 2]
                tnA += norm_chunk(subs, pro["hs"] if (pro is not None and first) else None)
                first = False
            if vsplit:
                vproj(hvA, tnA)
            for i0 in range(0, len(hvB), 2):
                norm_chunk(hvB[i0:i0 + 2])
            tn_ps = [tn_map[s] for s in range(nsub)]
            A_RDY = tn

            b, bd = fw.alloc("f")
            tga = mm_group(PSF[0:16, b, 0:TT], [(walr[:, c, 0:16], actT[:, c, 0:TT]) for c in range(KC)], deps=A_RDY + bd + INIT)
            t_al = fw.op("vector", lambda e, b=b: e.tensor_copy(out=alr[0:16, 0:TT], in_=PSF[0:16, b, 0:TT]), deps=[tga, t_m3] + state.get("alr_free", []))
            fw.release("f", b, [t_al])
            act_readers.append(tga)
            z_readers = []
            qk_w = []
            for h in range(4):
                if full:
                    (wq_s, wq_t, wq_i) = wget()
                (wk_s, wk_t, wk_i) = wget()
                for jc in range(2):
                    j = h * 2 + jc
                    bz, bd = fw.alloc("f")
                    tgz = mm_group(PSF[:, bz, 0:TT], [(wg2_t[0:33, j * 128:(j + 1) * 128], alr[0:33, 0:TT])], deps=[t_al] + bd + INIT)
                    z_readers.append(tgz)
                    wa, wad, wak = ring("w1", w1)
                    t1 = fw.op("scalar", lambda e, bz=bz, wa=wa: e.activation(out=wa[:, 0:TT], in_=PSF[:, bz, 0:TT], func=AF.Exp, scale=-1.0), deps=[tgz] + wad)
                    fw.release("f", bz, [t1])
                    t2 = fw.op("scalar", lambda e, wa=wa: e.activation(out=wa[:, 0:TT], in_=wa[:, 0:TT], func=AF.Ln, bias=1.0), deps=[t1])
                    wb_, wbd, wbk = ring("w2", w2)
                    ts = None
                    for s in range(nsub):
                        ts = fw.op("vector", lambda e, s=s, wa=wa, wb_=wb_: e.tensor_tensor_scan(out=wb_[:, s * 128:s * 128 + nt], data0=ones[:, 0:nt], data1=wa[:, s * 128:s * 128 + nt], initial=0.0, op0=ALU.mult, op1=ALU.add),
                                   deps=[t2, t_m0] + wbd if s == 0 else ())
                    used(wak, [ts])
                    wc, wcd, wck = ring("w3", w3)
                    t3 = fw.op("scalar", lambda e, wb_=wb_, wc=wc: e.activation(out=wc[:, 0:TT], in_=wb_[:, 0:TT], func=AF.Exp, scale=-1.0 / 16), deps=[ts] + wcd)
                    t3b = fw.op("vector", lambda e, wc=wc, j=j: e.tensor_copy(out=eblast[:, j, 0:nsub], in_=wc[:, nt - 1:TT:128] if nsub > 1 else wc[:, nt - 1:nt]), deps=[t3] + state.get("eblast_free" + PK, []))
                    t4 = fw.op("scalar", lambda e, wb_=wb_: e.activation(out=wb_[:, 0:TT], in_=wb_[:, 0:TT], func=AF.Exp, scale=1.0 / 16), deps=[t3])
                    if full:
                        bq, bd = fw.alloc("f")
                        tgq = mm_group(PSF[:, bq, 0:TT], [(wq_s[:, c, jc * 128:(jc + 1) * 128], actT[:, c, 0:TT]) for c in range(KC)], deps=[wq_t] + bd)
                        t5 = fw.op("vector", lambda e, bq=bq, wc=wc, j=j: e.scalar_tensor_tensor(out=qe[:, j, 0:TT], in0=PSF[:, bq, 0:TT], scalar=float(256 ** -0.5), in1=wc[:, 0:TT], op0=ALU.mult, op1=ALU.mult), deps=[tgq, t3] + RBF)
                        fw.release("f", bq, [t5])
                        qk_w.append(t5)
                        act_readers.append(tgq)
                        used(wck, [t5, t3b])
                    else:
                        used(wck, [t3b])
                    bk, bd = fw.alloc("f")
                    tgk = mm_group(PSF[:, bk, 0:TT], [(wk_s[:, c, jc * 128:(jc + 1) * 128], actT[:, c, 0:TT]) for c in range(KC)], deps=[wk_t] + bd)
                    t6 = fw.op("vector", lambda e, bk=bk, wb_=wb_, j=j: e.tensor_tensor(out=ke[:, j, 0:TT], in0=PSF[:, bk, 0:TT], in1=wb_[:, 0:TT], op=ALU.mult), deps=[tgk, t4] + RBF)
                    fw.release("f", bk, [t6])
                    used(wbk, [t6])
                    qk_w.append(t6)
                    act_readers.append(tgk)
                if full:
                    wdone(wq_i, [tgq])
                wdone(wk_i, [tgk])
                conv_some()
            state["alr_free"] = z_readers
            vproj(hvB if vsplit else hvA, A_RDY)
            o_w = []
            gla_readers = []
            G = {}

            def gla_T(s):
                c0 = s * 128
                kt, ktd, ktk = ring("ketm", ketm)
                tkt = []
                for g4 in range(2):
                    b, bd = fw.alloc("t")
                    tp = None
                    for j4 in range(4):
                        j = g4 * 4 + j4
                        tp = fw.op("tensor", lambda e, j=j, j4=j4, b=b, c0=c0: e.transpose(out=PST[0:nt, b, j4 * 128:(j4 + 1) * 128], in_=ke[:, j, c0:c0 + nt], identity=ident[:, :]),
                                   deps=qk_w + bd + INIT if j4 == 0 else (), signal=(j4 == 3))
                    te = fw.op("scalar", lambda e, b=b, g4=g4, kt=kt: e.activation(out=kt[0:nt, g4 * 512:(g4 + 1) * 512], in_=PST[0:nt, b, 0:512], func=AF.Copy), deps=[tp] + ktd)
                    fw.release("t", b, [te])
                    tkt.append(te)
                    gla_readers.append(tp)
                G[("kt", s)] = (kt, ktk, tkt)

            def gla_SC(s):
                c0 = s * 128
                b, bd = fw.alloc("f")
                tgs = None
                for h in range(4):
                    tgs = mm_group(PSF[0:nt, b, h * 128:h * 128 + nt], [(ke[:, h * 2 + kc, c0:c0 + nt], qe[:, h * 2 + kc, c0:c0 + nt]) for kc in range(2)], deps=qk_w + bd if h == 0 else [])
                sc, scd, sck = ring("scm", scm)
                tm_ = fw.op("vector", lambda e, b=b, sc=sc: e.tensor_tensor(out=sc[0:nt, :].rearrange("p (h t) -> p h t", t=128)[:, :, 0:nt], in0=PSF[0:nt, b, :].rearrange("p (h t) -> p h t", t=128)[:, :, 0:nt], in1=tri4[0:nt, :].rearrange("p (h t) -> p h t", t=128)[:, :, 0:nt], op=ALU.mult), deps=[tgs] + scd + INIT)
                fw.release("f", b, [tm_])
                gla_readers.append(tgs)
                G[("sc", s)] = (sc, sck, tm_)

            def gla_O(s):
                c0 = s * 128
                sc, sck, tm_ = G[("sc", s)]
                orw, ord_, ork = ring("xs", xs)
                t_or = []
                tgos = []
                t_sq = None
                for h in range(4):
                    bo, bd = fw.alloc("f")
                    pairs = [(qe[:, h * 2 + kc, c0:c0 + nt], Sbf[:, h * 2 + kc, :]) for kc in range(2)] + [(sc[0:nt, h * 128:h * 128 + nt], vtm[0:nt, s, h * 512:(h + 1) * 512])]
                    tgo = mm_group(PSF[0:nt, bo, :], pairs, deps=[tm_] + v_w + G["S_tok"] + bd)
                    t_sq = fw.op("scalar", lambda e, bo=bo, h=h, orw=orw: e.activation(out=orw[0:nt, h * 512:(h + 1) * 512], in_=PSF[0:nt, bo, :], func=AF.Square, accum_out=stat[0:nt, 4 + h:5 + h]), deps=[tgo] + state.get("stat_free", []) + ord_)
                    t_cp = fw.op("vector", lambda e, bo=bo, h=h, orw=orw: e.tensor_copy(out=orw[0:nt, h * 512:(h + 1) * 512], in_=PSF[0:nt, bo, :]), deps=[tgo, t_sq] + ord_)
                    fw.release("f", bo, [t_cp])
                    t_or.append(t_cp)
                    tgos.append(tgo)
                    gla_readers.append(tgo)
                used(sck, [tgos[-1]])
                t_ms = fw.op("vector", lambda e: e.tensor_scalar(out=stat[0:nt, 8:12], in0=stat[0:nt, 4:8], scalar1=1.0 / 512, scalar2=EPS, op0=ALU.mult, op1=ALU.add), deps=[t_sq])
                t_rs = fw.op("gpsimd", lambda e: e.tensor_tensor(out=stat[0:nt, 12:16], in0=stat[0:nt, 8:12], in1=mhalf[0:nt, 0:4], op=ALU.pow), deps=[t_ms, t_m1])
                tn_ = []
                for h in range(4):
                    tn_.append(fw.op("vector", lambda e, h=h, orw=orw: e.tensor_scalar(out=orw[0:nt, h * 512:(h + 1) * 512], in0=orw[0:nt, h * 512:(h + 1) * 512], scalar1=stat[0:nt, 12 + h:13 + h], scalar2=None, op0=ALU.mult), deps=[t_rs] + t_or))
                state["stat_free"] = [t_ms] + tn_
                G[("o", s)] = (orw, ork, tn_)
                G[("tgo", s)] = tgos

            def gla_P(s):
                kt, ktk, tkt = G[("kt", s)]
                tgos = G.get(("tgo", s), [None] * 4)
                kt_readers = []
                newS = []
                for h in range(4):
                    tgo = tgos[h]
                    for kc in range(2):
                        j = h * 2 + kc
                        bp, bd = fw.alloc("f")
                        tgp = mm_group(PSF[:, bp, :], [(kt[0:nt, j * 128:(j + 1) * 128], vtm[0:nt, s, h * 512:(h + 1) * 512])], deps=tkt + v_w + bd)
                        kt_readers.append(tgp)
                        if full:
                            tu1 = fw.op("gpsimd", lambda e, j=j, s=s: e.tensor_scalar(out=S[:, j, :], in0=S[:, j, :], scalar1=eblast[:, j, s:s + 1], scalar2=1.0, op0=ALU.mult, op1=ALU.mult), deps=G["S_tok"])
                        else:
                            tu1 = fw.op("vector", lambda e, j=j, s=s: e.tensor_scalar(out=S[:, j, :], in0=S[:, j, :], scalar1=eblast[:, j, s:s + 1], scalar2=None, op0=ALU.mult), deps=G["S_tok"])
                        tu2 = fw.op("vector", lambda e, j=j, bp=bp, s=s: e.scalar_tensor_tensor(out=S[:, j, :], in0=PSF[:, bp, :], scalar=eblast[:, j, s:s + 1], in1=S[:, j, :], op0=ALU.mult, op1=ALU.add), deps=[tgp, tu1])
                        fw.release("f", bp, [tu2])
                        tu3 = fw.op("scalar", lambda e, j=j: e.activation(out=Sbf[:, j, :], in_=S[:, j, :], func=AF.Copy), deps=[tu2, tgo])
                        newS += [tu2, tu3]
                        gla_readers.append(tgp)
                used(ktk, kt_readers)
                G["S_tok"] = newS

            def gla_N(s):
                c0 = s * 128
                orw, ork, tn_ = G[("o", s)]
                tls = []
                for g4 in range(4):
                    b, bd = fw.alloc("t")
                    tp = None
                    for j4 in range(4):
                        c = g4 * 4 + j4
                        tp = fw.op("tensor", lambda e, c=c, j4=j4, b=b, orw=orw: e.transpose(out=PST[:, b, j4 * 128:j4 * 128 + nt], in_=orw[0:nt, c * 128:(c + 1) * 128], identity=ident[0:nt, 0:nt]),
                                   deps=tn_ + bd if j4 == 0 else (), signal=(j4 == 3))
                    te = None
                    for j4 in range(4):
                        c = g4 * 4 + j4
                        if g4 % 2 == 0:
                            te = fw.op("vector", lambda e, c=c, j4=j4, b=b, c0=c0: e.tensor_scalar(out=R1[:, c, c0:c0 + nt], in0=PST[:, b, j4 * 128:j4 * 128 + nt], scalar1=gam_t[:, 3, c:c + 1], scalar2=None, op0=ALU.mult), deps=[tp] + RBF)
                        else:
                            te = fw.op("scalar", lambda e, c=c, j4=j4, b=b, c0=c0: e.activation(out=R1[:, c, c0:c0 + nt], in_=PST[:, b, j4 * 128:j4 * 128 + nt], func=AF.Copy, scale=gam_t[:, 3, c:c + 1]), deps=[tp] + RBF)
                        o_w.append(te)
                    fw.release("t", b, [te])
                    tls.append(tp)
                used(ork, tls)

            if not full:
                state["actT_readers"] = act_readers
                state["xres_free"] = tn_ps
                while ecnt:
                    conv_some()

                def run_B():
                    G["S_tok"] = state.get("S_tok", [t_m4, t_m5]) + ext
                    for s in range(min(2, nsub)):
                        gla_T(s)
                    for s in range(nsub):
                        gla_P(s)
                        if s + 2 < nsub:
                            gla_T(s + 2)
                    state["S_tok"] = G["S_tok"]
                    state["vtm_free" + PK] = list(gla_readers)
                    state["eblast_free" + PK] = G["S_tok"]
                    state["pre_readers" + PK] = list(gla_readers) + G["S_tok"]
                if defer:
                    return run_B
                run_B()
                return None
            G["S_tok"] = state.get("S_tok", [t_m4, t_m5]) + ext
            for s in range(min(2, nsub)):
                gla_T(s)
                if full:
                    gla_SC(s)
            for s in range(nsub):
                if full:
                    gla_O(s)
                gla_P(s)
                if full and s >= 1:
                    gla_N(s - 1)
                if s + 2 < nsub:
                    gla_T(s + 2)
                    if full:
                        gla_SC(s + 2)
            if full:
                gla_N(nsub - 1)
            S_tok = G["S_tok"]
            state["S_tok"] = S_tok
            state["vtm_free" + PK] = gla_readers
            state["eblast_free" + PK] = S_tok
            og_w = []
            for jj in range(8):
                (wg_s, wg_t, wg_i) = wget()
                for jc in range(2):
                    j = jj * 2 + jc
                    b, bd = fw.alloc("f")
                    tg = mm_group(PSF[:, b, 0:TT], [(wg_s[:, c, jc * 128:(jc + 1) * 128], actT[:, c, 0:TT]) for c in range(KC)], deps=[wg_t] + bd)
                    wa, wad, wak = ring("w1", w1)
                    t1 = fw.op("scalar", lambda e, b=b, wa=wa: e.activation(out=wa[:, 0:TT], in_=PSF[:, b, 0:TT], func=AF.Tanh, scale=0.5), deps=[tg] + wad)
                    wb_, wbd, wbk = ring("w2", w2)
                    t2 = fw.op("vector", lambda e, b=b, wa=wa, wb_=wb_: e.scalar_tensor_tensor(out=wb_[:, 0:TT], in0=wa[:, 0:TT], scalar=1.0, in1=PSF[:, b, 0:TT], op0=ALU.add, op1=ALU.mult), deps=[t1] + wbd)
                    fw.release("f", b, [t2])
                    t3 = fw.op("vector", lambda e, wb_=wb_, j=j: e.scalar_tensor_tensor(out=R1[:, j, 0:TT], in0=wb_[:, 0:TT], scalar=0.5, in1=R1[:, j, 0:TT], op0=ALU.mult, op1=ALU.mult), deps=[t2] + o_w)
                    used(wak, [t2]); used(wbk, [t3])
                    og_w.append(t3)
                    act_readers.append(tg)
                wdone(wg_i, [tg])
            mg2_w = []
            for jj in range(8):
                (wb_s, wb_t, wb_i), (wm_s, wm_t, wm_i) = wget(), wget()
                for jc in range(2):
                    j = jj * 2 + jc
                    by, bd = fw.alloc("f")
                    tgy = mm_group(PSF[:, by, 0:TT], [(wb_s[:, c, jc * 128:(jc + 1) * 128], R1[:, c, 0:TT]) for c in range(KC)], deps=og_w + [wb_t] + bd)
                    bm, bd = fw.alloc("f")
                    tgm = mm_group(PSF[:, bm, 0:TT], [(wm_s[:, c, jc * 128:(jc + 1) * 128], actT[:, c, 0:TT]) for c in range(KC)], deps=[wm_t] + bd)
                    wa, wad, wak = ring("w1", w1)
                    t1 = fw.op("scalar", lambda e, bm=bm, wa=wa: e.activation(out=wa[:, 0:TT], in_=PSF[:, bm, 0:TT], func=AF.Tanh, scale=0.5), deps=[tgm] + wad)
                    fw.release("f", bm, [t1])
                    wb_, wbd, wbk = ring("w2", w2)
                    t2 = fw.op("vector", lambda e, by=by, wa=wa, wb_=wb_: e.scalar_tensor_tensor(out=wb_[:, 0:TT], in0=wa[:, 0:TT], scalar=1.0, in1=PSF[:, by, 0:TT], op0=ALU.add, op1=ALU.mult), deps=[tgy, t1] + wbd)
                    fw.release("f", by, [t2])
                    t3 = fw.op("vector", lambda e, wb_=wb_, j=j: e.tensor_scalar(out=R2[:, j, 0:TT], in0=wb_[:, 0:TT], scalar1=0.5, scalar2=None, op0=ALU.mult), deps=[t2] + gla_readers)
                    used(wak, [t2]); used(wbk, [t3])
                    mg2_w.append(t3)
                    act_readers.append(tgm)
                wdone(wb_i, [tgy]); wdone(wm_i, [tgm])
            R1_rd7 = [tgy]
            ya_w = []
            for jj in range(8):
                (wsl, wtok, wi) = wget()
                ca = {}
                tg = None
                for jc in range(2):
                    b_, bd = fw.alloc("f")
                    tg = mm_group(PSF[:, b_, 0:TT], [(wsl[:, c, jc * 128:(jc + 1) * 128], actT[:, c, 0:TT]) for c in range(KC)], deps=A_RDY + [wtok] + bd)
                    wa, wad, wak = ring("w1", w1)
                    t1 = fw.op("scalar", lambda e, b_=b_, wa=wa: e.activation(out=wa[:, 0:TT], in_=PSF[:, b_, 0:TT], func=AF.Copy), deps=[tg] + wad)
                    fw.release("f", b_, [t1])
                    ca[jc] = (wa, wak, t1)
                    act_readers.append(tg)
                wdone(wi, [tg])
                (wsl, wtok, wi) = wget()
                cv = {}
                for jc in range(2):
                    j = jj * 2 + jc
                    wa, wak, t1 = ca[jc]
                    bv, bd = fw.alloc("f")
                    tg = mm_group(PSF[:, bv, 0:TT], [(wsl[:, c, jc * 128:(jc + 1) * 128], actT[:, c, 0:TT]) for c in range(KC)], deps=[wtok] + bd)
                    tgv = tg
                    ubf, ubd, ubk = ring("ub", ub)
                    t2 = fw.op("vector", lambda e, bv=bv, wa=wa, ubf=ubf: e.tensor_tensor(out=ubf[:, 2:2 + TT], in0=PSF[:, bv, 0:TT], in1=wa[:, 0:TT], op=ALU.mult), deps=[tgv, t1] + ubd)
                    t2h = fw.op("vector", lambda e, ubf=ubf, j=j: e.tensor_copy(out=ubf[:, 0:2], in_=cvhalo[:, j, :]), deps=ubd + [t_m6] + ext)
                    rel = [t2]
                    if emit_state_out is not None:
                        t2s = fw.op("vector", lambda e, bv=bv, wa=wa, j=j: e.tensor_tensor(out=cvst[:, j, :], in0=PSF[:, bv, TT - 2:TT], in1=wa[:, TT - 2:TT], op=ALU.mult), deps=[tgv, t1] + state.get("cvst_free", []))
                        rel.append(t2s)
                    fw.release("f", bv, rel)
                    used(wak, rel)
                    wb_, wbd, wbk = ring("w2", w2)
                    t3 = fw.op("vector", lambda e, ubf=ubf, wb_=wb_, j=j: e.tensor_scalar(out=wb_[:, 0:TT], in0=ubf[:, 2:2 + TT], scalar1=caw_t[:, j, 2:3], scalar2=None, op0=ALU.mult), deps=[t2, t2h] + wbd + INIT)
                    t4 = fw.op("vector", lambda e, ubf=ubf, wb_=wb_, j=j: e.scalar_tensor_tensor(out=wb_[:, 0:TT], in0=ubf[:, 1:1 + TT], scalar=caw_t[:, j, 1:2], in1=wb_[:, 0:TT], op0=ALU.mult, op1=ALU.add), deps=[t3])
                    t5 = fw.op("vector", lambda e, ubf=ubf, wb_=wb_, j=j: e.scalar_tensor_tensor(out=wb_[:, 0:TT], in0=ubf[:, 0:TT], scalar=caw_t[:, j, 0:1], in1=wb_[:, 0:TT], op0=ALU.mult, op1=ALU.add), deps=[t4])
                    t5h = fw.op("vector", lambda e, ubf=ubf, j=j: e.tensor_copy(out=cvhalo[:, j, :], in_=ubf[:, TT:TT + 2]), deps=[t5, t2h])
                    used(ubk, [t5h])
                    cv[jc] = (wb_, wbk, t5)
                    act_readers.append(tg)
                wdone(wi, [tg])
                (wsl, wtok, wi) = wget()
                for jc in range(2):
                    j = jj * 2 + jc
                    wb_, wbk, t5 = cv[jc]
                    bb, bd = fw.alloc("f")
                    tg = mm_group(PSF[:, bb, 0:TT], [(wsl[:, c, jc * 128:(jc + 1) * 128], actT[:, c, 0:TT]) for c in range(KC)], deps=[wtok] + bd)
                    t6 = fw.op("vector", lambda e, bb=bb, wb_=wb_, j=j: e.tensor_tensor(out=R1[:, j, 0:TT], in0=PSF[:, bb, 0:TT], in1=wb_[:, 0:TT], op=ALU.mult), deps=[tg, t5] + R1_rd7)
                    fw.release("f", bb, [t6])
                    used(wbk, [t6])
                    ya_w.append(t6)
                    act_readers.append(tg)
                wdone(wi, [tg])
            mg_w = []
            for jj in range(8):
                (wa_s, wa_t, wa_i), (wm_s, wm_t, wm_i) = wget(), wget()
                for jc in range(2):
                    j = jj * 2 + jc
                    by, bd = fw.alloc("f")
                    tgy = mm_group(PSF[:, by, 0:TT], [(wa_s[:, c, jc * 128:(jc + 1) * 128], R1[:, c, 0:TT]) for c in range(KC)], deps=ya_w + [wa_t] + bd)
                    bm, bd = fw.alloc("f")
                    tgm = mm_group(PSF[:, bm, 0:TT], [(wm_s[:, c, jc * 128:(jc + 1) * 128], actT[:, c, 0:TT]) for c in range(KC)], deps=[wm_t] + bd)
                    wa, wad, wak = ring("w1", w1)
                    t1 = fw.op("scalar", lambda e, bm=bm, wa=wa: e.activation(out=wa[:, 0:TT], in_=PSF[:, bm, 0:TT], func=AF.Tanh, scale=0.5), deps=[tgm] + wad)
                    fw.release("f", bm, [t1])
                    wb_, wbd, wbk = ring("w2", w2)
                    t2 = fw.op("vector", lambda e, by=by, wa=wa, wb_=wb_: e.scalar_tensor_tensor(out=wb_[:, 0:TT], in0=wa[:, 0:TT], scalar=1.0, in1=PSF[:, by, 0:TT], op0=ALU.add, op1=ALU.mult), deps=[tgy, t1] + wbd)
                    fw.release("f", by, [t2])
                    t3 = fw.op("vector", lambda e, wb_=wb_, j=j: e.scalar_tensor_tensor(out=R2[:, j, 0:TT], in0=wb_[:, 0:TT], scalar=0.5, in1=R2[:, j, 0:TT], op0=ALU.mult, op1=ALU.add), deps=[t2] + mg2_w)
                    used(wak, [t2]); used(wbk, [t3])
                    mg_w.append(t3)
                    act_readers.append(tgm)
                wdone(wa_i, [tgy]); wdone(wm_i, [tgm])
            halves = [list(range(0, nsub // 2)), list(range(nsub // 2, nsub))] if nsub >= 2 else [[0]]
            x1_ps = {}
            tn2 = []
            pend = None
            all_act_readers = act_readers
            for hv in halves:
                for jj in range(8):
                    (wo_s, wo_t, wo_i) = wget()
                    for s in hv:
                        b, bd = fw.alloc("f")
                        tg = mm_group(PSF[0:nt, b, 0:WB], [(R2[:, c, s * 128:s * 128 + nt], wo_s[:, c, 0:WB]) for c in range(KC)], deps=mg_w + [wo_t] + bd)
                        ta = fw.op("vector", lambda e, b=b, s=s, jj=jj: e.tensor_tensor(out=xres[0:nt, s, jj * WB:(jj + 1) * WB], in0=PSF[0:nt, b, 0:WB], in1=xres[0:nt, s, jj * WB:(jj + 1) * WB], op=ALU.add), deps=[tg] + tn)
                        fw.release("f", b, [ta])
                        x1_ps.setdefault(s, []).append(ta)
                    wdone(wo_i, [tg])
                if pend is not None:
                    t_, _p = norm_p2(pend[0], pend[1], nt, 1, lambda s: all_act_readers)
                    tn2 += t_
                hs_ = norm_p1(lambda s: xres[0:nt, s, :], hv, nt, lambda s: x1_ps[s])
                pend = (hs_, hv)
            t_, _p = norm_p2(pend[0], pend[1], nt, 1, lambda s: all_act_readers)
            tn2 += t_
            ra_readers = [tg]
            act_readers = []
            h_w = []
            for jj in range(22):
                (wv_s, wv_t, wv_i), (wg_s, wg_t, wg_i) = wget(), wget()
                for jc in range(2):
                    j = jj * 2 + jc
                    res = []
                    for (wsl, wtok, ch) in ((wv_s, wv_t, j), (wg_s, wg_t, 44 + j)):
                        b, bd = fw.alloc("f")
                        tg = mm_group(PSF[:, b, 0:TT], [(wsl[:, c, jc * 128:(jc + 1) * 128], actT[:, c, 0:TT]) for c in range(KC)], deps=tn2 + [wtok] + bd)
                        act_readers.append(tg)
                        ubf, ubd, ubk = ring("ub", ub)
                        t1 = fw.op("scalar", lambda e, b=b, ubf=ubf: e.activation(out=ubf[:, 2:2 + TT], in_=PSF[:, b, 0:TT], func=AF.Copy), deps=[tg] + ubd)
                        t1h = fw.op("gpsimd", lambda e, ubf=ubf, ch=ch: e.tensor_copy(out=ubf[:, 0:2], in_=uhalo[:, ch, :]), deps=ubd + [t_m7] + ext)
                        if emit_state_out is not None:
                            t1s = fw.op("scalar", lambda e, b=b, ch=ch: e.activation(out=ust[:, ch, :], in_=PSF[:, b, TT - 2:TT], func=AF.Copy), deps=[tg] + state.get("ust_free", []))
                            fw.release("f", b, [t1, t1s])
                        else:
                            fw.release("f", b, [t1])
                        wb_, wbd, wbk = ring("w2", w2) if ch < 44 else ring("w3", w3)
                        t3 = fw.op("vector", lambda e, ubf=ubf, wb_=wb_, ch=ch: e.tensor_scalar(out=wb_[:, 0:TT], in0=ubf[:, 2:2 + TT], scalar1=fcw_t[:, ch, 2:3], scalar2=fcb_t[:, ch:ch + 1], op0=ALU.mult, op1=ALU.add), deps=[t1, t1h] + wbd + INIT)
                        t4 = fw.op("vector", lambda e, ubf=ubf, wb_=wb_, ch=ch: e.scalar_tensor_tensor(out=wb_[:, 0:TT], in0=ubf[:, 1:1 + TT], scalar=fcw_t[:, ch, 1:2], in1=wb_[:, 0:TT], op0=ALU.mult, op1=ALU.add), deps=[t3])
                        t5 = fw.op("vector", lambda e, ubf=ubf, wb_=wb_, ch=ch: e.scalar_tensor_tensor(out=wb_[:, 0:TT], in0=ubf[:, 0:TT], scalar=fcw_t[:, ch, 0:1], in1=wb_[:, 0:TT], op0=ALU.mult, op1=ALU.add), deps=[t4])
                        t5h = fw.op("gpsimd", lambda e, ubf=ubf, ch=ch: e.tensor_copy(out=uhalo[:, ch, :], in_=ubf[:, TT:TT + 2]), deps=[t5, t1h])
                        used(ubk, [t5h, t5])
                        res.append((wb_, wbk, t5))
                    (uv, uvk, tv5), (ug, ugk, tg5) = res
                    wa, wad, wak = ring("w1", w1)
                    t6 = fw.op("scalar", lambda e, ug=ug, wa=wa: e.activation(out=wa[:, 0:TT], in_=ug[:, 0:TT], func=AF.Tanh, scale=0.5), deps=[tg5] + wad)
                    t7 = fw.op("vector", lambda e, ug=ug, wa=wa: e.scalar_tensor_tensor(out=wa[:, 0:TT], in0=wa[:, 0:TT], scalar=1.0, in1=ug[:, 0:TT], op0=ALU.add, op1=ALU.mult), deps=[t6])
                    t8 = fw.op("vector", lambda e, uv=uv, wa=wa, j=j: e.scalar_tensor_tensor(out=hbuf[:, j, 0:TT], in0=wa[:, 0:TT], scalar=0.5, in1=uv[:, 0:TT], op0=ALU.mult, op1=ALU.mult), deps=[t7, tv5] + ra_readers)
                    used(wak, [t8]); used(uvk, [t8]); used(ugk, [t7])
                    h_w.append(t8)
                wdone(wv_i, [tg]); wdone(wg_i, [tg])
            if skip_tail:
                state["RA_free"] = h_w[-1:] + ra_readers
                state["actT_readers"] = act_readers
                state["xres_free"] = [list(tn2)]
                return None
            x2_ps = {}
            tn3 = []
            pend = None
            all_act_readers = act_readers
            tg = None
            for hv in halves:
                for cb0 in range(0, 8, 2):
                  banks = {}
                  for cb_ in (cb0, cb0 + 1):
                    for s in hv:
                        b, bd = fw.alloc("f")
                        banks[(cb_, s)] = (b, bd)
                  for cb, rb in ((cb0, 0), (cb0, 1), (cb0 + 1, 0), (cb0 + 1, 1), (cb0, 2), (cb0 + 1, 2)):
                        nk = 12 if rb == 2 else 16
                        (wd_s, wd_t, wd_i) = wget()
                        hdep = [h_w[rb * 16 + nk - 1]]
                        for s in hv:
                            b, bd = banks[(cb, s)]
                            for c in range(nk):
                                kc = rb * 16 + c
                                first = (kc == 0)
                                last = (kc == 43)
                                tg = fw.op("tensor", lambda e, b=b, s=s, kc=kc, c=c, wd_s=wd_s, first=first, last=last: e.matmul(PSF[0:nt, b, 0:WB], lhsT=hbuf[:, kc, s * 128:s * 128 + nt], rhs=wd_s[:, c, 0:WB], start=first, stop=last),
                                           deps=(hdep + [wd_t] + bd) if c == 0 else (), signal=(c == nk - 1))
                            if rb == 2:
                                ta = fw.op("vector", lambda e, b=b, s=s, cb=cb: e.tensor_tensor(out=xres[0:nt, s, cb * WB:(cb + 1) * WB], in0=PSF[0:nt, b, 0:WB], in1=xres[0:nt, s, cb * WB:(cb + 1) * WB], op=ALU.add), deps=[tg] + tn2)
                                fw.release("f", b, [ta])
                                x2_ps.setdefault(s, []).append(ta)
                        wdone(wd_i, [tg])
                if pend is not None:
                    t_, _p = norm_p2(pend[0], pend[1], nt, 2, lambda s: all_act_readers)
                    tn3 += t_
                hs_ = norm_p1(lambda s: xres[0:nt, s, :], hv, nt, lambda s: x2_ps[s])
                pend = (hs_, hv)
            tg11 = tg
            t_p = fw.dma("gpsimd", "d_p", lambda e: e.dma_start(out=pin[0:nt, 0:nsub, :], in_=pcat[prow0:prow0 + TT, :].rearrange("(s p) d -> p s d", p=nt)), deps=[tg11])
            t_, _p = norm_p2(pend[0], pend[1], nt, 2, lambda s: all_act_readers)
            tn3 += t_
            act_readers = []
            t_pb = fw.op("vector", lambda e: e.tensor_copy(out=pbf[0:nt, 0:nsub, :], in_=pin[0:nt, 0:nsub, :]), deps=[t_p, tg11] + state.get("pbf_free", []))
            state["pin_free"] = [t_pb]
            pT_w = []
            tps = []
            for s in range(nsub):
                b, bd = fw.alloc("t")
                tp = None
                for c in range(2):
                    tp = fw.op("tensor", lambda e, c=c, b=b, s=s: e.transpose(out=PST[:, b, c * 128:c * 128 + nt], in_=pbf[0:nt, s, c * 128:(c + 1) * 128], identity=ident[0:nt, 0:nt]),
                               deps=[t_pb] + bd + state.get("pT_free", []) if c == 0 else (), signal=(c == 1))
                te = None
                for c in range(2):
                    te = fw.op("scalar", lambda e, c=c, b=b, s=s: e.activation(out=pT[:, c, s * 128:s * 128 + nt], in_=PST[:, b, c * 128:c * 128 + nt], func=AF.Copy), deps=[tp] + state.get("pT_free", []))
                    pT_w.append(te)
                fw.release("t", b, [te])
                tps.append(tp)
            state["pbf_free"] = tps
            x3_ps = {}
            pT_r = []
            ar_ps = {}
            youts = {}
            ylast = []
            for hv in halves:
                for cb in range(8):
                    (wg_s, wg_t, wg_i), (wp_s, wp_t, wp_i) = wget(), wget()
                    for s in hv:
                        bg, bd = fw.alloc("f")
                        tgg = mm_group(PSF[0:nt, bg, 0:WB], [(actT[:, c, s * 128:s * 128 + nt], wg_s[:, c, 0:WB]) for c in range(KC)], deps=tn3 + [wg_t] + bd)
                        bp, bd = fw.alloc("f")
                        tgp = mm_group(PSF[0:nt, bp, 0:WB], [(pT[:, c, s * 128:s * 128 + nt], wp_s[:, c, 0:WB]) for c in range(2)], deps=pT_w + [wp_t] + bd)
                        ar_ps.setdefault(s, []).append(tgg)
                        pT_r.append(tgp)
                        wa, wad, wak = ring("w1", w1)
                        t1 = fw.op("scalar", lambda e, bg=bg, wa=wa: e.activation(out=wa[0:nt, 0:WB], in_=PSF[0:nt, bg, 0:WB], func=AF.Tanh, scale=0.5), deps=[tgg] + wad)
                        fw.release("f", bg, [t1])
                        t2 = fw.op("vector", lambda e, bp=bp, wa=wa: e.scalar_tensor_tensor(out=wa[0:nt, 0:WB], in0=wa[0:nt, 0:WB], scalar=1.0, in1=PSF[0:nt, bp, 0:WB], op0=ALU.add, op1=ALU.mult), deps=[tgp, t1])
                        fw.release("f", bp, [t2])
                        t3 = fw.op("vector", lambda e, wa=wa, s=s, cb=cb: e.scalar_tensor_tensor(out=xres[0:nt, s, cb * WB:(cb + 1) * WB], in0=wa[0:nt, 0:WB], scalar=0.5, in1=xres[0:nt, s, cb * WB:(cb + 1) * WB], op0=ALU.mult, op1=ALU.add), deps=[t2] + tn3)
                        used(wak, [t3])
                        x3_ps.setdefault(s, []).append(t3)
                    wdone(wg_i, [tgg]); wdone(wp_i, [tgp])
                if hv is not halves[0] and state.get("pro_x") is not None:
                    txn, hvn = state.pop("pro_x")
                    hsn = norm_p1(lambda s: xres[0:nt, s, :], hvn, nt, lambda s: [txn[s]])
                    state["pro"] = {"t_x": txn, "hs": hsn}
                for s in hv:
                    yt, ytd, ytk = ring("yt", ytile)
                    t_sq = fw.op("scalar", lambda e, s=s, yt=yt: e.activation(out=yt[0:nt, :], in_=xres[0:nt, s, :], func=AF.Square, accum_out=stat[0:nt, 0:1]), deps=x3_ps[s] + ylast + ytd + [tg11])
                    t_ms = fw.op("vector", lambda e: e.tensor_scalar(out=stat[0:nt, 1:2], in0=stat[0:nt, 0:1], scalar1=1.0 / D, scalar2=EPS, op0=ALU.mult, op1=ALU.add), deps=[t_sq])
                    t_rs = fw.op("gpsimd", lambda e: e.tensor_tensor(out=stat[0:nt, 2:3], in0=stat[0:nt, 1:2], in1=mhalf[0:nt, 0:1], op=ALU.pow), deps=[t_ms, t_m1])
                    t_y = fw.op("vector", lambda e, s=s, yt=yt: e.scalar_tensor_tensor(out=yt[0:nt, :], in0=xres[0:nt, s, :], scalar=stat[0:nt, 2:3], in1=gfin_t[0:nt, :], op0=ALU.mult, op1=ALU.mult), deps=[t_rs, t_sq] + INIT)
                    youts[s] = t_y
                    ylast = [t_y]
                    if yout is not None:
                        t_o = fw.dma("gpsimd", "d_y", lambda e, s=s, yt=yt: e.dma_start(out=yout[s * 128:s * 128 + nt, :], in_=yt[0:nt, :]), deps=[t_y])
                        used(ytk, [t_o])
                        state["final"].append(t_o)
                    else:
                        used(ytk, [t_y])
                if next_x is not None and hv is halves[0] and len(halves) == 2 and nsub == 4:
                    txn = {}
                    for s in hv:
                        txn[s] = fw.dma("gpsimd", "d_x%d" % s, lambda e, s=s: e.dma_start(out=xres[0:nt, s, :], in_=xcat[next_x + s * nt:next_x + (s + 1) * nt, :]), deps=[youts[s]])
                    state["pro_x"] = (txn, hv)
            state["pT_free"] = pT_r
            state["actT_readers"] = ar_ps
            youts = [youts[s] for s in range(nsub)]
            state["xres_free"] = [[t] for t in youts]
            state["RA_free"] = [tg11, t_pb] + pT_r[-1:] + tps[-1:] + last_use.get(("yt", 0), []) + last_use.get(("yt", 1), []) + last_use.get(("yt", 2), [])
            if emit_state_out is not None:
                emit_state_out()

        state["final"] = []
        row = 0
        pendB = None
        for it_, tt_ in enumerate(pre_tiles):
            runB = emit_tile("pre", tt_, row, 0, None, extra_counts=pre_counts[it_], parity=it_ % 2, defer=True)
            if pendB is not None:
                pendB()
            pendB = runB
            row += tt_
        if pendB is not None:
            pendB()
        state["RA_free"] = state.get("pre_readers_0", []) + state.get("pre_readers_1", [])
        prow = 0
        n_main = len(main_tiles)
        for i in range(n_main):
            last = (i == n_main - 1)
            tt_ = main_tiles[i]

            def so_main():
                ws = fw.now()
                t1 = fw.dma("gpsimd", "d_so", lambda e: e.dma_start(out=o_ca[0], in_=cvst[:]), deps=ws)
                t2 = fw.dma("gpsimd", "d_so", lambda e: e.dma_start(out=o_gla[0], in_=S[:]), deps=ws)
                t3 = fw.dma("gpsimd", "d_so", lambda e: e.dma_start(out=o_ffn[0], in_=ust[:]), deps=ws)
                state["final"] += [t1, t2, t3]
                state["so_main"] = [t1, t2, t3]
            nx_ = (row + tt_) if (i >= 1 and not last and main_tiles[i + 1] == 512 and tt_ == 512) else None
            emit_tile("full", tt_, row, prow, None if i == 0 else y_main[(i - 1) * 512:i * 512, :], emit_state_out=so_main if last else None, next_x=nx_, skip_tail=(i == 0))
            row += tt_
            prow += tt_

        def load_sample_state():
            ws = fw.now() + state["so_main"]
            t1 = fw.dma("gpsimd", "d_st", lambda e: e.dma_start(out=S[:], in_=st_gla), deps=ws)
            t2 = fw.dma("gpsimd", "d_st", lambda e: e.dma_start(out=ust[:], in_=st_ffn), deps=ws)
            t3 = fw.dma("gpsimd", "d_st", lambda e: e.dma_start(out=cvst[:], in_=st_ca), deps=ws)
            t4 = fw.op("vector", lambda e: e.tensor_copy(out=Sbf[:], in_=S[:]), deps=[t3])
            t5 = fw.op("vector", lambda e: e.tensor_copy(out=uhalo[:], in_=ust[:]), deps=[t3])
            t6 = fw.op("vector", lambda e: e.tensor_copy(out=cvhalo[:], in_=cvst[:]), deps=[t3])
            state["S_tok"] = [t4]
            state["cvst_free"] = [t6]
            state["ust_free"] = [t5]
            return [t4, t5, t6]

        def so_samp():
            ws = fw.now()
            t1 = fw.dma("gpsimd", "d_so", lambda e: e.dma_start(out=o_ca[1], in_=cvst[:]), deps=ws)
            t2 = fw.dma("gpsimd", "d_so", lambda e: e.dma_start(out=o_gla[1], in_=S[:]), deps=ws)
            t3 = fw.dma("gpsimd", "d_so", lambda e: e.dma_start(out=o_ffn[1], in_=ust[:]), deps=ws)
            state["final"] += [t1, t2, t3]
        emit_tile("full", NSAMP, row, prow, y_samp, first_of_ctx_load=load_sample_state, emit_state_out=so_samp)
        fw._emit_waits("gpsimd", state["final"])
        with nc.Block() as block:
            fw.replay(block)
    return nc


_CACHE = {}


def _get_program(HALF):
    if HALF not in _CACHE:
        _CACHE[HALF] = build_program(HALF)
    return _CACHE[HALF]


def _fm(v, nchunk):
    return np.ascontiguousarray(np.asarray(v, np.float32).reshape(nchunk, 128).T)


def kernel(x_prompt, x_sample, p_prompt, p_sample, state_conv_a, state_gla, state_ffn_conv, norm_mix, w_in,
           conv_a_w, w_a_out, w_gate2, b_gate, gla_norm, w_b_out, w_o, norm_ffn, w_up, ffn_conv_w, ffn_conv_b,
           w_down, norm_ple, w_ple_gate, w_ple, norm_final):
    f = np.float32
    x_prompt = np.asarray(x_prompt, f); x_sample = np.asarray(x_sample, f)
    p_prompt = np.asarray(p_prompt, f); p_sample = np.asarray(p_sample, f)
    B, SEQ, _ = x_prompt.shape
    HALF = SEQ // 2
    ncores = 2 * B
    assert x_sample.shape[0] == ncores
    nc = _get_program(HALF)
    gam = np.stack([_fm(norm_mix[0], 16), _fm(norm_ffn[0], 16), _fm(norm_ple[0], 16), _fm(gla_norm[0], 16)], axis=1)
    gfin = np.ascontiguousarray(np.broadcast_to(np.asarray(norm_final, f)[None, :], (128, D)))
    caw = np.ascontiguousarray(np.asarray(conv_a_w[0], f).reshape(3, 16, 128).transpose(2, 1, 0))
    fcw = np.ascontiguousarray(np.asarray(ffn_conv_w[0], f).reshape(3, NFC, 128).transpose(2, 1, 0))
    fcb = _fm(ffn_conv_b[0], NFC)
    wg2 = np.zeros((33, 1024), f)
    wg2[0:16] = np.asarray(w_gate2[0], f)
    wg2[32] = np.asarray(b_gate[0], f)
    identd = np.eye(128, dtype=f).astype(ml_dtypes.bfloat16)
    trid = np.triu(np.ones((128, 128), f))
    shared = dict(w_in=np.ascontiguousarray(w_in[0], f), w_a_out=np.ascontiguousarray(w_a_out[0], f), w_b_out=np.ascontiguousarray(w_b_out[0], f),
                  w_o=np.ascontiguousarray(w_o[0], f), w_up=np.ascontiguousarray(w_up[0], f), w_down=np.ascontiguousarray(w_down[0], f),
                  w_pg=np.ascontiguousarray(w_ple_gate[0], f), w_ple=np.ascontiguousarray(w_ple[0], f), gam=np.ascontiguousarray(gam), gfin=gfin,
                  caw=caw, fcw=fcw, fcb=fcb, wg2=wg2, identd=identd, trid=trid)
    in_maps = []
    for c in range(ncores):
        b, h = c // 2, c % 2
        xc = np.zeros((2 * HALF + NSAMP, D), f)
        WARM = 128
        pc = np.zeros((HALF + WARM + NSAMP, DPLE), f)
        if h == 1:
            xc[0:2 * HALF] = x_prompt[b]
            pc[0:HALF + WARM] = p_prompt[0, b, HALF - WARM:]
        else:
            xc[HALF:2 * HALF] = x_prompt[b, 0:HALF]
            pc[WARM:HALF + WARM] = p_prompt[0, b, 0:HALF]
        xc[2 * HALF:] = x_sample[c]
        pc[HALF + WARM:] = p_sample[0, c]
        m = dict(shared)
        m["xcat"] = xc
        m["pcat"] = pc
        m["st_ca"] = np.ascontiguousarray(np.asarray(state_conv_a[0, c], f).reshape(2, 16, 128).transpose(2, 1, 0))
        m["st_gla"] = np.ascontiguousarray(np.asarray(state_gla[0, c], f).reshape(4, 2, 128, 512).transpose(2, 0, 1, 3).reshape(128, 8, 512))
        m["st_ffn"] = np.ascontiguousarray(np.asarray(state_ffn_conv[0, c], f).reshape(2, NFC, 128).transpose(2, 1, 0))
        in_maps.append(m)
    res = run_bass_kernel_spmd(nc, in_maps, core_ids=list(range(ncores)))
    R = res.results
    y_prompt = np.zeros((B, SEQ, D), f)
    for c in range(ncores):
        b, h = c // 2, c % 2
        y_prompt[b, h * HALF:(h + 1) * HALF] = R[c]["y_main"]
    y_sample = np.stack([R[c]["y_samp"] for c in range(ncores)], 0)

    def un_ca(a):
        return np.ascontiguousarray(a.transpose(2, 1, 0).reshape(2, D))

    def un_gla(a):
        return np.ascontiguousarray(a.reshape(128, 4, 2, 512).transpose(1, 2, 0, 3).reshape(4, 256, 512))

    def un_ffn(a):
        return np.ascontiguousarray(a.transpose(2, 1, 0).reshape(2, 2 * DFF))
    cap = np.stack([un_ca(R[2 * b + 1]["o_ca_p"]) for b in range(B)], 0)[None]
    gp = np.stack([un_gla(R[2 * b + 1]["o_gla_p"]) for b in range(B)], 0)[None]
    fp = np.stack([un_ffn(R[2 * b + 1]["o_ffn_p"]) for b in range(B)], 0)[None]
    cas = np.stack([un_ca(R[c]["o_ca_s"]) for c in range(ncores)], 0)[None]
    gs = np.stack([un_gla(R[c]["o_gla_s"]) for c in range(ncores)], 0)[None]
    fs = np.stack([un_ffn(R[c]["o_ffn_s"]) for c in range(ncores)], 0)[None]
    return (y_prompt, y_sample, cap.astype(f), gp.astype(f), fp.astype(f), cas.astype(f), gs.astype(f), fs.astype(f))
```

```python
import numpy as np
import ml_dtypes
from collections import deque
from contextlib import ExitStack
import concourse.bass as bass
import concourse.mybir as mybir
from concourse.bass_utils import run_bass_kernel_spmd

F32 = mybir.dt.float32
BF16 = mybir.dt.bfloat16
ALU = mybir.AluOpType
AF = mybir.ActivationFunctionType

D = 2048
KC = 16
DIN = 16400
DFF = 5632
NFC = 88
DPLE = 256
EPS = 1e-6
OFF = dict(b_a=0, c_a=2048, v_a=4096, q=6144, k=7168, v=8192, g=10240, alr=12288, m_a=12304, m_b=14352)
ENGS = ["tensor", "vector", "scalar", "gpsimd", "sync"]
NSLOT = 4
WB = 256
NSAMP = 32


class FW:
    def __init__(self, nc):
        self.nc = nc
        self.prog = {e: [] for e in ENGS}
        self.cnt = {e: 0 for e in ENGS}
        self.waited = {e: {} for e in ENGS}
        self.sems = {}
        self.dma_cnt = {}
        self.free = {"f": deque(), "t": deque()}
        self.bank_rel = {}

    def _emit_waits(self, eng, deps):
        w = self.waited[eng]
        need = {}
        for d in deps:
            if d is None:
                continue
            k, v = d
            if w.get(k, 0) >= v:
                continue
            if need.get(k, 0) < v:
                need[k] = v
        for k, v in need.items():
            w[k] = v
            sem = self.sems[k]
            self.prog[eng].append(lambda e, sem=sem, v=v: e.wait_ge(sem, v))

    def op(self, eng, fn, deps=(), signal=True):
        self._emit_waits(eng, deps)
        if signal:
            self.cnt[eng] += 1
            tok = ("p_" + eng, self.cnt[eng])
            sem = self.sems["p_" + eng]
            self.prog[eng].append(lambda e, fn=fn, sem=sem: fn(e).then_inc(sem, 1))
            return tok
        self.prog[eng].append(lambda e, fn=fn: fn(e))
        return None

    def dma(self, eng, semname, fn, deps=()):
        self._emit_waits(eng, deps)
        self.dma_cnt[semname] = self.dma_cnt.get(semname, 0) + 16
        sem = self.sems[semname]
        self.prog[eng].append(lambda e, fn=fn, sem=sem: fn(e).then_inc(sem, 16))
        return (semname, self.dma_cnt[semname])

    def now(self, engs=ENGS):
        return [("p_" + e, self.cnt[e]) for e in engs if self.cnt[e] > 0]

    def alloc(self, pool):
        b = self.free[pool].popleft()
        return b, self.bank_rel.get((pool, b), [])

    def release(self, pool, b, toks):
        self.bank_rel[(pool, b)] = [t for t in toks if t is not None]
        self.free[pool].append(b)

    def replay(self, block):
        for en in ENGS:
            lst = self.prog[en]

            def body(e, lst=lst):
                for f in lst:
                    f(e)
            getattr(block, en)(body)


def build_program(HALF):
    assert HALF % 512 == 0
    WARM = 128
    pre_tiles = [512] * (HALF // 512 - 1) + [512 - WARM]
    main_tiles = [WARM] + [512] * (HALF // 512)
    NTOK = 2 * HALF + NSAMP
    NP = HALF + WARM + NSAMP
    nc = bass.Bass("TRN2", target_bir_lowering=False)

    def din(name, shape, dt=F32):
        return nc.dram_tensor(name, shape, dt, kind="ExternalInput").ap()

    def dout(name, shape, dt=F32):
        return nc.dram_tensor(name, shape, dt, kind="ExternalOutput").ap()

    xcat = din("xcat", [NTOK, D])
    pcat = din("pcat", [NP, DPLE])
    st_ca = din("st_ca", [128, KC, 2])
    st_gla = din("st_gla", [128, 8, 512])
    st_ffn = din("st_ffn", [128, NFC, 2])
    w_in = din("w_in", [D, DIN])
    w_a_out = din("w_a_out", [D, D])
    w_b_out = din("w_b_out", [D, D])
    w_o = din("w_o", [D, D])
    w_up = din("w_up", [D, 2 * DFF])
    w_down = din("w_down", [DFF, D])
    w_pg = din("w_pg", [D, D])
    w_ple = din("w_ple", [DPLE, D])
    gam = din("gam", [128, 4, KC])
    gfin = din("gfin", [128, D])
    caw = din("caw", [128, KC, 3])
    fcw = din("fcw", [128, NFC, 3])
    fcb = din("fcb", [128, NFC])
    wg2 = din("wg2", [33, 1024])
    identd = din("identd", [128, 128], BF16)
    trid = din("trid", [128, 128])

    y_main = dout("y_main", [HALF, D])
    y_samp = dout("y_samp", [NSAMP, D])
    o_ca = [dout("o_ca_p", [128, KC, 2]), dout("o_ca_s", [128, KC, 2])]
    o_gla = [dout("o_gla_p", [128, 8, 512]), dout("o_gla_s", [128, 8, 512])]
    o_ffn = [dout("o_ffn_p", [128, NFC, 2]), dout("o_ffn_s", [128, NFC, 2])]

    es = ExitStack()
    with es:
        def sb(name, shape, dt):
            return es.enter_context(nc.sbuf_tensor(name, shape, dt))

        xres = sb("xres", [128, 4, D], F32)
        actT = sb("actT", [128, KC, 512], BF16)
        RA = sb("RA", [128, 24 * 1024], BF16)
        R1 = RA[:, 0:8192].rearrange("p (c t) -> p c t", t=512)
        R2 = RA[:, 8192:16384].rearrange("p (c t) -> p c t", t=512)
        qe = RA[:, 8192:12288].rearrange("p (c t) -> p c t", t=512)
        ke = RA[:, 12288:16384].rearrange("p (c t) -> p c t", t=512)
        vtm = RA[:, 16384:24576].rearrange("p (s d) -> p s d", d=D)
        ke_alt = RA[:, 0:4096].rearrange("p (c t) -> p c t", t=512)
        vtm_alt = RA[:, 4096:12288].rearrange("p (s d) -> p s d", d=D)
        hbuf = RA[:, 0:44 * 512].rearrange("p (c t) -> p c t", t=512)
        ytile = [RA[:, 16384:20480].bitcast(F32), RA[:, 0:4096].bitcast(F32), RA[:, 4096:8192].bitcast(F32)]
        pin = RA[:, 20480:22528].bitcast(F32).rearrange("p (s d) -> p s d", d=DPLE)
        pbf = RA[:, 22528:23552].rearrange("p (s d) -> p s d", d=DPLE)
        pT = RA[:, 23552:24576].rearrange("p (c t) -> p c t", t=512)
        S = sb("S", [128, 8, 512], F32)
        Sbf = sb("Sbf", [128, 8, 512], BF16)
        wring = [sb("wr%d" % i, [128, KC, WB], BF16) for i in range(NSLOT)]
        walr = sb("walr", [128, KC, 16], BF16)
        gam_t = sb("gam_t", [128, 4, KC], F32)
        gfin_t = sb("gfin_t", [128, D], F32)
        caw_t = sb("caw_t", [128, KC, 3], F32)
        fcw_t = sb("fcw_t", [128, NFC, 3], F32)
        fcb_t = sb("fcb_t", [128, NFC], F32)
        wg2_t = sb("wg2_t", [33, 1024], F32)
        ident = sb("ident", [128, 128], BF16)
        tri = sb("tri", [128, 128], F32)
        tri4 = sb("tri4", [128, 512], F32)
        ones = sb("ones", [128, 128], F32)
        mhalf = sb("mhalf", [128, 4], F32)
        cvhalo = sb("cvhalo", [128, KC, 2], BF16)
        uhalo = sb("uhalo", [128, NFC, 2], BF16)
        cvst = sb("cvst", [128, KC, 2], F32)
        ust = sb("ust", [128, NFC, 2], F32)
        alr = sb("alr", [33, 512], F32)
        eblast = sb("eblast", [128, 8, 4], F32)
        eblast_alt = sb("eblast_alt", [128, 8, 4], F32)
        KE_SETS = [(ke, vtm, eblast), (ke_alt, vtm_alt, eblast_alt)]
        stat = sb("stat", [128, 32], F32)
        xs = [sb("xs%d" % i, [128, D], BF16) for i in range(2)]
        w1 = [sb("w1_%d" % i, [128, 516], F32) for i in range(2)]
        w2 = [sb("w2_%d" % i, [128, 516], F32) for i in range(2)]
        w3 = [sb("w3_%d" % i, [128, 516], F32) for i in range(2)]
        ub = [sb("ub%d" % i, [128, 516], BF16) for i in range(4)]
        scm = [sb("scm%d" % i, [128, 512], BF16) for i in range(2)]
        ketm = [sb("ketm%d" % i, [128, 1024], BF16) for i in range(2)]
        PSF = es.enter_context(nc.psum_tensor("PSF", [128, 6, 512], F32))
        PST = es.enter_context(nc.psum_tensor("PST", [128, 2, 1024], BF16))

        fw = FW(nc)
        semnames = ["p_" + e for e in ENGS] + ["d_c", "d_x0", "d_x1", "d_x2", "d_x3", "d_p", "d_y", "d_st", "d_so"] + ["d_w%d" % i for i in range(NSLOT)] + ["d_wb%d" % i for i in range(NSLOT)] + ["d_wc%d" % i for i in range(NSLOT)] + ["d_wa"]
        for n in semnames:
            fw.sems[n] = es.enter_context(nc.semaphore(n))
        for b in range(6):
            fw.free["f"].append(b)
        for b in range(2):
            fw.free["t"].append(b)

        tc = []
        for dst, src in [(gam_t, gam), (gfin_t, gfin), (caw_t, caw), (fcw_t, fcw), (fcb_t, fcb), (wg2_t, wg2), (ident, identd), (tri, trid)]:
            tc.append(fw.dma("sync", "d_c", lambda e, dst=dst, src=src: e.dma_start(out=dst[:], in_=src)))
        t_c = tc[-1]
        t_walr = fw.dma("gpsimd", "d_wa", lambda e: e.dma_start(out=walr[:], in_=w_in[:, OFF["alr"]:OFF["alr"] + 16].rearrange("(k p) n -> p k n", p=128)))
        t_m0 = fw.op("vector", lambda e: e.memset(ones[:], 1.0))
        t_m1 = fw.op("vector", lambda e: e.memset(mhalf[:], -0.5))
        t_m2 = fw.op("vector", lambda e: e.memset(alr[:], 0.0))
        t_m3 = fw.op("vector", lambda e: e.memset(alr[32:33, :], 1.0), deps=[t_m2])
        t_m4 = fw.op("vector", lambda e: e.memset(S[:], 0.0))
        t_m5 = fw.op("vector", lambda e: e.memset(Sbf[:], 0.0))
        t_m6 = fw.op("vector", lambda e: e.memset(cvhalo[:], 0.0))
        t_m7 = fw.op("vector", lambda e: e.memset(uhalo[:], 0.0))
        t_m8 = fw.op("vector", lambda e: e.memset(stat[:], 1.0))
        t_t4 = None
        for h_ in range(4):
            t_t4 = fw.op("vector", lambda e, h_=h_: e.tensor_copy(out=tri4[:, h_ * 128:(h_ + 1) * 128], in_=tri[:, :]), deps=[t_c])
        INIT = [t_c, t_walr, t_m8, t_t4]
        for _e in ENGS:
            fw._emit_waits(_e, [t_c, t_walr, t_t4, t_m0, t_m1, t_m3, t_m4, t_m5, t_m6, t_m7, t_m8])

        blocks = []

        wkeys = {}

        def wsrc(w, r0, nk, c0, ncols):
            key = (w.tensor.name, r0, c0)
            if key not in wkeys:
                wkeys[key] = len(wkeys)
            return (w[r0:r0 + nk * 128, c0:c0 + ncols].rearrange("(k p) n -> p k n", p=128), nk, ncols, wkeys[key])

        def tile_blocks(kind, nhalf=2, tail=True):
            bl = []
            if kind == "pre":
                for h in range(4):
                    bl.append(wsrc(w_in, 0, KC, OFF["k"] + WB * h, WB))
                for jj in range(8):
                    bl.append(wsrc(w_in, 0, KC, OFF["v"] + WB * jj, WB))
                return bl
            if nhalf == 2:
                for jj in range(8):
                    bl.append(wsrc(w_in, 0, KC, OFF["v"] + WB * jj, WB))
            for h in range(4):
                bl.append(wsrc(w_in, 0, KC, OFF["q"] + WB * h, WB))
                bl.append(wsrc(w_in, 0, KC, OFF["k"] + WB * h, WB))
            for jj in range(8):
                bl.append(wsrc(w_in, 0, KC, OFF["v"] + WB * jj, WB))
            for jj in range(8):
                bl.append(wsrc(w_in, 0, KC, OFF["g"] + WB * jj, WB))
            for jj in range(8):
                bl.append(wsrc(w_b_out, 0, KC, WB * jj, WB))
                bl.append(wsrc(w_in, 0, KC, OFF["m_b"] + WB * jj, WB))
            for jj in range(8):
                for nm in ("c_a", "v_a", "b_a"):
                    bl.append(wsrc(w_in, 0, KC, OFF[nm] + WB * jj, WB))
            for jj in range(8):
                bl.append(wsrc(w_a_out, 0, KC, WB * jj, WB))
                bl.append(wsrc(w_in, 0, KC, OFF["m_a"] + WB * jj, WB))
            for _h in range(nhalf):
                for jj in range(8):
                    bl.append(wsrc(w_o, 0, KC, WB * jj, WB))
            for jj in range(22):
                bl.append(wsrc(w_up, 0, KC, WB * jj, WB))
                bl.append(wsrc(w_up, 0, KC, DFF + WB * jj, WB))
            if not tail:
                return bl
            for _h in range(nhalf):
                for cb0 in range(0, 8, 2):
                    for cb, rb in ((cb0, 0), (cb0, 1), (cb0 + 1, 0), (cb0 + 1, 1), (cb0, 2), (cb0 + 1, 2)):
                        bl.append(wsrc(w_down, 2048 * rb, 12 if rb == 2 else 16, WB * cb, WB))
            for _h in range(nhalf):
                for cb in range(8):
                    bl.append(wsrc(w_pg, 0, KC, WB * cb, WB))
                    bl.append(wsrc(w_ple, 0, 2, WB * cb, WB))
            return bl

        full_bl = tile_blocks("full")
        pre_own = tile_blocks("pre")
        own_idx = set(b_[3] for b_ in pre_own)
        extras_all = []
        seen_ = set()
        for b_ in full_bl:
            if b_[3] not in own_idx and b_[3] not in seen_:
                extras_all.append(b_)
                seen_.add(b_[3])
        NLEAVE = 40
        extras_all = extras_all[:len(extras_all) - NLEAVE]
        npre_ = len(pre_tiles)
        per_ = (len(extras_all) + npre_ - 1) // npre_
        extras = [extras_all[i_ * per_:(i_ + 1) * per_] for i_ in range(npre_)]
        pre_counts = []
        for t_ in range(npre_):
            n_t = len(extras[t_])
            no_ = len(pre_own)
            cnts = []
            for i_ in range(no_):
                lo_, hi_ = (i_ * n_t) // no_, ((i_ + 1) * n_t) // no_
                blocks.append(pre_own[i_])
                blocks.extend(extras[t_][lo_:hi_])
                cnts.append(hi_ - lo_)
            pre_counts.append(cnts)
        for it__, tt__ in enumerate(main_tiles + [NSAMP]):
            blocks.extend(tile_blocks("full", 2 if tt__ >= 512 else 1, tail=(it__ != 0)))
        wq = nc.dram_tensor("wq", [len(wkeys), 128, KC * WB], BF16, kind="Internal").ap()
        wst = {"issued": 0, "cur": -1, "load_tok": {}, "done_tok": {}, "wb_tok": {}, "conv": set()}

        def wget():
            i = wst["cur"] + 1
            wst["cur"] = i
            while wst["issued"] < min(len(blocks), i + NSLOT) and (wst["issued"] < NSLOT or (wst["issued"] - NSLOT) in wst["done_tok"]):
                m = wst["issued"]
                src, nk, ncols, widx = blocks[m]
                slot = m % NSLOT
                deps = wst["done_tok"].get(m - NSLOT, []) + wst["wb_tok"].get(m - NSLOT, [])
                qv = wq[widx, :, 0:nk * ncols].rearrange("p (k n) -> p k n", n=ncols)
                if widx in wst["conv"]:
                    wball = [("d_wb%d" % i_, fw.dma_cnt.get("d_wb%d" % i_, 0)) for i_ in range(NSLOT) if fw.dma_cnt.get("d_wb%d" % i_, 0) > 0]
                    wst["load_tok"][m] = fw.dma("sync", "d_w%d" % slot,
                                                lambda e, qv=qv, nk=nk, ncols=ncols, slot=slot: e.dma_start(out=wring[slot][:, 0:nk, 0:ncols], in_=qv),
                                                deps=deps + wball)
                else:
                    wst["load_tok"][m] = fw.dma("gpsimd", "d_wc%d" % slot,
                                                lambda e, src=src, nk=nk, ncols=ncols, slot=slot: e.dma_start(out=wring[slot][:, 0:nk, 0:ncols], in_=src),
                                                deps=deps)
                    wst["wb_tok"][m] = [fw.dma("sync", "d_wb%d" % slot,
                                               lambda e, qv=qv, nk=nk, ncols=ncols, slot=slot: e.dma_start(out=qv, in_=wring[slot][:, 0:nk, 0:ncols]),
                                               deps=[wst["load_tok"][m]])]
                    wst["conv"].add(widx)
                wst["issued"] += 1
            assert wst["issued"] > i, (i, wst["issued"])
            return wring[i % NSLOT], wst["load_tok"][i], i

        def wdone(i, toks):
            wst["done_tok"][i] = [t for t in toks if t is not None]

        def mm_group(out_ap, pairs, deps):
            n = len(pairs)
            tok = None
            for i, (l, r) in enumerate(pairs):
                tok = fw.op("tensor", lambda e, l=l, r=r, i=i: e.matmul(out_ap, lhsT=l, rhs=r, start=(i == 0), stop=(i == n - 1)),
                            deps=deps if i == 0 else (), signal=(i == n - 1))
            return tok

        rr = {"xs": 0, "w1": 0, "w2": 0, "w3": 0, "ub": 0, "yt": 0, "scm": 0, "ketm": 0, "oraw": 0}
        last_use = {}

        def ring(name, lst):
            i = rr[name]
            rr[name] = (i + 1) % len(lst)
            return lst[i], last_use.get((name, i), []), (name, i)

        def used(key, toks):
            last_use[key] = [t for t in toks if t is not None]

        def norm_p1(src_fn, subs, nt, ready_fn):
            hs = {}
            for s in subs:
                src = src_fn(s)
                xb, xdeps, xkey = ring("xs", xs)
                q0 = 16 + 4 * (xkey[1] % 2)
                t_sq = fw.op("scalar", lambda e, src=src, xb=xb, q0=q0: e.activation(out=xb[0:nt, :], in_=src, func=AF.Square, accum_out=stat[0:nt, q0:q0 + 1]), deps=ready_fn(s) + xdeps)
                t_ms = fw.op("vector", lambda e, q0=q0: e.tensor_scalar(out=stat[0:nt, q0 + 1:q0 + 2], in0=stat[0:nt, q0:q0 + 1], scalar1=1.0 / D, scalar2=EPS, op0=ALU.mult, op1=ALU.add), deps=[t_sq])
                t_rs = fw.op("gpsimd", lambda e, q0=q0: e.tensor_tensor(out=stat[0:nt, q0 + 2:q0 + 3], in0=stat[0:nt, q0 + 1:q0 + 2], in1=mhalf[0:nt, 0:1], op=ALU.pow), deps=[t_ms, t_m1])
                t_xs = fw.op("vector", lambda e, src=src, xb=xb, q0=q0: e.tensor_scalar(out=xb[0:nt, :], in0=src, scalar1=stat[0:nt, q0 + 2:q0 + 3], scalar2=None, op0=ALU.mult), deps=[t_rs, t_sq])
                hs[s] = (xb, xkey, t_xs)
            return hs

        def norm_p2(hs, subs, nt, gidx, actT_free_fn):
            toks = []
            per_s = {}
            for s in subs:
                xb, xkey, t_xs = hs[s]
                tl = []
                mine = []
                af = actT_free_fn(s)
                for g4 in range(4):
                    b, bd = fw.alloc("t")
                    tp = None
                    for j in range(4):
                        c = g4 * 4 + j
                        tp = fw.op("tensor", lambda e, c=c, j=j, b=b, xb=xb: e.transpose(out=PST[:, b, j * 128:j * 128 + nt], in_=xb[0:nt, c * 128:(c + 1) * 128], identity=ident[0:nt, 0:nt]),
                                   deps=[t_xs] + bd + INIT if j == 0 else (), signal=(j == 3))
                    te = None
                    for j in range(4):
                        c = g4 * 4 + j
                        if g4 % 2 == 0:
                            te = fw.op("vector", lambda e, c=c, j=j, b=b, s=s: e.tensor_scalar(out=actT[:, c, s * 128:s * 128 + nt], in0=PST[:, b, j * 128:j * 128 + nt], scalar1=gam_t[:, gidx, c:c + 1], scalar2=None, op0=ALU.mult),
                                       deps=[tp] + af)
                        else:
                            te = fw.op("scalar", lambda e, c=c, j=j, b=b, s=s: e.activation(out=actT[:, c, s * 128:s * 128 + nt], in_=PST[:, b, j * 128:j * 128 + nt], func=AF.Copy, scale=gam_t[:, gidx, c:c + 1]),
                                       deps=[tp] + af)
                    fw.release("t", b, [te])
                    tl.append(tp)
                    toks.append(te)
                    mine.append(te)
                used(xkey, tl)
                per_s[s] = mine
            return toks, per_s

        def norm_to_actT(src_fn, nsub, nt, gidx, ready_fn, actT_free_fn):
            toks, per_s = [], {}
            for s0 in range(0, nsub, 2):
                subs = list(range(s0, min(nsub, s0 + 2)))
                hs = norm_p1(src_fn, subs, nt, ready_fn)
                t_, p_ = norm_p2(hs, subs, nt, gidx, actT_free_fn)
                toks += t_
                per_s.update(p_)
            return toks, [per_s[s] for s in range(nsub)]

        state = {"actT_readers": [], "x_tok": None}

        def emit_tile(kind, TT, xrow0, prow0, yout, first_of_ctx_load=None, emit_state_out=None, nextra=0, next_x=None, extra_counts=None, parity=0, defer=False, skip_tail=False):
            nsub = max(1, TT // 128)
            nt = min(TT, 128)
            full = kind == "full"
            ke, vtm, eblast = KE_SETS[parity]
            PK = "_%d" % parity
            ecnt = list(extra_counts) if extra_counts is not None else []

            def conv_some():
                if ecnt:
                    for _ in range(ecnt.pop(0)):
                        (_sl, _tk, _wi) = wget()
                        wdone(_wi, [])
            pro = state.pop("pro", None)
            xfree = state.get("xres_free", [])
            t_x = {}
            for s in range(nsub):
                if pro is not None and s in pro["t_x"]:
                    t_x[s] = pro["t_x"][s]
                    continue
                if len(xfree) == nsub:
                    xd = xfree[s]
                else:
                    xd = [t for l_ in xfree for t in l_]
                t_x[s] = fw.dma("gpsimd", "d_x%d" % s, lambda e, s=s: e.dma_start(out=xres[0:nt, s, :], in_=xcat[xrow0 + s * nt:xrow0 + (s + 1) * nt, :]), deps=xd)
            ext = []
            if first_of_ctx_load is not None:
                ext = first_of_ctx_load()
            RBF = state.get("RA_free", [])
            arf = state["actT_readers"]
            arf_fn = (lambda s: arf[s]) if (isinstance(arf, dict) and len(arf) == nsub) else (lambda s: ([t for l_ in arf.values() for t in l_] if isinstance(arf, dict) else arf))
            vsplit = full and nsub >= 4
            hvA = list(range(0, nsub // 2)) if vsplit else list(range(nsub))
            hvB = list(range(nsub // 2, nsub)) if vsplit else []
            act_readers = []
            v_w = []
            vfree = state.get("vtm_free" + PK, [])

            def vproj(subs, rdy):
                for jj in range(8):
                    (wv_s, wv_t, wv_i) = wget()
                    tg = None
                    for s in subs:
                        b, bd = fw.alloc("f")
                        tg = mm_group(PSF[0:nt, b, 0:WB], [(actT[:, c, s * 128:s * 128 + nt], wv_s[:, c, 0:WB]) for c in range(KC)], deps=rdy + [wv_t] + bd)
                        tv = fw.op("scalar", lambda e, b=b, s=s, jj=jj: e.activation(out=vtm[0:nt, s, jj * WB:(jj + 1) * WB], in_=PSF[0:nt, b, 0:WB], func=AF.Copy), deps=[tg] + vfree + RBF)
                        fw.release("f", b, [tv])
                        v_w.append(tv)
                        act_readers.append(tg)
                    wdone(wv_i, [tg])
                    conv_some()

            src_fn = lambda s: xres[0:nt, s, :]
            tn = []
            tn_map = {}
            def norm_chunk(subs, hs=None):
                if hs is None:
                    hs = norm_p1(src_fn, subs, nt, lambda s: [t_x[s]])
                t_, p_ = norm_p2(hs, subs, nt, 0, arf_fn)
                tn.extend(t_)
                tn_map.update(p_)
                return t_
            tnA = []
            first = True
            for i0 in range(0, len(hvA), 2):
                subs = hvA[i0:i0 + 2]
                tnA += norm_chunk(subs, pro["hs"] if (pro is not None and first) else None)
                first = False
            if vsplit:
                vproj(hvA, tnA)
            for i0 in range(0, len(hvB), 2):
                norm_chunk(hvB[i0:i0 + 2])
            tn_ps = [tn_map[s] for s in range(nsub)]
            A_RDY = tn

            b, bd = fw.alloc("f")
            tga = mm_group(PSF[0:16, b, 0:TT], [(walr[:, c, 0:16], actT[:, c, 0:TT]) for c in range(KC)], deps=A_RDY + bd + INIT)
            t_al = fw.op("vector", lambda e, b=b: e.tensor_copy(out=alr[0:16, 0:TT], in_=PSF[0:16, b, 0:TT]), deps=[tga, t_m3] + state.get("alr_free", []))
            fw.release("f", b, [t_al])
            act_readers.append(tga)
            z_readers = []
            qk_w = []
            for h in range(4):
                if full:
                    (wq_s, wq_t, wq_i) = wget()
                (wk_s, wk_t, wk_i) = wget()
                for jc in range(2):
                    j = h * 2 + jc
                    bz, bd = fw.alloc("f")
                    tgz = mm_group(PSF[:, bz, 0:TT], [(wg2_t[0:33, j * 128:(j + 1) * 128], alr[0:33, 0:TT])], deps=[t_al] + bd + INIT)
                    z_readers.append(tgz)
                    wa, wad, wak = ring("w1", w1)
                    t1 = fw.op("scalar", lambda e, bz=bz, wa=wa: e.activation(out=wa[:, 0:TT], in_=PSF[:, bz, 0:TT], func=AF.Exp, scale=-1.0), deps=[tgz] + wad)
                    fw.release("f", bz, [t1])
                    t2 = fw.op("scalar", lambda e, wa=wa: e.activation(out=wa[:, 0:TT], in_=wa[:, 0:TT], func=AF.Ln, bias=1.0), deps=[t1])
                    wb_, wbd, wbk = ring("w2", w2)
                    ts = None
                    for s in range(nsub):
                        ts = fw.op("vector", lambda e, s=s, wa=wa, wb_=wb_: e.tensor_tensor_scan(out=wb_[:, s * 128:s * 128 + nt], data0=ones[:, 0:nt], data1=wa[:, s * 128:s * 128 + nt], initial=0.0, op0=ALU.mult, op1=ALU.add),
                                   deps=[t2, t_m0] + wbd if s == 0 else ())
                    used(wak, [ts])
                    wc, wcd, wck = ring("w3", w3)
                    t3 = fw.op("scalar", lambda e, wb_=wb_, wc=wc: e.activation(out=wc[:, 0:TT], in_=wb_[:, 0:TT], func=AF.Exp, scale=-1.0 / 16), deps=[ts] + wcd)
                    t3b = fw.op("vector", lambda e, wc=wc, j=j: e.tensor_copy(out=eblast[:, j, 0:nsub], in_=wc[:, nt - 1:TT:128] if nsub > 1 else wc[:, nt - 1:nt]), deps=[t3] + state.get("eblast_free" + PK, []))
                    t4 = fw.op("scalar", lambda e, wb_=wb_: e.activation(out=wb_[:, 0:TT], in_=wb_[:, 0:TT], func=AF.Exp, scale=1.0 / 16), deps=[t3])
                    if full:
                        bq, bd = fw.alloc("f")
                        tgq = mm_group(PSF[:, bq, 0:TT], [(wq_s[:, c, jc * 128:(jc + 1) * 128], actT[:, c, 0:TT]) for c in range(KC)], deps=[wq_t] + bd)
                        t5 = fw.op("vector", lambda e, bq=bq, wc=wc, j=j: e.scalar_tensor_tensor(out=qe[:, j, 0:TT], in0=PSF[:, bq, 0:TT], scalar=float(256 ** -0.5), in1=wc[:, 0:TT], op0=ALU.mult, op1=ALU.mult), deps=[tgq, t3] + RBF)
                        fw.release("f", bq, [t5])
                        qk_w.append(t5)
                        act_readers.append(tgq)
                        used(wck, [t5, t3b])
                    else:
                        used(wck, [t3b])
                    bk, bd = fw.alloc("f")
                    tgk = mm_group(PSF[:, bk, 0:TT], [(wk_s[:, c, jc * 128:(jc + 1) * 128], actT[:, c, 0:TT]) for c in range(KC)], deps=[wk_t] + bd)
                    t6 = fw.op("vector", lambda e, bk=bk, wb_=wb_, j=j: e.tensor_tensor(out=ke[:, j, 0:TT], in0=PSF[:, bk, 0:TT], in1=wb_[:, 0:TT], op=ALU.mult), deps=[tgk, t4] + RBF)
                    fw.release("f", bk, [t6])
                    used(wbk, [t6])
                    qk_w.append(t6)
                    act_readers.append(tgk)
                if full:
                    wdone(wq_i, [tgq])
                wdone(wk_i, [tgk])
                conv_some()
            state["alr_free"] = z_readers
            vproj(hvB if vsplit else hvA, A_RDY)
            o_w = []
            gla_readers = []
            G = {}

            def gla_T(s):
                c0 = s * 128
                kt, ktd, ktk = ring("ketm", ketm)
                tkt = []
                for g4 in range(2):
                    b, bd = fw.alloc("t")
                    tp = None
                    for j4 in range(4):
                        j = g4 * 4 + j4
                        tp = fw.op("tensor", lambda e, j=j, j4=j4, b=b, c0=c0: e.transpose(out=PST[0:nt, b, j4 * 128:(j4 + 1) * 128], in_=ke[:, j, c0:c0 + nt], identity=ident[:, :]),
                                   deps=qk_w + bd + INIT if j4 == 0 else (), signal=(j4 == 3))
                    te = fw.op("scalar", lambda e, b=b, g4=g4, kt=kt: e.activation(out=kt[0:nt, g4 * 512:(g4 + 1) * 512], in_=PST[0:nt, b, 0:512], func=AF.Copy), deps=[tp] + ktd)
                    fw.release("t", b, [te])
                    tkt.append(te)
                    gla_readers.append(tp)
                G[("kt", s)] = (kt, ktk, tkt)

            def gla_SC(s):
                c0 = s * 128
                b, bd = fw.alloc("f")
                tgs = None
                for h in range(4):
                    tgs = mm_group(PSF[0:nt, b, h * 128:h * 128 + nt], [(ke[:, h * 2 + kc, c0:c0 + nt], qe[:, h * 2 + kc, c0:c0 + nt]) for kc in range(2)], deps=qk_w + bd if h == 0 else [])
                sc, scd, sck = ring("scm", scm)
                tm_ = fw.op("vector", lambda e, b=b, sc=sc: e.tensor_tensor(out=sc[0:nt, :].rearrange("p (h t) -> p h t", t=128)[:, :, 0:nt], in0=PSF[0:nt, b, :].rearrange("p (h t) -> p h t", t=128)[:, :, 0:nt], in1=tri4[0:nt, :].rearrange("p (h t) -> p h t", t=128)[:, :, 0:nt], op=ALU.mult), deps=[tgs] + scd + INIT)
                fw.release("f", b, [tm_])
                gla_readers.append(tgs)
                G[("sc", s)] = (sc, sck, tm_)

            def gla_O(s):
                c0 = s * 128
                sc, sck, tm_ = G[("sc", s)]
                orw, ord_, ork = ring("xs", xs)
                t_or = []
                tgos = []
                t_sq = None
                for h in range(4):
                    bo, bd = fw.alloc("f")
                    pairs = [(qe[:, h * 2 + kc, c0:c0 + nt], Sbf[:, h * 2 + kc, :]) for kc in range(2)] + [(sc[0:nt, h * 128:h * 128 + nt], vtm[0:nt, s, h * 512:(h + 1) * 512])]
                    tgo = mm_group(PSF[0:nt, bo, :], pairs, deps=[tm_] + v_w + G["S_tok"] + bd)
                    t_sq = fw.op("scalar", lambda e, bo=bo, h=h, orw=orw: e.activation(out=orw[0:nt, h * 512:(h + 1) * 512], in_=PSF[0:nt, bo, :], func=AF.Square, accum_out=stat[0:nt, 4 + h:5 + h]), deps=[tgo] + state.get("stat_free", []) + ord_)
                    t_cp = fw.op("vector", lambda e, bo=bo, h=h, orw=orw: e.tensor_copy(out=orw[0:nt, h * 512:(h + 1) * 512], in_=PSF[0:nt, bo, :]), deps=[tgo, t_sq] + ord_)
                    fw.release("f", bo, [t_cp])
                    t_or.append(t_cp)
                    tgos.append(tgo)
                    gla_readers.append(tgo)
                used(sck, [tgos[-1]])
                t_ms = fw.op("vector", lambda e: e.tensor_scalar(out=stat[0:nt, 8:12], in0=stat[0:nt, 4:8], scalar1=1.0 / 512, scalar2=EPS, op0=ALU.mult, op1=ALU.add), deps=[t_sq])
                t_rs = fw.op("gpsimd", lambda e: e.tensor_tensor(out=stat[0:nt, 12:16], in0=stat[0:nt, 8:12], in1=mhalf[0:nt, 0:4], op=ALU.pow), deps=[t_ms, t_m1])
                tn_ = []
                for h in range(4):
                    tn_.append(fw.op("vector", lambda e, h=h, orw=orw: e.tensor_scalar(out=orw[0:nt, h * 512:(h + 1) * 512], in0=orw[0:nt, h * 512:(h + 1) * 512], scalar1=stat[0:nt, 12 + h:13 + h], scalar2=None, op0=ALU.mult), deps=[t_rs] + t_or))
                state["stat_free"] = [t_ms] + tn_
                G[("o", s)] = (orw, ork, tn_)
                G[("tgo", s)] = tgos

            def gla_P(s):
                kt, ktk, tkt = G[("kt", s)]
                tgos = G.get(("tgo", s), [None] * 4)
                kt_readers = []
                newS = []
                for h in range(4):
                    tgo = tgos[h]
                    for kc in range(2):
                        j = h * 2 + kc
                        bp, bd = fw.alloc("f")
                        tgp = mm_group(PSF[:, bp, :], [(kt[0:nt, j * 128:(j + 1) * 128], vtm[0:nt, s, h * 512:(h + 1) * 512])], deps=tkt + v_w + bd)
                        kt_readers.append(tgp)
                        if full:
                            tu1 = fw.op("gpsimd", lambda e, j=j, s=s: e.tensor_scalar(out=S[:, j, :], in0=S[:, j, :], scalar1=eblast[:, j, s:s + 1], scalar2=1.0, op0=ALU.mult, op1=ALU.mult), deps=G["S_tok"])
                        else:
                            tu1 = fw.op("vector", lambda e, j=j, s=s: e.tensor_scalar(out=S[:, j, :], in0=S[:, j, :], scalar1=eblast[:, j, s:s + 1], scalar2=None, op0=ALU.mult), deps=G["S_tok"])
                        tu2 = fw.op("vector", lambda e, j=j, bp=bp, s=s: e.scalar_tensor_tensor(out=S[:, j, :], in0=PSF[:, bp, :], scalar=eblast[:, j, s:s + 1], in1=S[:, j, :], op0=ALU.mult, op1=ALU.add), deps=[tgp, tu1])
                        fw.release("f", bp, [tu2])
                        tu3 = fw.op("scalar", lambda e, j=j: e.activation(out=Sbf[:, j, :], in_=S[:, j, :], func=AF.Copy), deps=[tu2, tgo])
                        newS += [tu2, tu3]
                        gla_readers.append(tgp)
                used(ktk, kt_readers)
                G["S_tok"] = newS

            def gla_N(s):
                c0 = s * 128
                orw, ork, tn_ = G[("o", s)]
                tls = []
                for g4 in range(4):
                    b, bd = fw.alloc("t")
                    tp = None
                    for j4 in range(4):
                        c = g4 * 4 + j4
                        tp = fw.op("tensor", lambda e, c=c, j4=j4, b=b, orw=orw: e.transpose(out=PST[:, b, j4 * 128:j4 * 128 + nt], in_=orw[0:nt, c * 128:(c + 1) * 128], identity=ident[0:nt, 0:nt]),
                                   deps=tn_ + bd if j4 == 0 else (), signal=(j4 == 3))
                    te = None
                    for j4 in range(4):
                        c = g4 * 4 + j4
                        if g4 % 2 == 0:
                            te = fw.op("vector", lambda e, c=c, j4=j4, b=b, c0=c0: e.tensor_scalar(out=R1[:, c, c0:c0 + nt], in0=PST[:, b, j4 * 128:j4 * 128 + nt], scalar1=gam_t[:, 3, c:c + 1], scalar2=None, op0=ALU.mult), deps=[tp] + RBF)
                        else:
                            te = fw.op("scalar", lambda e, c=c, j4=j4, b=b, c0=c0: e.activation(out=R1[:, c, c0:c0 + nt], in_=PST[:, b, j4 * 128:j4 * 128 + nt], func=AF.Copy, scale=gam_t[:, 3, c:c + 1]), deps=[tp] + RBF)
                        o_w.append(te)
                    fw.release("t", b, [te])
                    tls.append(tp)
                used(ork, tls)

            if not full:
                state["actT_readers"] = act_readers
                state["xres_free"] = tn_ps
                while ecnt:
                    conv_some()

                def run_B():
                    G["S_tok"] = state.get("S_tok", [t_m4, t_m5]) + ext
                    for s in range(min(2, nsub)):
                        gla_T(s)
                    for s in range(nsub):
                        gla_P(s)
                        if s + 2 < nsub:
                            gla_T(s + 2)
                    state["S_tok"] = G["S_tok"]
                    state["vtm_free" + PK] = list(gla_readers)
                    state["eblast_free" + PK] = G["S_tok"]
                    state["pre_readers" + PK] = list(gla_readers) + G["S_tok"]
                if defer:
                    return run_B
                run_B()
                return None
            G["S_tok"] = state.get("S_tok", [t_m4, t_m5]) + ext
            for s in range(min(2, nsub)):
                gla_T(s)
                if full:
                    gla_SC(s)
            for s in range(nsub):
                if full:
                    gla_O(s)
                gla_P(s)
                if full and s >= 1:
                    gla_N(s - 1)
                if s + 2 < nsub:
                    gla_T(s + 2)
                    if full:
                        gla_SC(s + 2)
            if full:
                gla_N(nsub - 1)
            S_tok = G["S_tok"]
            state["S_tok"] = S_tok
            state["vtm_free" + PK] = gla_readers
            state["eblast_free" + PK] = S_tok
            og_w = []
            for jj in range(8):
                (wg_s, wg_t, wg_i) = wget()
                for jc in range(2):
                    j = jj * 2 + jc
                    b, bd = fw.alloc("f")
                    tg = mm_group(PSF[:, b, 0:TT], [(wg_s[:, c, jc * 128:(jc + 1) * 128], actT[:, c, 0:TT]) for c in range(KC)], deps=[wg_t] + bd)
                    wa, wad, wak = ring("w1", w1)
                    t1 = fw.op("scalar", lambda e, b=b, wa=wa: e.activation(out=wa[:, 0:TT], in_=PSF[:, b, 0:TT], func=AF.Tanh, scale=0.5), deps=[tg] + wad)
                    wb_, wbd, wbk = ring("w2", w2)
                    t2 = fw.op("vector", lambda e, b=b, wa=wa, wb_=wb_: e.scalar_tensor_tensor(out=wb_[:, 0:TT], in0=wa[:, 0:TT], scalar=1.0, in1=PSF[:, b, 0:TT], op0=ALU.add, op1=ALU.mult), deps=[t1] + wbd)
                    fw.release("f", b, [t2])
                    t3 = fw.op("vector", lambda e, wb_=wb_, j=j: e.scalar_tensor_tensor(out=R1[:, j, 0:TT], in0=wb_[:, 0:TT], scalar=0.5, in1=R1[:, j, 0:TT], op0=ALU.mult, op1=ALU.mult), deps=[t2] + o_w)
                    used(wak, [t2]); used(wbk, [t3])
                    og_w.append(t3)
                    act_readers.append(tg)
                wdone(wg_i, [tg])
            mg2_w = []
            for jj in range(8):
                (wb_s, wb_t, wb_i), (wm_s, wm_t, wm_i) = wget(), wget()
                for jc in range(2):
                    j = jj * 2 + jc
                    by, bd = fw.alloc("f")
                    tgy = mm_group(PSF[:, by, 0:TT], [(wb_s[:, c, jc * 128:(jc + 1) * 128], R1[:, c, 0:TT]) for c in range(KC)], deps=og_w + [wb_t] + bd)
                    bm, bd = fw.alloc("f")
                    tgm = mm_group(PSF[:, bm, 0:TT], [(wm_s[:, c, jc * 128:(jc + 1) * 128], actT[:, c, 0:TT]) for c in range(KC)], deps=[wm_t] + bd)
                    wa, wad, wak = ring("w1", w1)
                    t1 = fw.op("scalar", lambda e, bm=bm, wa=wa: e.activation(out=wa[:, 0:TT], in_=PSF[:, bm, 0:TT], func=AF.Tanh, scale=0.5), deps=[tgm] + wad)
                    fw.release("f", bm, [t1])
                    wb_, wbd, wbk = ring("w2", w2)
                    t2 = fw.op("vector", lambda e, by=by, wa=wa, wb_=wb_: e.scalar_tensor_tensor(out=wb_[:, 0:TT], in0=wa[:, 0:TT], scalar=1.0, in1=PSF[:, by, 0:TT], op0=ALU.add, op1=ALU.mult), deps=[tgy, t1] + wbd)
                    fw.release("f", by, [t2])
                    t3 = fw.op("vector", lambda e, wb_=wb_, j=j: e.tensor_scalar(out=R2[:, j, 0:TT], in0=wb_[:, 0:TT], scalar1=0.5, scalar2=None, op0=ALU.mult), deps=[t2] + gla_readers)
                    used(wak, [t2]); used(wbk, [t3])
                    mg2_w.append(t3)
                    act_readers.append(tgm)
                wdone(wb_i, [tgy]); wdone(wm_i, [tgm])
            R1_rd7 = [tgy]
            ya_w = []
            for jj in range(8):
                (wsl, wtok, wi) = wget()
                ca = {}
                tg = None
                for jc in range(2):
                    b_, bd = fw.alloc("f")
                    tg = mm_group(PSF[:, b_, 0:TT], [(wsl[:, c, jc * 128:(jc + 1) * 128], actT[:, c, 0:TT]) for c in range(KC)], deps=A_RDY + [wtok] + bd)
                    wa, wad, wak = ring("w1", w1)
                    t1 = fw.op("scalar", lambda e, b_=b_, wa=wa: e.activation(out=wa[:, 0:TT], in_=PSF[:, b_, 0:TT], func=AF.Copy), deps=[tg] + wad)
                    fw.release("f", b_, [t1])
                    ca[jc] = (wa, wak, t1)
                    act_readers.append(tg)
                wdone(wi, [tg])
                (wsl, wtok, wi) = wget()
                cv = {}
                for jc in range(2):
                    j = jj * 2 + jc
                    wa, wak, t1 = ca[jc]
                    bv, bd = fw.alloc("f")
                    tg = mm_group(PSF[:, bv, 0:TT], [(wsl[:, c, jc * 128:(jc + 1) * 128], actT[:, c, 0:TT]) for c in range(KC)], deps=[wtok] + bd)
                    tgv = tg
                    ubf, ubd, ubk = ring("ub", ub)
                    t2 = fw.op("vector", lambda e, bv=bv, wa=wa, ubf=ubf: e.tensor_tensor(out=ubf[:, 2:2 + TT], in0=PSF[:, bv, 0:TT], in1=wa[:, 0:TT], op=ALU.mult), deps=[tgv, t1] + ubd)
                    t2h = fw.op("vector", lambda e, ubf=ubf, j=j: e.tensor_copy(out=ubf[:, 0:2], in_=cvhalo[:, j, :]), deps=ubd + [t_m6] + ext)
                    rel = [t2]
                    if emit_state_out is not None:
                        t2s = fw.op("vector", lambda e, bv=bv, wa=wa, j=j: e.tensor_tensor(out=cvst[:, j, :], in0=PSF[:, bv, TT - 2:TT], in1=wa[:, TT - 2:TT], op=ALU.mult), deps=[tgv, t1] + state.get("cvst_free", []))
                        rel.append(t2s)
                    fw.release("f", bv, rel)
                    used(wak, rel)
                    wb_, wbd, wbk = ring("w2", w2)
                    t3 = fw.op("vector", lambda e, ubf=ubf, wb_=wb_, j=j: e.tensor_scalar(out=wb_[:, 0:TT], in0=ubf[:, 2:2 + TT], scalar1=caw_t[:, j, 2:3], scalar2=None, op0=ALU.mult), deps=[t2, t2h] + wbd + INIT)
                    t4 = fw.op("vector", lambda e, ubf=ubf, wb_=wb_, j=j: e.scalar_tensor_tensor(out=wb_[:, 0:TT], in0=ubf[:, 1:1 + TT], scalar=caw_t[:, j, 1:2], in1=wb_[:, 0:TT], op0=ALU.mult, op1=ALU.add), deps=[t3])
                    t5 = fw.op("vector", lambda e, ubf=ubf, wb_=wb_, j=j: e.scalar_tensor_tensor(out=wb_[:, 0:TT], in0=ubf[:, 0:TT], scalar=caw_t[:, j, 0:1], in1=wb_[:, 0:TT], op0=ALU.mult, op1=ALU.add), deps=[t4])
                    t5h = fw.op("vector", lambda e, ubf=ubf, j=j: e.tensor_copy(out=cvhalo[:, j, :], in_=ubf[:, TT:TT + 2]), deps=[t5, t2h])
                    used(ubk, [t5h])
                    cv[jc] = (wb_, wbk, t5)
                    act_readers.append(tg)
                wdone(wi, [tg])
                (wsl, wtok, wi) = wget()
                for jc in range(2):
                    j = jj * 2 + jc
                    wb_, wbk, t5 = cv[jc]
                    bb, bd = fw.alloc("f")
                    tg = mm_group(PSF[:, bb, 0:TT], [(wsl[:, c, jc * 128:(jc + 1) * 128], actT[:, c, 0:TT]) for c in range(KC)], deps=[wtok] + bd)
                    t6 = fw.op("vector", lambda e, bb=bb, wb_=wb_, j=j: e.tensor_tensor(out=R1[:, j, 0:TT], in0=PSF[:, bb, 0:TT], in1=wb_[:, 0:TT], op=ALU.mult), deps=[tg, t5] + R1_rd7)
                    fw.release("f", bb, [t6])
                    used(wbk, [t6])
                    ya_w.append(t6)
                    act_readers.append(tg)
                wdone(wi, [tg])
            mg_w = []
            for jj in range(8):
                (wa_s, wa_t, wa_i), (wm_s, wm_t, wm_i) = wget(), wget()
                for jc in range(2):
                    j = jj * 2 + jc
                    by, bd = fw.alloc("f")
                    tgy = mm_group(PSF[:, by, 0:TT], [(wa_s[:, c, jc * 128:(jc + 1) * 128], R1[:, c, 0:TT]) for c in range(KC)], deps=ya_w + [wa_t] + bd)
                    bm, bd = fw.alloc("f")
                    tgm = mm_group(PSF[:, bm, 0:TT], [(wm_s[:, c, jc * 128:(jc + 1) * 128], actT[:, c, 0:TT]) for c in range(KC)], deps=[wm_t] + bd)
                    wa, wad, wak = ring("w1", w1)
                    t1 = fw.op("scalar", lambda e, bm=bm, wa=wa: e.activation(out=wa[:, 0:TT], in_=PSF[:, bm, 0:TT], func=AF.Tanh, scale=0.5), deps=[tgm] + wad)
                    fw.release("f", bm, [t1])
                    wb_, wbd, wbk = ring("w2", w2)
                    t2 = fw.op("vector", lambda e, by=by, wa=wa, wb_=wb_: e.scalar_tensor_tensor(out=wb_[:, 0:TT], in0=wa[:, 0:TT], scalar=1.0, in1=PSF[:, by, 0:TT], op0=ALU.add, op1=ALU.mult), deps=[tgy, t1] + wbd)
                    fw.release("f", by, [t2])
                    t3 = fw.op("vector", lambda e, wb_=wb_, j=j: e.scalar_tensor_tensor(out=R2[:, j, 0:TT], in0=wb_[:, 0:TT], scalar=0.5, in1=R2[:, j, 0:TT], op0=ALU.mult, op1=ALU.add), deps=[t2] + mg2_w)
                    used(wak, [t2]); used(wbk, [t3])
                    mg_w.append(t3)
                    act_readers.append(tgm)
                wdone(wa_i, [tgy]); wdone(wm_i, [tgm])
            halves = [list(range(0, nsub // 2)), list(range(nsub // 2, nsub))] if nsub >= 2 else [[0]]
            x1_ps = {}
            tn2 = []
            pend = None
            all_act_readers = act_readers
            for hv in halves:
                for jj in range(8):
                    (wo_s, wo_t, wo_i) = wget()
                    for s in hv:
                        b, bd = fw.alloc("f")
                        tg = mm_group(PSF[0:nt, b, 0:WB], [(R2[:, c, s * 128:s * 128 + nt], wo_s[:, c, 0:WB]) for c in range(KC)], deps=mg_w + [wo_t] + bd)
                        ta = fw.op("vector", lambda e, b=b, s=s, jj=jj: e.tensor_tensor(out=xres[0:nt, s, jj * WB:(jj + 1) * WB], in0=PSF[0:nt, b, 0:WB], in1=xres[0:nt, s, jj * WB:(jj + 1) * WB], op=ALU.add), deps=[tg] + tn)
                        fw.release("f", b, [ta])
                        x1_ps.setdefault(s, []).append(ta)
                    wdone(wo_i, [tg])
                if pend is not None:
                    t_, _p = norm_p2(pend[0], pend[1], nt, 1, lambda s: all_act_readers)
                    tn2 += t_
                hs_ = norm_p1(lambda s: xres[0:nt, s, :], hv, nt, lambda s: x1_ps[s])
                pend = (hs_, hv)
            t_, _p = norm_p2(pend[0], pend[1], nt, 1, lambda s: all_act_readers)
            tn2 += t_
            ra_readers = [tg]
            act_readers = []
            h_w = []
            for jj in range(22):
                (wv_s, wv_t, wv_i), (wg_s, wg_t, wg_i) = wget(), wget()
                for jc in range(2):
                    j = jj * 2 + jc
                    res = []
                    for (wsl, wtok, ch) in ((wv_s, wv_t, j), (wg_s, wg_t, 44 + j)):
                        b, bd = fw.alloc("f")
                        tg = mm_group(PSF[:, b, 0:TT], [(wsl[:, c, jc * 128:(jc + 1) * 128], actT[:, c, 0:TT]) for c in range(KC)], deps=tn2 + [wtok] + bd)
                        act_readers.append(tg)
                        ubf, ubd, ubk = ring("ub", ub)
                        t1 = fw.op("scalar", lambda e, b=b, ubf=ubf: e.activation(out=ubf[:, 2:2 + TT], in_=PSF[:, b, 0:TT], func=AF.Copy), deps=[tg] + ubd)
                        t1h = fw.op("gpsimd", lambda e, ubf=ubf, ch=ch: e.tensor_copy(out=ubf[:, 0:2], in_=uhalo[:, ch, :]), deps=ubd + [t_m7] + ext)
                        if emit_state_out is not None:
                            t1s = fw.op("scalar", lambda e, b=b, ch=ch: e.activation(out=ust[:, ch, :], in_=PSF[:, b, TT - 2:TT], func=AF.Copy), deps=[tg] + state.get("ust_free", []))
                            fw.release("f", b, [t1, t1s])
                        else:
                            fw.release("f", b, [t1])
                        if skip_tail:
                            t5h = fw.op("gpsimd", lambda e, ubf=ubf, ch=ch: e.tensor_copy(out=uhalo[:, ch, :], in_=ubf[:, TT:TT + 2]), deps=[t1, t1h, t_m7] + ext)
                            used(ubk, [t5h, t1h])
                            res.append(None)
                            continue
                        wb_, wbd, wbk = ring("w2", w2) if ch < 44 else ring("w3", w3)
                        t3 = fw.op("vector", lambda e, ubf=ubf, wb_=wb_, ch=ch: e.tensor_scalar(out=wb_[:, 0:TT], in0=ubf[:, 2:2 + TT], scalar1=fcw_t[:, ch, 2:3], scalar2=fcb_t[:, ch:ch + 1], op0=ALU.mult, op1=ALU.add), deps=[t1, t1h] + wbd + INIT)
                        t4 = fw.op("vector", lambda e, ubf=ubf, wb_=wb_, ch=ch: e.scalar_tensor_tensor(out=wb_[:, 0:TT], in0=ubf[:, 1:1 + TT], scalar=fcw_t[:, ch, 1:2], in1=wb_[:, 0:TT], op0=ALU.mult, op1=ALU.add), deps=[t3])
                        t5 = fw.op("vector", lambda e, ubf=ubf, wb_=wb_, ch=ch: e.scalar_tensor_tensor(out=wb_[:, 0:TT], in0=ubf[:, 0:TT], scalar=fcw_t[:, ch, 0:1], in1=wb_[:, 0:TT], op0=ALU.mult, op1=ALU.add), deps=[t4])
                        t5h = fw.op("gpsimd", lambda e, ubf=ubf, ch=ch: e.tensor_copy(out=uhalo[:, ch, :], in_=ubf[:, TT:TT + 2]), deps=[t5, t1h])
                        used(ubk, [t5h, t5])
                        res.append((wb_, wbk, t5))
                    if skip_tail:
                        continue
                    (uv, uvk, tv5), (ug, ugk, tg5) = res
                    wa, wad, wak = ring("w1", w1)
                    t6 = fw.op("scalar", lambda e, ug=ug, wa=wa: e.activation(out=wa[:, 0:TT], in_=ug[:, 0:TT], func=AF.Tanh, scale=0.5), deps=[tg5] + wad)
                    t7 = fw.op("vector", lambda e, ug=ug, wa=wa: e.scalar_tensor_tensor(out=wa[:, 0:TT], in0=wa[:, 0:TT], scalar=1.0, in1=ug[:, 0:TT], op0=ALU.add, op1=ALU.mult), deps=[t6])
                    t8 = fw.op("vector", lambda e, uv=uv, wa=wa, j=j: e.scalar_tensor_tensor(out=hbuf[:, j, 0:TT], in0=wa[:, 0:TT], scalar=0.5, in1=uv[:, 0:TT], op0=ALU.mult, op1=ALU.mult), deps=[t7, tv5] + ra_readers)
                    used(wak, [t8]); used(uvk, [t8]); used(ugk, [t7])
                    h_w.append(t8)
                wdone(wv_i, [tg]); wdone(wg_i, [tg])
            if skip_tail:
                state["RA_free"] = h_w[-1:] + ra_readers
                state["actT_readers"] = act_readers
                state["xres_free"] = [list(tn2)]
                return None
            x2_ps = {}
            tn3 = []
            pend = None
            all_act_readers = act_readers
            tg = None
            for hv in halves:
                for cb0 in range(0, 8, 2):
                  banks = {}
                  for cb_ in (cb0, cb0 + 1):
                    for s in hv:
                        b, bd = fw.alloc("f")
                        banks[(cb_, s)] = (b, bd)
                  for cb, rb in ((cb0, 0), (cb0, 1), (cb0 + 1, 0), (cb0 + 1, 1), (cb0, 2), (cb0 + 1, 2)):
                        nk = 12 if rb == 2 else 16
                        (wd_s, wd_t, wd_i) = wget()
                        hdep = [h_w[rb * 16 + nk - 1]]
                        for s in hv:
                            b, bd = banks[(cb, s)]
                            for c in range(nk):
                                kc = rb * 16 + c
                                first = (kc == 0)
                                last = (kc == 43)
                                tg = fw.op("tensor", lambda e, b=b, s=s, kc=kc, c=c, wd_s=wd_s, first=first, last=last: e.matmul(PSF[0:nt, b, 0:WB], lhsT=hbuf[:, kc, s * 128:s * 128 + nt], rhs=wd_s[:, c, 0:WB], start=first, stop=last),
                                           deps=(hdep + [wd_t] + bd) if c == 0 else (), signal=(c == nk - 1))
                            if rb == 2:
                                ta = fw.op("vector", lambda e, b=b, s=s, cb=cb: e.tensor_tensor(out=xres[0:nt, s, cb * WB:(cb + 1) * WB], in0=PSF[0:nt, b, 0:WB], in1=xres[0:nt, s, cb * WB:(cb + 1) * WB], op=ALU.add), deps=[tg] + tn2)
                                fw.release("f", b, [ta])
                                x2_ps.setdefault(s, []).append(ta)
                        wdone(wd_i, [tg])
                if pend is not None:
                    t_, _p = norm_p2(pend[0], pend[1], nt, 2, lambda s: all_act_readers)
                    tn3 += t_
                hs_ = norm_p1(lambda s: xres[0:nt, s, :], hv, nt, lambda s: x2_ps[s])
                pend = (hs_, hv)
            tg11 = tg
            t_p = fw.dma("gpsimd", "d_p", lambda e: e.dma_start(out=pin[0:nt, 0:nsub, :], in_=pcat[prow0:prow0 + TT, :].rearrange("(s p) d -> p s d", p=nt)), deps=[tg11])
            t_, _p = norm_p2(pend[0], pend[1], nt, 2, lambda s: all_act_readers)
            tn3 += t_
            act_readers = []
            t_pb = fw.op("vector", lambda e: e.tensor_copy(out=pbf[0:nt, 0:nsub, :], in_=pin[0:nt, 0:nsub, :]), deps=[t_p, tg11] + state.get("pbf_free", []))
            state["pin_free"] = [t_pb]
            pT_w = []
            tps = []
            for s in range(nsub):
                b, bd = fw.alloc("t")
                tp = None
                for c in range(2):
                    tp = fw.op("tensor", lambda e, c=c, b=b, s=s: e.transpose(out=PST[:, b, c * 128:c * 128 + nt], in_=pbf[0:nt, s, c * 128:(c + 1) * 128], identity=ident[0:nt, 0:nt]),
                               deps=[t_pb] + bd + state.get("pT_free", []) if c == 0 else (), signal=(c == 1))
                te = None
                for c in range(2):
                    te = fw.op("scalar", lambda e, c=c, b=b, s=s: e.activation(out=pT[:, c, s * 128:s * 128 + nt], in_=PST[:, b, c * 128:c * 128 + nt], func=AF.Copy), deps=[tp] + state.get("pT_free", []))
                    pT_w.append(te)
                fw.release("t", b, [te])
                tps.append(tp)
            state["pbf_free"] = tps
            x3_ps = {}
            pT_r = []
            ar_ps = {}
            youts = {}
            ylast = []
            for hv in halves:
                for cb in range(8):
                    (wg_s, wg_t, wg_i), (wp_s, wp_t, wp_i) = wget(), wget()
                    for s in hv:
                        bg, bd = fw.alloc("f")
                        tgg = mm_group(PSF[0:nt, bg, 0:WB], [(actT[:, c, s * 128:s * 128 + nt], wg_s[:, c, 0:WB]) for c in range(KC)], deps=tn3 + [wg_t] + bd)
                        bp, bd = fw.alloc("f")
                        tgp = mm_group(PSF[0:nt, bp, 0:WB], [(pT[:, c, s * 128:s * 128 + nt], wp_s[:, c, 0:WB]) for c in range(2)], deps=pT_w + [wp_t] + bd)
                        ar_ps.setdefault(s, []).append(tgg)
                        pT_r.append(tgp)
                        wa, wad, wak = ring("w1", w1)
                        t1 = fw.op("scalar", lambda e, bg=bg, wa=wa: e.activation(out=wa[0:nt, 0:WB], in_=PSF[0:nt, bg, 0:WB], func=AF.Tanh, scale=0.5), deps=[tgg] + wad)
                        fw.release("f", bg, [t1])
                        t2 = fw.op("vector", lambda e, bp=bp, wa=wa: e.scalar_tensor_tensor(out=wa[0:nt, 0:WB], in0=wa[0:nt, 0:WB], scalar=1.0, in1=PSF[0:nt, bp, 0:WB], op0=ALU.add, op1=ALU.mult), deps=[tgp, t1])
                        fw.release("f", bp, [t2])
                        t3 = fw.op("vector", lambda e, wa=wa, s=s, cb=cb: e.scalar_tensor_tensor(out=xres[0:nt, s, cb * WB:(cb + 1) * WB], in0=wa[0:nt, 0:WB], scalar=0.5, in1=xres[0:nt, s, cb * WB:(cb + 1) * WB], op0=ALU.mult, op1=ALU.add), deps=[t2] + tn3)
                        used(wak, [t3])
                        x3_ps.setdefault(s, []).append(t3)
                    wdone(wg_i, [tgg]); wdone(wp_i, [tgp])
                if hv is not halves[0] and state.get("pro_x") is not None:
                    txn, hvn = state.pop("pro_x")
                    hsn = norm_p1(lambda s: xres[0:nt, s, :], hvn, nt, lambda s: [txn[s]])
                    state["pro"] = {"t_x": txn, "hs": hsn}
                for s in hv:
                    yt, ytd, ytk = ring("yt", ytile)
                    t_sq = fw.op("scalar", lambda e, s=s, yt=yt: e.activation(out=yt[0:nt, :], in_=xres[0:nt, s, :], func=AF.Square, accum_out=stat[0:nt, 0:1]), deps=x3_ps[s] + ylast + ytd + [tg11])
                    t_ms = fw.op("vector", lambda e: e.tensor_scalar(out=stat[0:nt, 1:2], in0=stat[0:nt, 0:1], scalar1=1.0 / D, scalar2=EPS, op0=ALU.mult, op1=ALU.add), deps=[t_sq])
                    t_rs = fw.op("gpsimd", lambda e: e.tensor_tensor(out=stat[0:nt, 2:3], in0=stat[0:nt, 1:2], in1=mhalf[0:nt, 0:1], op=ALU.pow), deps=[t_ms, t_m1])
                    t_y = fw.op("vector", lambda e, s=s, yt=yt: e.scalar_tensor_tensor(out=yt[0:nt, :], in0=xres[0:nt, s, :], scalar=stat[0:nt, 2:3], in1=gfin_t[0:nt, :], op0=ALU.mult, op1=ALU.mult), deps=[t_rs, t_sq] + INIT)
                    youts[s] = t_y
                    ylast = [t_y]
                    if yout is not None:
                        t_o = fw.dma("gpsimd", "d_y", lambda e, s=s, yt=yt: e.dma_start(out=yout[s * 128:s * 128 + nt, :], in_=yt[0:nt, :]), deps=[t_y])
                        used(ytk, [t_o])
                        state["final"].append(t_o)
                    else:
                        used(ytk, [t_y])
                if next_x is not None and hv is halves[0] and len(halves) == 2 and nsub == 4:
                    txn = {}
                    for s in hv:
                        txn[s] = fw.dma("gpsimd", "d_x%d" % s, lambda e, s=s: e.dma_start(out=xres[0:nt, s, :], in_=xcat[next_x + s * nt:next_x + (s + 1) * nt, :]), deps=[youts[s]])
                    state["pro_x"] = (txn, hv)
            state["pT_free"] = pT_r
            state["actT_readers"] = ar_ps
            youts = [youts[s] for s in range(nsub)]
            state["xres_free"] = [[t] for t in youts]
            state["RA_free"] = [tg11, t_pb] + pT_r[-1:] + tps[-1:] + last_use.get(("yt", 0), []) + last_use.get(("yt", 1), []) + last_use.get(("yt", 2), [])
            if emit_state_out is not None:
                emit_state_out()

        state["final"] = []
        row = 0
        pendB = None
        for it_, tt_ in enumerate(pre_tiles):
            runB = emit_tile("pre", tt_, row, 0, None, extra_counts=pre_counts[it_], parity=it_ % 2, defer=True)
            if pendB is not None:
                pendB()
            pendB = runB
            row += tt_
        if pendB is not None:
            pendB()
        state["RA_free"] = state.get("pre_readers_0", []) + state.get("pre_readers_1", [])
        prow = 0
        n_main = len(main_tiles)
        for i in range(n_main):
            last = (i == n_main - 1)
            tt_ = main_tiles[i]

            def so_main():
                ws = fw.now()
                t1 = fw.dma("gpsimd", "d_so", lambda e: e.dma_start(out=o_ca[0], in_=cvst[:]), deps=ws)
                t2 = fw.dma("gpsimd", "d_so", lambda e: e.dma_start(out=o_gla[0], in_=S[:]), deps=ws)
                t3 = fw.dma("gpsimd", "d_so", lambda e: e.dma_start(out=o_ffn[0], in_=ust[:]), deps=ws)
                state["final"] += [t1, t2, t3]
                state["so_main"] = [t1, t2, t3]
            nx_ = (row + tt_) if (i >= 1 and not last and main_tiles[i + 1] == 512 and tt_ == 512) else None
            emit_tile("full", tt_, row, prow, None if i == 0 else y_main[(i - 1) * 512:i * 512, :], emit_state_out=so_main if last else None, next_x=nx_, skip_tail=(i == 0))
            row += tt_
            prow += tt_

        def load_sample_state():
            ws = fw.now() + state["so_main"]
            t1 = fw.dma("gpsimd", "d_st", lambda e: e.dma_start(out=S[:], in_=st_gla), deps=ws)
            t2 = fw.dma("gpsimd", "d_st", lambda e: e.dma_start(out=ust[:], in_=st_ffn), deps=ws)
            t3 = fw.dma("gpsimd", "d_st", lambda e: e.dma_start(out=cvst[:], in_=st_ca), deps=ws)
            t4 = fw.op("vector", lambda e: e.tensor_copy(out=Sbf[:], in_=S[:]), deps=[t3])
            t5 = fw.op("vector", lambda e: e.tensor_copy(out=uhalo[:], in_=ust[:]), deps=[t3])
            t6 = fw.op("vector", lambda e: e.tensor_copy(out=cvhalo[:], in_=cvst[:]), deps=[t3])
            state["S_tok"] = [t4]
            state["cvst_free"] = [t6]
            state["ust_free"] = [t5]
            return [t4, t5, t6]

        def so_samp():
            ws = fw.now()
            t1 = fw.dma("gpsimd", "d_so", lambda e: e.dma_start(out=o_ca[1], in_=cvst[:]), deps=ws)
            t2 = fw.dma("gpsimd", "d_so", lambda e: e.dma_start(out=o_gla[1], in_=S[:]), deps=ws)
            t3 = fw.dma("gpsimd", "d_so", lambda e: e.dma_start(out=o_ffn[1], in_=ust[:]), deps=ws)
            state["final"] += [t1, t2, t3]
        emit_tile("full", NSAMP, row, prow, y_samp, first_of_ctx_load=load_sample_state, emit_state_out=so_samp)
        fw._emit_waits("gpsimd", state["final"])
        with nc.Block() as block:
            fw.replay(block)
    return nc


_CACHE = {}


def _get_program(HALF):
    if HALF not in _CACHE:
        _CACHE[HALF] = build_program(HALF)
    return _CACHE[HALF]


def _fm(v, nchunk):
    return np.ascontiguousarray(np.asarray(v, np.float32).reshape(nchunk, 128).T)


def kernel(x_prompt, x_sample, p_prompt, p_sample, state_conv_a, state_gla, state_ffn_conv, norm_mix, w_in,
           conv_a_w, w_a_out, w_gate2, b_gate, gla_norm, w_b_out, w_o, norm_ffn, w_up, ffn_conv_w, ffn_conv_b,
           w_down, norm_ple, w_ple_gate, w_ple, norm_final):
    f = np.float32
    x_prompt = np.asarray(x_prompt, f); x_sample = np.asarray(x_sample, f)
    p_prompt = np.asarray(p_prompt, f); p_sample = np.asarray(p_sample, f)
    B, SEQ, _ = x_prompt.shape
    HALF = SEQ // 2
    ncores = 2 * B
    assert x_sample.shape[0] == ncores
    nc = _get_program(HALF)
    gam = np.stack([_fm(norm_mix[0], 16), _fm(norm_ffn[0], 16), _fm(norm_ple[0], 16), _fm(gla_norm[0], 16)], axis=1)
    gfin = np.ascontiguousarray(np.broadcast_to(np.asarray(norm_final, f)[None, :], (128, D)))
    caw = np.ascontiguousarray(np.asarray(conv_a_w[0], f).reshape(3, 16, 128).transpose(2, 1, 0))
    fcw = np.ascontiguousarray(np.asarray(ffn_conv_w[0], f).reshape(3, NFC, 128).transpose(2, 1, 0))
    fcb = _fm(ffn_conv_b[0], NFC)
    wg2 = np.zeros((33, 1024), f)
    wg2[0:16] = np.asarray(w_gate2[0], f)
    wg2[32] = np.asarray(b_gate[0], f)
    identd = np.eye(128, dtype=f).astype(ml_dtypes.bfloat16)
    trid = np.triu(np.ones((128, 128), f))
    shared = dict(w_in=np.ascontiguousarray(w_in[0], f), w_a_out=np.ascontiguousarray(w_a_out[0], f), w_b_out=np.ascontiguousarray(w_b_out[0], f),
                  w_o=np.ascontiguousarray(w_o[0], f), w_up=np.ascontiguousarray(w_up[0], f), w_down=np.ascontiguousarray(w_down[0], f),
                  w_pg=np.ascontiguousarray(w_ple_gate[0], f), w_ple=np.ascontiguousarray(w_ple[0], f), gam=np.ascontiguousarray(gam), gfin=gfin,
                  caw=caw, fcw=fcw, fcb=fcb, wg2=wg2, identd=identd, trid=trid)
    in_maps = []
    for c in range(ncores):
        b, h = c // 2, c % 2
        xc = np.zeros((2 * HALF + NSAMP, D), f)
        WARM = 128
        pc = np.zeros((HALF + WARM + NSAMP, DPLE), f)
        if h == 1:
            xc[0:2 * HALF] = x_prompt[b]
            pc[0:HALF + WARM] = p_prompt[0, b, HALF - WARM:]
        else:
            xc[HALF:2 * HALF] = x_prompt[b, 0:HALF]
            pc[WARM:HALF + WARM] = p_prompt[0, b, 0:HALF]
        xc[2 * HALF:] = x_sample[c]
        pc[HALF + WARM:] = p_sample[0, c]
        m = dict(shared)
        m["xcat"] = xc
        m["pcat"] = pc
        m["st_ca"] = np.ascontiguousarray(np.asarray(state_conv_a[0, c], f).reshape(2, 16, 128).transpose(2, 1, 0))
        m["st_gla"] = np.ascontiguousarray(np.asarray(state_gla[0, c], f).reshape(4, 2, 128, 512).transpose(2, 0, 1, 3).reshape(128, 8, 512))
        m["st_ffn"] = np.ascontiguousarray(np.asarray(state_ffn_conv[0, c], f).reshape(2, NFC, 128).transpose(2, 1, 0))
        in_maps.append(m)
    res = run_bass_kernel_spmd(nc, in_maps, core_ids=list(range(ncores)))
    R = res.results
    y_prompt = np.zeros((B, SEQ, D), f)
    for c in range(ncores):
        b, h = c // 2, c % 2
        y_prompt[b, h * HALF:(h + 1) * HALF] = R[c]["y_main"]
    y_sample = np.stack([R[c]["y_samp"] for c in range(ncores)], 0)

    def un_ca(a):
        return np.ascontiguousarray(a.transpose(2, 1, 0).reshape(2, D))

    def un_gla(a):
        return np.ascontiguousarray(a.reshape(128, 4, 2, 512).transpose(1, 2, 0, 3).reshape(4, 256, 512))

    def un_ffn(a):
        return np.ascontiguousarray(a.transpose(2, 1, 0).reshape(2, 2 * DFF))
    cap = np.stack([un_ca(R[2 * b + 1]["o_ca_p"]) for b in range(B)], 0)[None]
    gp = np.stack([un_gla(R[2 * b + 1]["o_gla_p"]) for b in range(B)], 0)[None]
    fp = np.stack([un_ffn(R[2 * b + 1]["o_ffn_p"]) for b in range(B)], 0)[None]
    cas = np.stack([un_ca(R[c]["o_ca_s"]) for c in range(ncores)], 0)[None]
    gs = np.stack([un_gla(R[c]["o_gla_s"]) for c in range(ncores)], 0)[None]
    fs = np.stack([un_ffn(R[c]["o_ffn_s"]) for c in range(ncores)], 0)[None]
    return (y_prompt, y_sample, cap.astype(f), gp.astype(f), fp.astype(f), cas.astype(f), gs.astype(f), fs.astype(f))
```
